# Optimizing a Trainium2 kernel written in Bass

```python
import math
import jax, jax.numpy as jnp
from jax import lax
import numpy as np

D_MODEL = 2048
BATCH = 8
SEQ = 2048
DEPTH = 1
DEC_BATCH = 1
DEC_SEQ = 16384
PAST_LEN = 128

HEAD_DIM = 128
A_HEADS = 8
A_KV_HEADS = 2
A_GROUP = A_HEADS // A_KV_HEADS
WINDOW = 128
BLOCK = 128
N_BUCKETS = 32
MAX_DISTANCE = 128
B_HEADS = 8
KV_RANK = 512
QK_NOPE = 128
QK_ROPE = 64
V_DIM = 128
ROPE_THETA = 10000.0
D_FF = 4 * D_MODEL
EPS = 1e-6
NEG = -1e30

A_Q = A_HEADS * HEAD_DIM
A_KV = A_KV_HEADS * HEAD_DIM
B_QN = B_HEADS * QK_NOPE
B_QR = B_HEADS * QK_ROPE
D_IN = A_Q + 2 * A_KV + B_QN + B_QR + KV_RANK + QK_ROPE
D_MIX_OUT = A_HEADS * HEAD_DIM + B_HEADS * V_DIM

kernel_name = "hymba_swa_mla_adaln_encoder"


def rmsnorm(x, g):
    xf = x.astype(jnp.float32)
    y = xf * lax.rsqrt(jnp.mean(xf * xf, axis=-1, keepdims=True) + EPS)
    return (y * g.astype(jnp.float32)).astype(x.dtype)


def t5_bucket(rel):
    half = N_BUCKETS // 2
    max_exact = half // 2
    ret = jnp.where(rel > 0, half, 0)
    n = jnp.abs(rel)
    nf = jnp.maximum(n, 1).astype(jnp.float32)
    large = max_exact + (jnp.log(nf / max_exact) / math.log(MAX_DISTANCE / max_exact)
                         * (half - max_exact)).astype(jnp.int32)
    large = jnp.minimum(large, half - 1)
    return ret + jnp.where(n < max_exact, n, large)


def rope(x, pos):
    half = QK_ROPE // 2
    inv = ROPE_THETA ** (-jnp.arange(half, dtype=jnp.float32) / half)
    ang = pos.astype(jnp.float32)[:, None] * inv[None, :]
    cos = jnp.cos(ang)[:, None, :]
    sin = jnp.sin(ang)[:, None, :]
    xf = x.astype(jnp.float32)
    x1, x2 = xf[..., :half], xf[..., half:]
    return jnp.concatenate([x1 * cos - x2 * sin, x1 * sin + x2 * cos], axis=-1).astype(x.dtype)


def window_attention(q, k, v, sink, rel_bias):
    B, S = q.shape[0], q.shape[1]
    nb = S // BLOCK
    qb = q.reshape(B, nb, BLOCK, A_KV_HEADS, A_GROUP, HEAD_DIM)
    pad = ((0, 0), (BLOCK, BLOCK), (0, 0), (0, 0))
    kp = jnp.pad(k, pad).reshape(B, nb + 2, BLOCK, A_KV_HEADS, HEAD_DIM)
    vp = jnp.pad(v, pad).reshape(B, nb + 2, BLOCK, A_KV_HEADS, HEAD_DIM)
    kb = jnp.concatenate([kp[:, :-2], kp[:, 1:-1], kp[:, 2:]], axis=2)
    vb = jnp.concatenate([vp[:, :-2], vp[:, 1:-1], vp[:, 2:]], axis=2)
    s = jnp.einsum('bnqgrd,bnkgd->bngrqk', qb, kb,
                   preferred_element_type=jnp.float32) * (HEAD_DIM ** -0.5)
    q_off = jnp.arange(BLOCK)[:, None]
    k_off = jnp.arange(3 * BLOCK)[None, :] - BLOCK
    rel = k_off - q_off
    bias = rel_bias.astype(jnp.float32)[t5_bucket(rel)]
    bias = bias.transpose(2, 0, 1).reshape(A_KV_HEADS, A_GROUP, BLOCK, 3 * BLOCK)
    k_abs = jnp.arange(nb)[:, None] * BLOCK + k_off
    valid = (jnp.abs(rel) <= WINDOW)[None] & ((k_abs >= 0) & (k_abs < S))[:, None, :]
    s = jnp.where(valid[None, :, None, None], s + bias, NEG)
    sink_l = sink.astype(jnp.float32).reshape(A_KV_HEADS, A_GROUP)[None, None, :, :, None, None]
    m = jnp.maximum(jnp.max(s, axis=-1, keepdims=True), sink_l)
    p = jnp.exp(s - m)
    denom = jnp.sum(p, axis=-1, keepdims=True) + jnp.exp(sink_l - m)
    p = (p / denom).astype(v.dtype)
    o = jnp.einsum('bngrqk,bnkgd->bnqgrd', p, vb)
    return o.reshape(B, S, A_HEADS * HEAD_DIM)


def latent_attention(q_nope, q_rope, c_kv, k_rope, g_kv, w_kv_b):
    B, S = q_nope.shape[0], q_nope.shape[1]
    pos = jnp.arange(S)
    qr = rope(q_rope.reshape(B, S, B_HEADS, QK_ROPE), pos)
    kr = rope(k_rope[:, :, None, :], pos)[:, :, 0]
    kv = (rmsnorm(c_kv, g_kv) @ w_kv_b).reshape(B, S, B_HEADS, QK_NOPE + V_DIM)
    k_nope, v = kv[..., :QK_NOPE], kv[..., QK_NOPE:]
    qn = q_nope.reshape(B, S, B_HEADS, QK_NOPE)
    scale = (QK_NOPE + QK_ROPE) ** -0.5
    nb = S // BLOCK
    qn_b = qn.reshape(B, nb, BLOCK, B_HEADS, QK_NOPE).transpose(1, 0, 2, 3, 4)
    qr_b = qr.reshape(B, nb, BLOCK, B_HEADS, QK_ROPE).transpose(1, 0, 2, 3, 4)

    def attend(blk):
        qn_i, qr_i = blk
        s = (jnp.einsum('bqhd,bkhd->bhqk', qn_i, k_nope, preferred_element_type=jnp.float32)
             + jnp.einsum('bqhr,bkr->bhqk', qr_i, kr, preferred_element_type=jnp.float32)) * scale
        p = jax.nn.softmax(s, axis=-1).astype(v.dtype)
        return jnp.einsum('bhqk,bkhd->bqhd', p, v)

    o = lax.map(attend, (qn_b, qr_b))
    return o.transpose(1, 0, 2, 3, 4).reshape(B, S, B_HEADS * V_DIM)


def trunk(x, c, w_ada, b_ada, g_mix, w_in, sink, g_kv, w_kv_b, w_o,
          g_mlp, w_ff1, w_ff2, rel_bias, g_final):
    B, S, _ = x.shape
    splits = np.cumsum([A_Q, A_KV, A_KV, B_QN, B_QR, KV_RANK]).tolist()
    for l in range(DEPTH):
        mod = jax.nn.silu(c) @ w_ada[l] + b_ada[l]
        sh1, sc1, gt1, sh2, sc2, gt2 = jnp.split(mod[:, None, :], 6, axis=-1)
        h = rmsnorm(x, g_mix[l]) * (1 + sc1) + sh1
        proj = h @ w_in[l]
        qa, ka, va, qn, qr, ckv, kr = jnp.split(proj, splits, axis=-1)
        out_a = window_attention(qa.reshape(B, S, A_HEADS, HEAD_DIM),
                                 ka.reshape(B, S, A_KV_HEADS, HEAD_DIM),
                                 va.reshape(B, S, A_KV_HEADS, HEAD_DIM),
                                 sink[l], rel_bias)
        out_b = latent_attention(qn, qr, ckv, kr, g_kv[l], w_kv_b[l])
        x = x + gt1 * (jnp.concatenate([out_a, out_b], axis=-1) @ w_o[l])
        h = rmsnorm(x, g_mlp[l]) * (1 + sc2) + sh2
        f = jnp.square(jax.nn.relu(h @ w_ff1[l])) @ w_ff2[l]
        x = x + gt2 * f
    return rmsnorm(x, g_final)


def setup_inputs(seed: int = 0) -> dict:
    key = jax.random.key(seed)
    ks = jax.random.split(key, 20)
    f32 = jnp.float32

    def nrm(k, shape, scale):
        return jax.random.normal(k, shape, f32) * scale

    return {
        "x_prompt": nrm(ks[0], (BATCH, SEQ, D_MODEL), 1.0),
        "x_sample": nrm(ks[1], (DEC_BATCH, DEC_SEQ, D_MODEL), 1.0),
        "c_prompt": nrm(ks[2], (BATCH, D_MODEL), 1.0),
        "c_sample": nrm(ks[3], (DEC_BATCH, D_MODEL), 1.0),
        "w_ada": nrm(ks[4], (DEPTH, D_MODEL, 6 * D_MODEL), 0.5 * D_MODEL ** -0.5),
        "b_ada": nrm(ks[5], (DEPTH, 6 * D_MODEL), 0.02),
        "g_mix": 1.0 + nrm(ks[6], (DEPTH, D_MODEL), 0.02),
        "w_in": nrm(ks[7], (DEPTH, D_MODEL, D_IN), D_MODEL ** -0.5),
        "sink": nrm(ks[8], (DEPTH, A_HEADS), 0.5),
        "g_kv": 1.0 + nrm(ks[9], (DEPTH, KV_RANK), 0.02),
        "w_kv_b": nrm(ks[10], (DEPTH, KV_RANK, B_HEADS * (QK_NOPE + V_DIM)), KV_RANK ** -0.5),
        "w_o": nrm(ks[11], (DEPTH, D_MIX_OUT, D_MODEL), D_MIX_OUT ** -0.5),
        "g_mlp": 1.0 + nrm(ks[12], (DEPTH, D_MODEL), 0.02),
        "w_ff1": nrm(ks[13], (DEPTH, D_MODEL, D_FF), D_MODEL ** -0.5),
        "w_ff2": nrm(ks[14], (DEPTH, D_FF, D_MODEL), D_FF ** -0.5),
        "rel_bias": nrm(ks[15], (N_BUCKETS, A_HEADS), 0.5),
        "g_final": 1.0 + nrm(ks[16], (D_MODEL,), 0.02),
    }


def reference(x_prompt, x_sample, c_prompt, c_sample, w_ada, b_ada, g_mix, w_in, sink,
              g_kv, w_kv_b, w_o, g_mlp, w_ff1, w_ff2, rel_bias, g_final):
    y_prompt = trunk(x_prompt, c_prompt, w_ada, b_ada, g_mix, w_in, sink, g_kv, w_kv_b, w_o,
                     g_mlp, w_ff1, w_ff2, rel_bias, g_final)
    y_sample = trunk(x_sample, c_sample, w_ada, b_ada, g_mix, w_in, sink, g_kv, w_kv_b, w_o,
                     g_mlp, w_ff1, w_ff2, rel_bias, g_final)
    return (y_prompt, y_sample)
```

```python
import contextlib
import math
import numpy as np
import concourse.bass as bass
import concourse.mybir as mybir
from concourse.bass_utils import run_bass_kernel_spmd

F32 = mybir.dt.float32
BF16 = mybir.dt.bfloat16
AF = mybir.ActivationFunctionType
ALU = mybir.AluOpType

D = 2048
TT = 512
DFF = 8192
EPS = 1e-6
NEGM = -30000.0
EPOCH = 4096
DMA_R = 8


class Op:
    __slots__ = ("eng", "fn", "deps", "signal", "idx", "sig", "is_dma", "tok", "waits")

    def __init__(self, eng, fn, is_dma=False):
        self.eng = eng
        self.fn = fn
        self.deps = []
        self.signal = False
        self.idx = -1
        self.sig = -1
        self.is_dma = is_dma
        self.tok = None
        self.waits = []


class Res:
    __slots__ = ("lw", "rd")

    def __init__(self):
        self.lw = None
        self.rd = []


class Sched:
    ENGS = ("pe", "act", "dve", "pool", "sp")

    def __init__(self, nc):
        self.nc = nc
        self.ops = {e: [] for e in self.ENGS}
        self.res = {}
        self.dma_n = {e: 0 for e in self.ENGS}
        self.dma_ops = {e: [] for e in self.ENGS}
        self.bar = {e: [] for e in self.ENGS}

    def _r(self, name):
        r = self.res.get(name)
        if r is None:
            r = self.res[name] = Res()
        return r

    def barrier(self, engs):
        lst = []
        for e in engs:
            ops = self.ops[e]
            for o in reversed(ops):
                if not o.is_dma:
                    lst.append(o)
                    break
            lst.extend(self.dma_ops[e][-DMA_R:])
        for e in engs:
            self.bar[e] = list(lst)

    def add(self, eng, fn, reads=(), writes=(), is_dma=False):
        op = Op(eng, fn, is_dma)
        deps = []
        for n in reads:
            r = self._r(n)
            if r.lw is not None:
                deps.append((r.lw, "raw"))
        for n in writes:
            r = self._r(n)
            if r.lw is not None:
                deps.append((r.lw, "waw"))
            for o in r.rd:
                deps.append((o, "war"))
        for d, kind in deps:
            if d is op:
                continue
            if not d.is_dma and d.eng == eng and not is_dma:
                if kind != "raw" or eng == "pe":
                    continue
            op.deps.append(d)
        if self.bar[eng]:
            for d in self.bar[eng]:
                if d.is_dma or d.eng != eng:
                    op.deps.append(d)
            self.bar[eng] = []
        for n in reads:
            self._r(n).rd.append(op)
        for n in writes:
            r = self._r(n)
            r.lw = op
            r.rd = []
        op.idx = len(self.ops[eng])
        self.ops[eng].append(op)
        if is_dma:
            n = self.dma_n[eng]
            self.dma_n[eng] = n + 1
            op.tok = (eng, n % DMA_R, 16 * (n // DMA_R + 1))
            if n >= DMA_R:
                op.deps.append(self.dma_ops[eng][n - DMA_R])
            self.dma_ops[eng].append(op)
        return op

    def dma(self, q, out, in_, reads=(), writes=()):
        return self.add(q, lambda e: e.dma_start(out=out, in_=in_), reads, writes, is_dma=True)

    def mm(self, out, lhsT, rhs, start, stop, reads=(), writes=()):
        return self.add("pe", lambda e: e.matmul(out, lhsT, rhs, start=start, stop=stop), reads, writes)

    def tr(self, out, in_, ident, reads=(), writes=()):
        return self.add("pe", lambda e: e.transpose(out, in_, ident), reads, writes)

    def finalize_and_emit(self):
        nc = self.nc
        for eng in self.ENGS:
            seen = {}
            seen_dma = {}
            for op in self.ops[eng]:
                for d in op.deps:
                    if d.is_dma:
                        k = (d.tok[0], d.tok[1])
                        if seen_dma.get(k, 0) < d.tok[2]:
                            seen_dma[k] = d.tok[2]
                            op.waits.append(d)
                    else:
                        if seen.get(d.eng, -1) < d.idx:
                            seen[d.eng] = d.idx
                            d.signal = True
                            op.waits.append(d)
        final_waits = []
        for q in self.ENGS:
            final_waits.extend(self.dma_ops[q][-DMA_R:])
        nsig = {}
        for eng in self.ENGS:
            c = 0
            for op in self.ops[eng]:
                if op.signal and not op.is_dma:
                    op.sig = c
                    c += 1
            nsig[eng] = c
        with contextlib.ExitStack() as st:
            csem = {}
            for eng in self.ENGS:
                n_ep = (nsig[eng] + EPOCH - 1) // EPOCH
                csem[eng] = [st.enter_context(nc.semaphore(f"c_{eng}_{i}")) for i in range(max(n_ep, 1))]
            dsem = {}
            for q in self.ENGS:
                if self.dma_n[q]:
                    dsem[q] = [st.enter_context(nc.semaphore(f"d_{q}_{i}")) for i in range(DMA_R)]

            def emit(eng, e):
                for op in self.ops[eng]:
                    for d in op.waits:
                        if d.is_dma:
                            e.wait_ge(dsem[d.tok[0]][d.tok[1]], d.tok[2])
                        else:
                            e.wait_ge(csem[d.eng][d.sig // EPOCH], d.sig % EPOCH + 1)
                    ins = op.fn(e)
                    if op.is_dma:
                        ins.then_inc(dsem[op.tok[0]][op.tok[1]], 16)
                    elif op.signal:
                        ins.then_inc(csem[eng][op.sig // EPOCH], 1)
                if eng == "sp":
                    for d in final_waits:
                        e.wait_ge(dsem[d.tok[0]][d.tok[1]], d.tok[2])

            with nc.Block() as block:
                @block.tensor
                def _(e):
                    emit("pe", e)

                @block.scalar
                def _(e):
                    emit("act", e)

                @block.vector
                def _(e):
                    emit("dve", e)

                @block.gpsimd
                def _(e):
                    emit("pool", e)

                @block.sync
                def _(e):
                    emit("sp", e)
        return {e: len(self.ops[e]) for e in self.ENGS}, nsig


class Arena:
    def __init__(self, nc, name, kib):
        self.words = kib * 256
        self.t = nc.alloc_sbuf_tensor(name, [128, self.words], F32)
        self.off = 0
        self.phase = 0

    def reset(self, to=0):
        self.off = to
        self.phase += 1

    def take(self, shape, dtype):
        n = 1
        for s in shape:
            n *= s
        words = n if dtype == F32 else (n + 1) // 2
        words = (words + 7) // 8 * 8
        assert self.off + words <= self.words, ("arena overflow", self.off, words, self.words)
        v = self.t[:, self.off:self.off + words]
        self.off += words
        if dtype != F32:
            v = v.bitcast(dtype)
        v = v[:, 0:n]
        if len(shape) == 2:
            v = v.rearrange("p (a b) -> p a b", a=shape[0])
        elif len(shape) == 3:
            v = v.rearrange("p (a b c) -> p a b c", a=shape[0], b=shape[1])
        return v


class Ring:
    def __init__(self, views, name):
        self.v = views
        self.name = name
        self.i = 0

    def next(self):
        k = self.i % len(self.v)
        self.i += 1
        return self.v[k], f"{self.name}{k}"


def build(NCORES, SP, SS):
    SA = NCORES * SS
    NPT = SP // TT
    NSO = SS // TT
    NSA = SA // TT
    NBP = SP // 128
    NBS = SS // 128 + 2
    NKs = (SP, SA)
    NQs = (SP, SS)
    NBW = (NBP, NBS)

    nc = bass.Bass("TRN2", target_bir_lowering=False)

    def din(name, shape, dt=F32):
        return nc.dram_tensor(name, list(shape), dt, kind="ExternalInput").ap()

    def dscr(name, shape, dt=BF16):
        return nc.dram_tensor(name, list(shape), dt).ap()

    xg = (din("xp", [SP, D]), din("xs", [SA, D]))
    cT_d = din("cT", [128, 32])
    wada_d = din("w_ada", [D, 6 * D])
    bT2_d = din("bT2", [128, 192])
    gmix2_d = din("gmix2", [128, 32])
    gmlp2_d = din("gmlp2", [128, 32])
    gfin_d = din("gfinT", [128, 16])
    gkv_d = din("gkvT", [128, 4])
    w1_d = din("w1", [D, 26 * 128 + 256])
    w2_d = din("w2", [D, 768])
    wkvb_d = din("wkvb", [512, 2048])
    wo_d = din("w_o", [D, D])
    wf1_d = din("w_ff1", [D, DFF])
    wf2_d = din("w_ff2", [DFF, D])
    sinkb_d = din("sinkb", [128, 8])
    relb_d = din("relb", [128, 256])
    oht_d = din("oht", [128, 99, 128])
    hm_d = din("hm", [128, 2])
    ident_d = din("ident", [128, 128])
    cos_d = (din("cosP", [128, SP]), din("cosS", [128, SA]))
    sin_d = (din("sinP", [128, SP]), din("sinS", [128, SA]))
    y_d = (nc.dram_tensor("yp", [SP, D], F32, kind="ExternalOutput").ap(),
           nc.dram_tensor("ys", [SS, D], F32, kind="ExternalOutput").ap())

    W1s = dscr("W1s", [26, 128, 2048])
    WOs = dscr("WOs", [16, 128, 2048])
    WF1s = dscr("WF1s", [64, 128, 2048])
    WF2s = dscr("WF2s", [16, 128, 8192])
    cn_s = [dscr(f"cn{g}", [NKs[g] // TT, 128, 4 * TT]) for g in range(2)]
    kr_s = [dscr(f"kr{g}", [128, NKs[g]]) for g in range(2)]
    qa_s = [dscr(f"qa{g}", [8, 128, NQs[g]]) for g in range(2)]
    qn_s = [dscr(f"qn{g}", [8, 128, NQs[g]]) for g in range(2)]
    qr_s = [dscr(f"qr{g}", [4, 128, NQs[g]]) for g in range(2)]
    ka_s = [dscr(f"ka{g}", [2, 128, NBW[g] * 128]) for g in range(2)]
    va_s = [dscr(f"va{g}", [NBW[g], 128, 256]) for g in range(2)]
    at_s = [dscr(f"at{g}", [16, 128, NQs[g]]) for g in range(2)]

    S = Sched(nc)
    SQ = "act"
    A = lambda name, shape, dt: nc.alloc_sbuf_tensor("sb_" + name, shape, dt)
    identf = A("identf", [128, 128], F32)
    identb = A("identb", [128, 128], BF16)
    onesb = A("onesb", [128, 128], BF16)
    epsb = A("epsb", [128, 1], F32)
    zerob = A("zerob", [128, 1], F32)
    modT = A("modT", [128, 192], F32)
    A1 = A("A1", [128, 32], F32)
    A2 = A("A2", [128, 32], F32)
    gmix2 = A("gmix2", [128, 32], F32)
    gmlp2 = A("gmlp2", [128, 32], F32)
    gfin = A("gfin", [128, 16], F32)
    gkv = A("gkv", [128, 4], F32)
    hm = A("hm", [128, 2], F32)
    Ttab = A("Ttab", [128, 3, 8, 128], F32)
    ES = A("ES", [128, 8, 128], F32)
    esk = A("esk", [128, 8], F32)
    relb = A("relb", [128, 264], F32)
    onesf = A("onesf", [128, 128], F32)
    B1 = modT[:, 0:32]
    G1 = modT[:, 64:96]
    B2 = modT[:, 96:128]
    G2 = modT[:, 160:192]

    pp = [nc.alloc_psum_tensor(f"pp{i}", [128, 1024], F32) for i in range(4)]
    ps = [pp[i // 2][:, (i % 2) * 512:(i % 2 + 1) * 512] for i in range(8)]
    psring = Ring(ps, "ps")
    pring6 = Ring(ps[0:6], "ps")

    class _TR:
        i = 0

        def next(self):
            k = 6 + self.i % 2
            self.i += 1
            return ps[k], f"ps{k}"
    tring = _TR()
    cur_ring = [pring6]

    def psb(i):
        return ps[i], f"ps{i}"

    arena = Arena(nc, "arena", 184)
    CASTW = 0

    S.dma("sp", identf[:, :], ident_d[:, :], writes=["identf"])
    S.add("dve", lambda e: e.tensor_copy(identb[:, :], identf[:, :]), reads=["identf"], writes=["identb"])
    S.add("dve", lambda e: e.memset(onesb[:, :], 1.0), writes=["onesb"])
    S.add("dve", lambda e: e.memset(onesf[:, :], 1.0), writes=["onesf"])
    S.add("dve", lambda e: e.memset(epsb[:, :], EPS), writes=["epsb"])
    S.add("dve", lambda e: e.memset(zerob[:, :], 0.0), writes=["zerob"])
    for t, d_, n in ((gmix2, gmix2_d, "gmix2"), (gmlp2, gmlp2_d, "gmlp2"), (gfin, gfin_d, "gfin"),
                     (gkv, gkv_d, "gkv"), (hm, hm_d, "hm"), (esk, sinkb_d, "esk")):
        S.dma("sp", t[:, :], d_[:, :], writes=[n])
    S.dma("sp", relb[:, 0:256], relb_d[:, :], writes=["relb"])
    S.add("dve", lambda e: e.memset(relb[:, 256:264], NEGM), writes=["relb"])


    def cast_slab(src_ap, dst_sb=None, dst_dram=None, dst_res=None):
        a = src_ap.shape[1]
        if dst_sb is not None:
            S.dma("pool", dst_sb, src_ap, writes=[dst_res])
        else:
            S.dma("pool", dst_dram.rearrange("p (a b) -> p a b", a=a), src_ap, writes=[dst_res])

    w2sb = arena.take([16, 768], BF16)
    vaw = arena.take([16, 256], BF16)
    for j in range(6):
        cast_slab(w2_d[:, j * 128:(j + 1) * 128].rearrange("(k p) n -> p k n", p=128),
                  dst_sb=w2sb[:, :, j * 128:(j + 1) * 128], dst_res="w2sb")
    for j in range(26):
        cast_slab(w1_d[:, j * 128:(j + 1) * 128].rearrange("(k p) n -> p k n", p=128),
                  dst_dram=W1s[j], dst_res=f"W1s{j}")
    for j in range(2):
        cast_slab(w1_d[:, 26 * 128 + j * 128:26 * 128 + (j + 1) * 128].rearrange("(k p) n -> p k n", p=128),
                  dst_sb=vaw[:, :, j * 128:(j + 1) * 128], dst_res="vaw")
    for j in range(16):
        cast_slab(wo_d[:, j * 128:(j + 1) * 128].rearrange("(k p) n -> p k n", p=128),
                  dst_dram=WOs[j], dst_res=f"WOs{j}")
    for j in range(64):
        cast_slab(wf1_d[:, j * 128:(j + 1) * 128].rearrange("(k p) n -> p k n", p=128),
                  dst_dram=WF1s[j], dst_res=f"WF1s{j}")
    for j in range(16):
        for q in range(4):
            cast_slab(wf2_d[q * 2048:(q + 1) * 2048, j * 128:(j + 1) * 128].rearrange("(k p) n -> p k n", p=128),
                      dst_dram=WF2s[j][:, q * 2048:(q + 1) * 2048], dst_res=f"WF2s{j}")

    P1_BASE = arena.off

    cTt = arena.take([32], F32)
    sT = arena.take([32], F32)
    bT2 = arena.take([192], F32)
    waring = Ring([arena.take([16, 128], F32) for _ in range(3)], "wa")
    ohst = arena.take([33, 128], F32)
    S.dma("sp", cTt, cT_d[:, :], writes=["cTt"])
    S.dma("sp", bT2, bT2_d[:, :], writes=["bT2"])
    S.add("act", lambda e: e.activation(sT, cTt, AF.Silu), reads=["cTt"], writes=["sT"])
    S.add("act", lambda e: e.activation(esk[:, :], esk[:, :], AF.Exp), reads=["esk"], writes=["esk"])
    S.add("dve", lambda e: e.tensor_copy(ES[:, :, :], esk[:, :].unsqueeze(2).to_broadcast([128, 8, 128])),
          reads=["esk"], writes=["ES"])
    pm, pmn = psb(7)
    for ch in range(96):
        wa, wan = waring.next()
        S.dma("sp", wa, wada_d[:, ch * 128:(ch + 1) * 128].rearrange("(k p) n -> p k n", p=128), writes=[wan])
        for k in range(16):
            S.mm(pm[:, 2 * ch:2 * ch + 2], wa[:, k, :], sT[:, 2 * k:2 * k + 2], k == 0, k == 15,
                 reads=[wan, "sT"], writes=[pmn])
    S.add("dve", lambda e: e.tensor_tensor(modT[:, :], pm[:, 0:192], bT2, ALU.add), reads=[pmn, "bT2"], writes=["modT"])
    S.add("dve", lambda e: e.scalar_tensor_tensor(A1[:, :], modT[:, 32:64], 1.0, gmix2[:, :], ALU.add, ALU.mult),
          reads=["modT", "gmix2"], writes=["A1"])
    S.add("dve", lambda e: e.scalar_tensor_tensor(A2[:, :], modT[:, 128:160], 1.0, gmlp2[:, :], ALU.add, ALU.mult),
          reads=["modT", "gmlp2"], writes=["A2"])
    for o in range(3):
        S.dma("sp", ohst, oht_d[:, o * 33:(o + 1) * 33, :], writes=["ohst"])
        for h in range(8):
            S.add("dve", lambda e, o=o, h=h: e.tensor_scalar_mul(Ttab[:, o, h, :], ohst[:, 0, :], relb[:, h:h + 1]),
                  reads=["ohst", "relb"], writes=["Ttab"])
            for b in range(1, 33):
                S.add("dve", lambda e, o=o, h=h, b=b: e.scalar_tensor_tensor(
                    Ttab[:, o, h, :], ohst[:, b, :], relb[:, b * 8 + h:b * 8 + h + 1], Ttab[:, o, h, :], ALU.mult, ALU.add),
                    reads=["ohst", "relb", "Ttab"], writes=["Ttab"])

    MAIN = ("pe", "act", "dve", "sp")
    S.barrier(MAIN)
    arena.reset(P1_BASE)

    xring = Ring([arena.take([2048], F32) for _ in range(4)], "xb")
    junk = arena.take([2048], BF16)
    ssvs = [arena.take([8], F32) for _ in range(2)]
    xsbs = [arena.take([4, 2048], BF16) for _ in range(2)]
    ntile = [0]
    hTs = [arena.take([16, 512], BF16) for _ in range(2)]
    hTcur = [None, None]
    w1ring = Ring([arena.take([16, 128], BF16) for _ in range(3)], "w1r")
    cosring = Ring([arena.take([512], F32) for _ in range(2)], "cos")
    sinring = Ring([arena.take([512], F32) for _ in range(2)], "sin")
    ckvf = arena.take([4, 512], F32)
    sqb = arena.take([4, 512], BF16)
    rsb = arena.take([512], F32)
    cnb = arena.take([4, 512], BF16)
    t1 = arena.take([512], F32)
    t2 = arena.take([512], F32)
    stg = Ring([arena.take([512], BF16) for _ in range(4)], "stg")
    vstg = arena.take([4, 256], BF16)
    alt = [0]

    def evac_copy(dst, src, reads, writes):
        alt[0] += 1
        if alt[0] % 2:
            S.add("act", lambda e: e.activation(dst, src, AF.Copy), reads=reads, writes=writes)
        else:
            S.add("dve", lambda e: e.tensor_copy(dst, src), reads=reads, writes=writes)

    def norm_stats(x_ap):
        par = ntile[0] % 2
        ntile[0] += 1
        ssv, xsb = ssvs[par], xsbs[par]
        ssvn, xsbn = f"ssv{par}", f"xsb{par}"
        S.add("dve", lambda e: e.memset(ssv, 0.0), writes=[f"{ssvn}_{b}" for b in range(4)])
        for blk in range(4):
            xb, xbn = xring.next()
            S.dma("sp", xb, x_ap[blk * 128:(blk + 1) * 128, :], writes=[xbn])
            S.add("act", lambda e, xb=xb, blk=blk: e.activation(junk, xb, AF.Square, scale=float(D ** -0.5),
                                                                accum_out=ssv[:, blk:blk + 1]),
                  reads=[xbn], writes=["junk", f"{ssvn}_{blk}"])
            S.add("act", lambda e, blk=blk: e.activation(ssv[:, 4 + blk:5 + blk], ssv[:, blk:blk + 1], AF.Sqrt,
                                                         bias=epsb[:, 0:1], scale=1.0),
                  reads=[f"{ssvn}_{blk}", "epsb"], writes=[f"{ssvn}_{blk}"])
            S.add("dve", lambda e, blk=blk: e.reciprocal(ssv[:, 4 + blk:5 + blk], ssv[:, 4 + blk:5 + blk]),
                  reads=[f"{ssvn}_{blk}"], writes=[f"{ssvn}_{blk}"])
            S.add("dve", lambda e, xb=xb, blk=blk: e.tensor_scalar_mul(xsb[:, blk, :], xb, ssv[:, 4 + blk:5 + blk]),
                  reads=[xbn, f"{ssvn}_{blk}"], writes=[xsbn])
        return par

    def norm_transpose(par, g):
        xsb, xsbn = xsbs[par], f"xsb{par}"
        hT, hTn = hTs[par], f"hT{par}"
        for c2 in range(8):
            pb, pbn = tring.next()
            pv = pb.bitcast(BF16)
            for cc in range(2):
                c = c2 * 2 + cc
                for blk in range(4):
                    S.tr(pv[:, cc * 512 + blk * 128:cc * 512 + (blk + 1) * 128], xsb[:, blk, c * 128:(c + 1) * 128],
                         identb[:, :], reads=[xsbn, "identb"], writes=[pbn])
            for cc in range(2):
                c = c2 * 2 + cc
                src = pv[:, cc * 512:(cc + 1) * 512]
                if cc == 0:
                    S.add("dve", lambda e, c=c, src=src: e.tensor_scalar(
                        hT[:, c, :], src, A1[:, 2 * c + g:2 * c + g + 1], B1[:, 2 * c + g:2 * c + g + 1], ALU.mult, ALU.add),
                        reads=[pbn, "A1", "modT"], writes=[hTn])
                else:
                    S.add("act", lambda e, c=c, src=src: e.activation(
                        hT[:, c, :], src, AF.Identity, bias=B1[:, 2 * c + g:2 * c + g + 1], scale=A1[:, 2 * c + g:2 * c + g + 1]),
                        reads=[pbn, "A1", "modT"], writes=[hTn])

    def load_rope(g, t0):
        cs, csn = cosring.next()
        sn, snn = sinring.next()
        S.dma("sp", cs, cos_d[g][:, t0:t0 + TT], writes=[csn])
        S.dma("sp", sn, sin_d[g][:, t0:t0 + TT], writes=[snn])
        return cs, csn, sn, snn

    def rope_out(pa, pan, pbk, pbkn, rp, dst_dram, dres):
        cs, csn, sn, snn = rp
        S.add("dve", lambda e: e.tensor_tensor(t1, pa, cs, ALU.mult), reads=[pan, csn], writes=["t1"])
        S.add("dve", lambda e: e.tensor_tensor(t2, pbk, sn, ALU.mult), reads=[pbkn, snn], writes=["t2"])
        sg, sgn = stg.next()
        S.add("dve", lambda e: e.tensor_tensor(sg, t1, t2, ALU.add), reads=["t1", "t2"], writes=[sgn])
        S.dma(SQ, dst_dram, sg, reads=[sgn], writes=[dres])

    def proj_fm(wview, wres, nk, rhs_of, rhs_res):
        pb, pbn = cur_ring[0].next()
        for k in range(nk):
            S.mm(pb, wview(k), rhs_of(k), k == 0, k == nk - 1, reads=[wres] + rhs_res, writes=[pbn])
        return pb, pbn

    def latent_tile(g, t0, rp):
        banks = []
        for j in range(6):
            banks.append(proj_fm(lambda k, j=j: w2sb[:, k, j * 128:(j + 1) * 128], "w2sb", 16,
                                 lambda k: hTcur[0][:, k, :], [hTcur[1]]))
        for j in range(4):
            pb, pbn = banks[j]
            S.add("act", lambda e, j=j, pb=pb: e.activation(ckvf[:, j, :], pb, AF.Copy), reads=[pbn], writes=["ckvf"])
            S.add("act", lambda e, j=j, pb=pb: e.activation(sqb[:, j, :], pb, AF.Square), reads=[pbn], writes=["sqb"])
        rope_out(banks[4][0], banks[4][1], banks[5][0], banks[5][1], rp, kr_s[g][:, t0:t0 + TT], f"kr{g}")

        def fin():
            pn, pnn = cur_ring[0].next()
            for j in range(4):
                S.mm(pn, onesb[:, :], sqb[:, j, :], j == 0, j == 3, reads=["onesb", "sqb"], writes=[pnn])
            S.add("act", lambda e: e.activation(rsb, pn, AF.Sqrt, bias=epsb[:, 0:1], scale=1.0 / 512), reads=[pnn, "epsb"], writes=["rsb"])
            S.add("dve", lambda e: e.reciprocal(rsb, rsb), reads=["rsb"], writes=["rsb"])
            for j in range(4):
                S.add("dve", lambda e, j=j: e.scalar_tensor_tensor(cnb[:, j, :], ckvf[:, j, :], gkv[:, j:j + 1], rsb, ALU.mult, ALU.mult),
                      reads=["ckvf", "gkv", "rsb"], writes=["cnb"])
            S.dma(SQ, cn_s[g][t0 // TT].rearrange("p (j t) -> p j t", j=4), cnb, reads=["cnb"], writes=[f"cn{g}"])
        return fin

    def w1_chunk(j):
        wv, wn = w1ring.next()
        S.dma("sp", wv, W1s[j].rearrange("p (k n) -> p k n", k=16), reads=[f"W1s{j}"], writes=[wn])
        return proj_fm(lambda k: wv[:, k, :], wn, 16, lambda k: hTcur[0][:, k, :], [hTcur[1]])

    def store_fm(pb, pbn, dst, dres, w=TT):
        sg, sgn = stg.next()
        evac_copy(sg[:, 0:w], pb, [pbn], [sgn])
        S.dma(SQ, dst, sg[:, 0:w], reads=[sgn], writes=[dres])

    def qkv_tile(g, q0, wb0, rp, own, halo_blk=None):
        if own:
            for h in range(8):
                pb, pbn = w1_chunk(h)
                store_fm(pb, pbn, qa_s[g][h][:, q0:q0 + TT], f"qa{g}")
            for h in range(8):
                pb, pbn = w1_chunk(8 + h)
                store_fm(pb, pbn, qn_s[g][h][:, q0:q0 + TT], f"qn{g}")
            for i in range(4):
                pa, pan = w1_chunk(16 + i)
                pbk, pbkn = w1_chunk(20 + i)
                rope_out(pa, pan, pbk, pbkn, rp, qr_s[g][i][:, q0:q0 + TT], f"qr{g}")
        for gi in range(2):
            pb, pbn = w1_chunk(24 + gi)
            if own:
                store_fm(pb, pbn, ka_s[g][gi][:, wb0 * 128:wb0 * 128 + TT], f"ka{g}")
            else:
                hb, wbi = halo_blk
                store_fm(pb[:, hb * 128:(hb + 1) * 128], pbn, ka_s[g][gi][:, wbi * 128:(wbi + 1) * 128], f"ka{g}", w=128)
        for half in range(2):
            pb, pbn = cur_ring[0].next()
            for b2 in range(2):
                blk = half * 2 + b2
                for k in range(16):
                    S.mm(pb[:, b2 * 256:(b2 + 1) * 256], hTcur[0][:, k, blk * 128:(blk + 1) * 128], vaw[:, k, :], k == 0, k == 15,
                         reads=[hTcur[1], "vaw"], writes=[pbn])
            evac_copy(vstg[:, half * 2:half * 2 + 2, :], pb.rearrange("p (a b) -> p a b", a=2), [pbn], ["vstg"])
        if own:
            S.dma(SQ, va_s[g][wb0:wb0 + 4].rearrange("b p n -> p b n"), vstg, reads=["vstg"], writes=[f"va{g}"])
        else:
            hb, wbi = halo_blk
            S.dma(SQ, va_s[g][wbi], vstg[:, hb, :], reads=["vstg"], writes=[f"va{g}"])

    tiles = []
    for ti in range(NPT):
        tiles.append((0, ti))
    for ti in range(NSA):
        tiles.append((1, ti))

    def tile_body(g, ti, rp):
        t0 = ti * TT
        fin = latent_tile(g, t0, rp)
        if g == 0:
            qkv_tile(0, t0, ti * 4, rp, True)
        else:
            if ti < NSO:
                qkv_tile(1, t0, 1 + ti * 4, rp, True)
            if ti == NSO % NSA and NSA > NSO:
                qkv_tile(1, 0, 0, rp, False, halo_blk=(0, NBS - 1))
            if ti == NSA - 1 and NSA > NSO:
                qkv_tile(1, 0, 0, rp, False, halo_blk=(3, 0))
        return fin

    def xt_of(k):
        return xg[tiles[k][0]][tiles[k][1] * TT:(tiles[k][1] + 1) * TT, :]

    pend_fin = [None]
    pars = {0: norm_stats(xt_of(0))}
    norm_transpose(pars[0], tiles[0][0])
    if len(tiles) > 1:
        pars[1] = norm_stats(xt_of(1))
    for i, (g, ti) in enumerate(tiles):
        par = pars.pop(i)
        rp = load_rope(g, ti * TT)
        if i + 1 < len(tiles):
            norm_transpose(pars[i + 1], tiles[i + 1][0])
        if pend_fin[0] is not None:
            pend_fin[0]()
        if i + 2 < len(tiles):
            pars[i + 2] = norm_stats(xt_of(i + 2))
        hTcur[0], hTcur[1] = hTs[par], f"hT{par}"
        pend_fin[0] = tile_body(g, ti, rp)
    pend_fin[0]()

    cur_ring[0] = psring
    S.barrier(MAIN)
    arena.reset(CASTW)

    qat = arena.take([8, 512], BF16)
    kat = arena.take([2, 768], BF16)
    vat = arena.take([6, 256], BF16)
    wtmp = Ring([arena.take([512], F32) for _ in range(2)], "wtmp")
    wP = Ring([arena.take([512], BF16) for _ in range(4)], "wP")
    den = arena.take([512], F32)
    aoT = arena.take([8, 512], BF16)
    wscale = float(128 ** -0.5)
    for g in range(2):
        nbw = NBW[g]
        for ti in range(NQs[g] // TT):
            q0 = ti * TT
            wb0 = ti * 4 + (1 if g == 1 else 0)
            lo = max(wb0 - 1, 0)
            hi = min(wb0 + 5, nbw)
            nb_ld = hi - lo
            S.dma("sp", qat, qa_s[g][:, :, q0:q0 + TT].rearrange("h p t -> p h t"), reads=[f"qa{g}"], writes=["qat"])
            S.dma("sp", kat[:, :, 0:nb_ld * 128], ka_s[g][:, :, lo * 128:hi * 128].rearrange("c p t -> p c t"),
                  reads=[f"ka{g}"], writes=["kat"])
            S.dma("sp", vat[:, 0:nb_ld, :], va_s[g][lo:hi].rearrange("b p n -> p b n"), reads=[f"va{g}"], writes=["vat"])
            for nb in range(4):
                wn = wb0 + nb
                for gi in range(2):
                    qsl = qat[:, 4 * gi:4 * gi + 4, nb * 128:(nb + 1) * 128]
                    offs = [o for o in (-1, 0, 1) if 0 <= wn + o < nbw]
                    Ps = []
                    for o in offs:
                        kb = wn + o - lo
                        pb, pbn = psring.next()
                        S.mm(pb.rearrange("p (r q) -> p r q", r=4), kat[:, gi, kb * 128:(kb + 1) * 128], qsl, True, True,
                             reads=["kat", "qat"], writes=[pbn])
                        wt, wtn = wtmp.next()
                        S.add("dve", lambda e, wt=wt, pb=pb, o=o, gi=gi: e.scalar_tensor_tensor(
                            wt.rearrange("p (r q) -> p r q", r=4), pb.rearrange("p (r q) -> p r q", r=4), wscale,
                            Ttab[:, o + 1, 4 * gi:4 * gi + 4, :], ALU.mult, ALU.add),
                            reads=[pbn, "Ttab"], writes=[wtn])
                        if g == 1 and wn + o == 0:
                            bias = hm[:, 0:1]
                        elif g == 1 and wn + o == nbw - 1:
                            bias = hm[:, 1:2]
                        else:
                            bias = zerob[:, 0:1]
                        pt, ptn = wP.next()
                        S.add("act", lambda e, pt=pt, wt=wt, bias=bias: e.activation(pt, wt, AF.Exp, bias=bias, scale=1.0),
                              reads=[wtn, "hm", "zerob"], writes=[ptn])
                        Ps.append((pt, ptn, kb))
                    po, pon = psring.next()
                    pss, pssn = psring.next()
                    for i, (pt, ptn, kb) in enumerate(Ps):
                        S.mm(po, vat[:, kb, gi * 128:(gi + 1) * 128], pt, i == 0, i == len(Ps) - 1, reads=["vat", ptn], writes=[pon])
                    for i, (pt, ptn, kb) in enumerate(Ps):
                        S.mm(pss, onesb[:, :], pt, i == 0, i == len(Ps) - 1, reads=["onesb", ptn], writes=[pssn])
                    S.add("dve", lambda e, pss=pss, gi=gi: e.tensor_tensor(
                        den.rearrange("p (r q) -> p r q", r=4), pss.rearrange("p (r q) -> p r q", r=4),
                        ES[:, 4 * gi:4 * gi + 4, :], ALU.add), reads=[pssn, "ES"], writes=["den"])
                    S.add("dve", lambda e: e.reciprocal(den, den), reads=["den"], writes=["den"])
                    S.add("dve", lambda e, po=po, gi=gi, nb=nb: e.tensor_tensor(
                        aoT[:, 4 * gi:4 * gi + 4, nb * 128:(nb + 1) * 128], po.rearrange("p (r q) -> p r q", r=4),
                        den.rearrange("p (r q) -> p r q", r=4), ALU.mult), reads=[pon, "den"], writes=["aoT"])
            S.dma(SQ, at_s[g][0:8, :, q0:q0 + TT].rearrange("h p t -> p h t"), aoT, reads=["aoT"], writes=[f"at{g}"])

    S.barrier(MAIN)
    arena.reset(CASTW)

    NKmax = max(NKs)
    NQmax = max(NQs)
    knT = arena.take([NKmax], BF16)
    vh = arena.take([NKmax // 128, 128], BF16)
    krT = arena.take([NKmax], BF16)
    cnring = Ring([arena.take([4, 512], BF16) for _ in range(4)], "cnr")
    qnh = arena.take([NQmax], BF16)
    qrh = arena.take([NQmax], BF16)
    mP = Ring([arena.take([1024], BF16) for _ in range(4)], "mP")
    ostg = Ring([arena.take([512], BF16) for _ in range(2)], "ostg")
    rec = arena.take([512], F32)
    accs2 = [[arena.take([1024], F32) for _ in range(2)] for _ in range(2)]
    wkvb = arena.take([4, 2048], BF16)
    wkst_view = accs2[0][0].rearrange("p (a b) -> p a b", a=2)
    wkst_view2 = accs2[0][1].rearrange("p (a b) -> p a b", a=2)
    for j in range(4):
        for hf, wv in enumerate((wkst_view, wkst_view2)):
            S.dma("sp", wv, wkvb_d[hf * 256:(hf + 1) * 256, j * 512:(j + 1) * 512].rearrange("(k p) n -> p k n", p=128),
                  writes=[f"acc0_{hf}"])
            evac_copy(wkvb[:, 2 * hf:2 * hf + 2, j * 512:(j + 1) * 512], wv, [f"acc0_{hf}"], ["wkvb"])
    mscale = float(192 ** -0.5)
    for g in range(2):
        NK, NQ = NKs[g], NQs[g]
        S.dma("sp", krT[:, 0:NK], kr_s[g][:, :], reads=[f"kr{g}"], writes=["krT"])
        for h in range(8):
            for kt in range(NK // TT):
                cnv, cnn = cnring.next()
                S.dma("sp", cnv, cn_s[g][kt].rearrange("p (j t) -> p j t", j=4), reads=[f"cn{g}"], writes=[cnn])
                pb, pbn = psb(4 + kt % 2)
                for j in range(4):
                    S.mm(pb, wkvb[:, j, h * 128:(h + 1) * 128], cnv[:, j, :], j == 0, j == 3, reads=["wkvb", cnn], writes=[pbn])
                S.add("act", lambda e, pb=pb, kt=kt: e.activation(knT[:, kt * TT:(kt + 1) * TT], pb, AF.Copy), reads=[pbn], writes=["knT"])
                pv_, pvn = psb(2 * (kt % 2))
                for blk in range(4):
                    for j in range(4):
                        S.mm(pv_[:, blk * 128:(blk + 1) * 128], cnv[:, j, blk * 128:(blk + 1) * 128],
                             wkvb[:, j, 1024 + h * 128:1024 + (h + 1) * 128], j == 0, j == 3, reads=["wkvb", cnn], writes=[pvn])
                S.add("dve", lambda e, pv_=pv_, kt=kt: e.tensor_copy(vh[:, kt * 4:kt * 4 + 4, :], pv_.rearrange("p (a b) -> p a b", a=4)),
                      reads=[pvn], writes=["vh"])
            S.dma("sp", qnh[:, 0:NQ], qn_s[g][h], reads=[f"qn{g}"], writes=["qnh"])
            ro = 64 * (h % 2)
            S.add("dve", lambda e, ro=ro: e.memset(qrh[64 - ro:128 - ro, :], 0.0), writes=["qrh"])
            S.dma("sp", qrh[ro:ro + 64, 0:NQ], qr_s[g][h // 2][ro:ro + 64, :], reads=[f"qr{g}"], writes=["qrh"])
            NKB = NK // 128
            NKP = NKB // 2
            pending = [None]
            for qt in range(NQ // TT):
                po, pon = psb(6 + qt % 2)
                pss, pssn = psb(5)
                accs = accs2[qt % 2]
                an = [f"acc{qt % 2}_{a}" for a in range(2)]

                def s_pair(kp, qt=qt):
                    pbp = pp[kp % 3][:, :]
                    pbn2 = [f"ps{2 * (kp % 3)}", f"ps{2 * (kp % 3) + 1}"]
                    for hf in range(2):
                        kb = 2 * kp + hf
                        o_ = pbp[:, hf * 512:(hf + 1) * 512]
                        S.mm(o_, knT[:, kb * 128:(kb + 1) * 128], qnh[:, qt * TT:(qt + 1) * TT], True, False,
                             reads=["knT", "qnh"], writes=pbn2)
                        S.mm(o_, krT[:, kb * 128:(kb + 1) * 128], qrh[:, qt * TT:(qt + 1) * TT], False, True,
                             reads=["krT", "qrh"], writes=pbn2)
                    pt, ptn = mP.next()
                    S.add("act", lambda e, pt=pt, pbp=pbp: e.activation(pt, pbp, AF.Exp, scale=mscale), reads=pbn2, writes=[ptn])
                    return pt, ptn

                def pv_pair(kp, pt, ptn, po=po, pon=pon, accs=accs, an=an):
                    for hf in range(2):
                        kb = 2 * kp + hf
                        S.mm(po, vh[:, kb, :], pt[:, hf * 512:(hf + 1) * 512], kb == 0, kb == NKB - 1, reads=["vh", ptn], writes=[pon])
                    a = kp % 2
                    eng = "dve"
                    if kp < 2:
                        S.add(eng, lambda e, a=a: e.tensor_copy(accs[a], pt), reads=[ptn], writes=[an[a]])
                    else:
                        S.add(eng, lambda e, a=a: e.tensor_tensor(accs[a], accs[a], pt, ALU.add), reads=[ptn, an[a]], writes=[an[a]])

                def epilogue(qt=qt, po=po, pon=pon, pss=pss, pssn=pssn, accs=accs, an=an):
                    S.add("dve", lambda e: e.tensor_tensor(accs[0], accs[0], accs[1], ALU.add), reads=[an[0], an[1]], writes=[an[0]])
                    S.add("dve", lambda e: e.tensor_tensor(accs[0][:, 0:512], accs[0][:, 0:512], accs[0][:, 512:1024], ALU.add),
                          reads=[an[0]], writes=[an[0]])
                    S.mm(pss, onesf[:, :], accs[0][:, 0:512], True, True, reads=["onesf", an[0]], writes=[pssn])
                    S.add("dve", lambda e: e.tensor_copy(rec, pss), reads=[pssn], writes=["rec"])
                    S.add("dve", lambda e: e.reciprocal(rec, rec), reads=["rec"], writes=["rec"])
                    sg, sgn = ostg.next()
                    S.add("dve", lambda e: e.tensor_tensor(sg, po, rec, ALU.mult), reads=[pon, "rec"], writes=[sgn])
                    S.dma(SQ, at_s[g][8 + h][:, qt * TT:(qt + 1) * TT], sg, reads=[sgn], writes=[f"at{g}"])

                pend = {0: s_pair(0), 1: s_pair(1), 2: s_pair(2)}
                for kp in range(NKP):
                    pt, ptn = pend.pop(kp)
                    pv_pair(kp, pt, ptn)
                    if kp + 3 < NKP:
                        pend[kp + 3] = s_pair(kp + 3)
                    if kp == 2 and pending[0] is not None:
                        pending[0]()
                        pending[0] = None
                if pending[0] is not None:
                    pending[0]()
                pending[0] = epilogue
            pending[0]()

    S.barrier(("pe", "act", "dve", "sp", "pool"))
    arena.reset(0)

    xt = arena.take([4, 2048], F32)
    xT = arena.take([16, 512], F32)
    atT = arena.take([16, 512], BF16)
    h2T = arena.take([16, 512], BF16)
    actT = arena.take([32, 512], BF16)
    wring = Ring([arena.take([16, 128], BF16) for _ in range(3)], "wr")
    w2ring = Ring([arena.take([32, 128], BF16) for _ in range(2)], "w2r")
    yst = Ring([arena.take([2048], F32) for _ in range(2)], "yst")
    sq3 = Ring([arena.take([512], BF16) for _ in range(4)], "sq3")
    rs3 = arena.take([512], F32)
    tmp3 = Ring([arena.take([512], F32) for _ in range(2)], "tmp3")
    p3ring = Ring(ps[0:7], "ps")
    cur_ring[0] = p3ring
    pstat, pstatn = psb(7)

    class Stats:
        def __init__(self):
            self.pend = []

        def chunk(self, c):
            sq, sqn = sq3.next()
            S.add("act", lambda e, sq=sq, c=c: e.activation(sq, xT[:, c, :], AF.Square), reads=[f"xT{c}"], writes=[sqn])
            self.pend.append((c, sq, sqn))
            if len(self.pend) > 2:
                self._mm()

        def _mm(self):
            c, sq, sqn = self.pend.pop(0)
            S.mm(pstat, onesb[:, :], sq, c == 0, c == 15, reads=["onesb", sqn], writes=[pstatn])

        def fin(self):
            while self.pend:
                self._mm()
            S.add("act", lambda e: e.activation(rs3, pstat, AF.Sqrt, bias=epsb[:, 0:1], scale=1.0 / D), reads=[pstatn, "epsb"], writes=["rs3"])
            S.add("dve", lambda e: e.reciprocal(rs3, rs3), reads=["rs3"], writes=["rs3"])

    p3tiles = [(g, ti) for g in range(2) for ti in range(NQs[g] // TT)]

    class Streamer:
        def __init__(self, views, name, seq):
            self.views, self.name, self.seq = views, name, seq
            self.depth = len(views)
            self.i = 0
            for k in range(min(self.depth, len(seq))):
                self._load(k)

        def _load(self, k):
            src, res, kk = self.seq[k]
            v = self.views[k % self.depth]
            S.dma("sp", v, src.rearrange("p (k n) -> p k n", k=kk), reads=[res], writes=[f"{self.name}{k % self.depth}"])

        def get(self):
            k = self.i
            return self.views[k % self.depth], f"{self.name}{k % self.depth}"

        def release(self):
            k = self.i
            self.i += 1
            if k + self.depth < len(self.seq):
                self._load(k + self.depth)

    seq_small, seq_big = [], []
    for _ in p3tiles:
        for f in range(16):
            seq_small.append((WOs[f], f"WOs{f}", 16))
        for half in range(2):
            for j in range(32):
                seq_small.append((WF1s[half * 32 + j], f"WF1s{half * 32 + j}", 16))
            for f in range(16):
                seq_big.append((WF2s[f][:, half * 4096:(half + 1) * 4096], f"WF2s{f}", 32))
    wst = Streamer(wring.v, "wr", seq_small)
    w2st = Streamer(w2ring.v, "w2r", seq_big)

    def load_inputs(g, ti):
        t0 = ti * TT
        for blk in range(4):
            S.dma("sp", xt[:, blk, :], xg[g][t0 + blk * 128:t0 + (blk + 1) * 128, :], writes=["xt"])
        S.dma("sp", atT, at_s[g][:, :, t0:t0 + TT].rearrange("c p t -> p c t"), reads=[f"at{g}"], writes=["atT"])

    load_inputs(*p3tiles[0])
    for it, (g, ti) in enumerate(p3tiles):
        t0 = ti * TT
        for c in range(16):
            pb, pbn = p3ring.next()
            for blk in range(4):
                S.tr(pb[:, blk * 128:(blk + 1) * 128], xt[:, blk, c * 128:(c + 1) * 128], identf[:, :],
                     reads=["xt", "identf"], writes=[pbn])
            evac_copy(xT[:, c, :], pb, [pbn], [f"xT{c}"])
        st = Stats()
        for f in range(16):
            wv, wn = wst.get()
            pb, pbn = proj_fm(lambda k: wv[:, k, :], wn, 16, lambda k: atT[:, k, :], ["atT"])
            wst.release()
            S.add("dve", lambda e, f=f, pb=pb, g=g: e.scalar_tensor_tensor(
                xT[:, f, :], pb, G1[:, 2 * f + g:2 * f + g + 1], xT[:, f, :], ALU.mult, ALU.add),
                reads=[pbn, "modT", f"xT{f}"], writes=[f"xT{f}"])
            st.chunk(f)
        st.fin()
        for c in range(16):
            tp, tpn = tmp3.next()
            S.add("dve", lambda e, c=c, tp=tp, g=g: e.scalar_tensor_tensor(
                tp, xT[:, c, :], A2[:, 2 * c + g:2 * c + g + 1], rs3, ALU.mult, ALU.mult),
                reads=[f"xT{c}", "A2", "rs3"], writes=[tpn])
            S.add("act", lambda e, c=c, tp=tp, g=g: e.activation(h2T[:, c, :], tp, AF.Identity, bias=B2[:, 2 * c + g:2 * c + g + 1], scale=1.0),
                  reads=[tpn, "modT"], writes=["h2T"])
        st = Stats()
        for half in range(2):
            for j in range(32):
                f1 = half * 32 + j
                wv, wn = wst.get()
                pb, pbn = proj_fm(lambda k: wv[:, k, :], wn, 16, lambda k: h2T[:, k, :], ["h2T"])
                wst.release()
                tp, tpn = tmp3.next()
                S.add("act", lambda e, tp=tp, pb=pb: e.activation(tp, pb, AF.Relu), reads=[pbn], writes=[tpn])
                S.add("pool", lambda e, tp=tp, j=j: e.tensor_tensor(actT[:, j, :], tp, tp, ALU.mult), reads=[tpn], writes=["actT"])
            if half == 1 and it + 1 < len(p3tiles):
                load_inputs(*p3tiles[it + 1])
            for f in range(16):
                wv, wn = w2st.get()
                pb, pbn = proj_fm(lambda k: wv[:, k, :], wn, 32, lambda k: actT[:, k, :], ["actT"])
                w2st.release()
                S.add("dve", lambda e, f=f, pb=pb, g=g: e.scalar_tensor_tensor(
                    xT[:, f, :], pb, G2[:, 2 * f + g:2 * f + g + 1], xT[:, f, :], ALU.mult, ALU.add),
                    reads=[pbn, "modT", f"xT{f}"], writes=[f"xT{f}"])
                if half == 1:
                    st.chunk(f)
        st.fin()
        for c in range(16):
            S.add("dve", lambda e, c=c: e.scalar_tensor_tensor(
                xT[:, c, :], xT[:, c, :], gfin[:, c:c + 1], rs3, ALU.mult, ALU.mult),
                reads=[f"xT{c}", "gfin", "rs3"], writes=[f"xT{c}"])
        for blk in range(4):
            ys, ysn = yst.next()
            for c4 in range(4):
                pb, pbn = p3ring.next()
                for cc in range(4):
                    c = c4 * 4 + cc
                    S.tr(pb[:, cc * 128:(cc + 1) * 128], xT[:, c, blk * 128:(blk + 1) * 128], identf[:, :],
                         reads=[f"xT{c}", "identf"], writes=[pbn])
                evac_copy(ys[:, c4 * 512:(c4 + 1) * 512], pb, [pbn], [ysn])
            S.dma(SQ, y_d[g][t0 + blk * 128:t0 + (blk + 1) * 128, :], ys, reads=[ysn], writes=["y"])

    stats_ = S.finalize_and_emit()
    return nc, stats_


def _t5_bucket(rel):
    half = 16
    max_exact = 8
    ret = np.where(rel > 0, half, 0)
    n = np.abs(rel)
    nf = np.maximum(n, 1).astype(np.float32)
    large = max_exact + (np.log(nf / max_exact) / math.log(128 / max_exact) * (half - max_exact)).astype(np.int32)
    large = np.minimum(large, half - 1)
    return ret + np.where(n < max_exact, n, large)


def _oht():
    k = np.arange(128)[:, None]
    q = np.arange(128)[None, :]
    out = np.zeros((128, 99, 128), np.float32)
    for oi, o in enumerate((-1, 0, 1)):
        rel = o * 128 + k - q
        bk = _t5_bucket(rel)
        valid = np.abs(rel) <= 128
        for b in range(32):
            out[:, oi * 33 + b, :] = ((bk == b) & valid)
        out[:, oi * 33 + 32, :] = ~valid
    return out


def _rope_tables(pos):
    half = 32
    inv = (np.float32(10000.0) ** (-np.arange(half, dtype=np.float32) / np.float32(half))).astype(np.float32)
    ang = pos.astype(np.float32)[:, None] * inv[None, :]
    cos = np.cos(ang).astype(np.float32).T
    sin = np.sin(ang).astype(np.float32).T
    cosT = np.concatenate([cos, cos, cos, cos], 0)
    sinT = np.concatenate([-sin, sin, -sin, sin], 0)
    return np.ascontiguousarray(cosT), np.ascontiguousarray(sinT)


def _fm(v, reps=1):
    a = np.asarray(v, np.float32).reshape(-1, 128).T
    return np.ascontiguousarray(np.repeat(a, reps, axis=1))


_CACHE = {}


def run(inputs, NCORES, SP, SS):
    f = lambda k: np.asarray(inputs[k], np.float32)
    SA = NCORES * SS
    w_in = f("w_in")[0]
    QA, KA, VA, QN, QR, CKV, KR = 0, 1024, 1280, 1536, 2560, 3072, 3584
    sw = np.arange(512).reshape(8, 2, 32)[:, ::-1, :].reshape(-1)
    swk = np.arange(64).reshape(2, 32)[::-1].reshape(-1)
    cols1 = np.concatenate([np.arange(QA, QA + 1024), np.arange(QN, QN + 1024), np.arange(QR, QR + 512), QR + sw,
                            np.arange(KA, KA + 256), np.arange(VA, VA + 256)])
    krc = np.arange(KR, KR + 64)
    cols2 = np.concatenate([np.arange(CKV, CKV + 512), krc, krc, KR + swk, KR + swk])
    w1 = np.ascontiguousarray(w_in[:, cols1])
    w2 = np.ascontiguousarray(w_in[:, cols2])
    wk = f("w_kv_b")[0].reshape(512, 8, 256)
    wkvb = np.ascontiguousarray(np.concatenate([wk[:, :, :128].reshape(512, 1024), wk[:, :, 128:].reshape(512, 1024)], 1))
    common = {
        "w_ada": f("w_ada")[0], "bT2": _fm(f("b_ada")[0], 2), "gmix2": _fm(f("g_mix")[0], 2), "gmlp2": _fm(f("g_mlp")[0], 2),
        "gfinT": _fm(f("g_final")), "gkvT": _fm(f("g_kv")[0]), "w1": w1, "w2": w2, "wkvb": wkvb,
        "w_o": f("w_o")[0], "w_ff1": f("w_ff1")[0], "w_ff2": f("w_ff2")[0],
        "sinkb": np.ascontiguousarray(np.broadcast_to(f("sink")[0][None, :], (128, 8))),
        "relb": np.ascontiguousarray(np.broadcast_to(f("rel_bias").reshape(1, 256), (128, 256))),
        "oht": _oht(), "ident": np.eye(128, dtype=np.float32),
    }
    cosP, sinP = _rope_tables(np.arange(SP))
    common["cosP"], common["sinP"] = cosP, sinP
    xpr, xsm = f("x_prompt"), f("x_sample")[0]
    cp, cs = f("c_prompt"), f("c_sample")[0]
    in_maps = []
    for i in range(NCORES):
        m = dict(common)
        m["xp"] = np.ascontiguousarray(xpr[i])
        m["xs"] = np.ascontiguousarray(np.roll(xsm, -i * SS, axis=0))
        pos = (np.arange(SA) + i * SS) % SA
        m["cosS"], m["sinS"] = _rope_tables(pos)
        cc = np.stack([cp[i], cs], 1)
        m["cT"] = np.ascontiguousarray(cc.reshape(16, 128, 2).transpose(1, 0, 2).reshape(128, 32))
        hmv = np.zeros((128, 2), np.float32)
        if i == 0:
            hmv[:, 0] = NEGM
        if i == NCORES - 1:
            hmv[:, 1] = NEGM
        m["hm"] = hmv
        in_maps.append(m)
    key = (NCORES, SP, SS)
    if key not in _CACHE:
        _CACHE[key] = build(NCORES, SP, SS)
    nc, st = _CACHE[key]
    res = run_bass_kernel_spmd(nc, in_maps, core_ids=list(range(NCORES)))
    yp = np.stack([np.asarray(r["yp"], np.float32) for r in res.results], 0)
    ys = np.concatenate([np.asarray(r["ys"], np.float32) for r in res.results], 0)[None]
    return yp, ys


def kernel(**inputs):
    return run(inputs, 8, 2048, 2048)
```

```python
import contextlib
import math
import numpy as np
import concourse.bass as bass
import concourse.mybir as mybir
from concourse.bass_utils import run_bass_kernel_spmd

F32 = mybir.dt.float32
BF16 = mybir.dt.bfloat16
AF = mybir.ActivationFunctionType
ALU = mybir.AluOpType

D = 2048
TT = 512
DFF = 8192
EPS = 1e-6
NEGM = -30000.0
EPOCH = 4096
DMA_R = 8


class Op:
    __slots__ = ("eng", "fn", "deps", "signal", "idx", "sig", "is_dma", "tok", "waits")

    def __init__(self, eng, fn, is_dma=False):
        self.eng = eng
        self.fn = fn
        self.deps = []
        self.signal = False
        self.idx = -1
        self.sig = -1
        self.is_dma = is_dma
        self.tok = None
        self.waits = []


class Res:
    __slots__ = ("lw", "rd")

    def __init__(self):
        self.lw = None
        self.rd = []


class Sched:
    ENGS = ("pe", "act", "dve", "pool", "sp")

    def __init__(self, nc):
        self.nc = nc
        self.ops = {e: [] for e in self.ENGS}
        self.res = {}
        self.dma_n = {e: 0 for e in self.ENGS}
        self.dma_ops = {e: [] for e in self.ENGS}
        self.bar = {e: [] for e in self.ENGS}

    def _r(self, name):
        r = self.res.get(name)
        if r is None:
            r = self.res[name] = Res()
        return r

    def barrier(self, engs):
        lst = []
        for e in engs:
            ops = self.ops[e]
            for o in reversed(ops):
                if not o.is_dma:
                    lst.append(o)
                    break
            lst.extend(self.dma_ops[e][-DMA_R:])
        for e in engs:
            self.bar[e] = list(lst)

    def add(self, eng, fn, reads=(), writes=(), is_dma=False):
        op = Op(eng, fn, is_dma)
        deps = []
        for n in reads:
            r = self._r(n)
            if r.lw is not None:
                deps.append((r.lw, "raw"))
        for n in writes:
            r = self._r(n)
            if r.lw is not None:
                deps.append((r.lw, "waw"))
            for o in r.rd:
                deps.append((o, "war"))
        for d, kind in deps:
            if d is op:
                continue
            if not d.is_dma and d.eng == eng and not is_dma:
                if kind != "raw" or eng == "pe":
                    continue
            op.deps.append(d)
        if self.bar[eng]:
            for d in self.bar[eng]:
                if d.is_dma or d.eng != eng:
                    op.deps.append(d)
            self.bar[eng] = []
        for n in reads:
            self._r(n).rd.append(op)
        for n in writes:
            r = self._r(n)
            r.lw = op
            r.rd = []
        op.idx = len(self.ops[eng])
        self.ops[eng].append(op)
        if is_dma:
            n = self.dma_n[eng]
            self.dma_n[eng] = n + 1
            op.tok = (eng, n % DMA_R, 16 * (n // DMA_R + 1))
            if n >= DMA_R:
                op.deps.append(self.dma_ops[eng][n - DMA_R])
            self.dma_ops[eng].append(op)
        return op

    def dma(self, q, out, in_, reads=(), writes=()):
        return self.add(q, lambda e: e.dma_start(out=out, in_=in_), reads, writes, is_dma=True)

    def mm(self, out, lhsT, rhs, start, stop, reads=(), writes=()):
        return self.add("pe", lambda e: e.matmul(out, lhsT, rhs, start=start, stop=stop), reads, writes)

    def tr(self, out, in_, ident, reads=(), writes=()):
        return self.add("pe", lambda e: e.transpose(out, in_, ident), reads, writes)

    def finalize_and_emit(self):
        nc = self.nc
        for eng in self.ENGS:
            seen = {}
            seen_dma = {}
            for op in self.ops[eng]:
                for d in op.deps:
                    if d.is_dma:
                        k = (d.tok[0], d.tok[1])
                        if seen_dma.get(k, 0) < d.tok[2]:
                            seen_dma[k] = d.tok[2]
                            op.waits.append(d)
                    else:
                        if seen.get(d.eng, -1) < d.idx:
                            seen[d.eng] = d.idx
                            d.signal = True
                            op.waits.append(d)
        final_waits = []
        for q in self.ENGS:
            final_waits.extend(self.dma_ops[q][-DMA_R:])
        nsig = {}
        for eng in self.ENGS:
            c = 0
            for op in self.ops[eng]:
                if op.signal and not op.is_dma:
                    op.sig = c
                    c += 1
            nsig[eng] = c
        with contextlib.ExitStack() as st:
            csem = {}
            for eng in self.ENGS:
                n_ep = (nsig[eng] + EPOCH - 1) // EPOCH
                csem[eng] = [st.enter_context(nc.semaphore(f"c_{eng}_{i}")) for i in range(max(n_ep, 1))]
            dsem = {}
            for q in self.ENGS:
                if self.dma_n[q]:
                    dsem[q] = [st.enter_context(nc.semaphore(f"d_{q}_{i}")) for i in range(DMA_R)]

            def emit(eng, e):
                for op in self.ops[eng]:
                    for d in op.waits:
                        if d.is_dma:
                            e.wait_ge(dsem[d.tok[0]][d.tok[1]], d.tok[2])
                        else:
                            e.wait_ge(csem[d.eng][d.sig // EPOCH], d.sig % EPOCH + 1)
                    ins = op.fn(e)
                    if op.is_dma:
                        ins.then_inc(dsem[op.tok[0]][op.tok[1]], 16)
                    elif op.signal:
                        ins.then_inc(csem[eng][op.sig // EPOCH], 1)
                if eng == "sp":
                    for d in final_waits:
                        e.wait_ge(dsem[d.tok[0]][d.tok[1]], d.tok[2])

            with nc.Block() as block:
                @block.tensor
                def _(e):
                    emit("pe", e)

                @block.scalar
                def _(e):
                    emit("act", e)

                @block.vector
                def _(e):
                    emit("dve", e)

                @block.gpsimd
                def _(e):
                    emit("pool", e)

                @block.sync
                def _(e):
                    emit("sp", e)
        return {e: len(self.ops[e]) for e in self.ENGS}, nsig


class Arena:
    def __init__(self, nc, name, kib):
        self.words = kib * 256
        self.t = nc.alloc_sbuf_tensor(name, [128, self.words], F32)
        self.off = 0
        self.phase = 0

    def reset(self, to=0):
        self.off = to
        self.phase += 1

    def take(self, shape, dtype):
        n = 1
        for s in shape:
            n *= s
        words = n if dtype == F32 else (n + 1) // 2
        words = (words + 7) // 8 * 8
        assert self.off + words <= self.words, ("arena overflow", self.off, words, self.words)
        v = self.t[:, self.off:self.off + words]
        self.off += words
        if dtype != F32:
            v = v.bitcast(dtype)
        v = v[:, 0:n]
        if len(shape) == 2:
            v = v.rearrange("p (a b) -> p a b", a=shape[0])
        elif len(shape) == 3:
            v = v.rearrange("p (a b c) -> p a b c", a=shape[0], b=shape[1])
        return v


class Ring:
    def __init__(self, views, name):
        self.v = views
        self.name = name
        self.i = 0

    def next(self):
        k = self.i % len(self.v)
        self.i += 1
        return self.v[k], f"{self.name}{k}"


def build(NCORES, SP, SS):
    SA = NCORES * SS
    NPT = SP // TT
    NSO = SS // TT
    NSA = SA // TT
    NBP = SP // 128
    NBS = SS // 128 + 2
    NKs = (SP, SA)
    NQs = (SP, SS)
    NBW = (NBP, NBS)

    nc = bass.Bass("TRN2", target_bir_lowering=False)

    def din(name, shape, dt=F32):
        return nc.dram_tensor(name, list(shape), dt, kind="ExternalInput").ap()

    def dscr(name, shape, dt=BF16):
        return nc.dram_tensor(name, list(shape), dt).ap()

    xg = (din("xp", [SP, D]), din("xs", [SA, D]))
    cT_d = din("cT", [128, 32])
    wada_d = din("w_ada", [D, 6 * D])
    bT2_d = din("bT2", [128, 192])
    gmix2_d = din("gmix2", [128, 32])
    gmlp2_d = din("gmlp2", [128, 32])
    gfin_d = din("gfinT", [128, 16])
    gkv_d = din("gkvT", [128, 4])
    w1_d = din("w1", [D, 26 * 128 + 256])
    w2_d = din("w2", [D, 768])
    wkvb_d = din("wkvb", [512, 2048])
    wo_d = din("w_o", [D, D])
    wf1_d = din("w_ff1", [D, DFF])
    wf2_d = din("w_ff2", [DFF, D])
    sinkb_d = din("sinkb", [128, 8])
    relb_d = din("relb", [128, 256])
    oht_d = din("oht", [128, 99, 128])
    hm_d = din("hm", [128, 2])
    ident_d = din("ident", [128, 128])
    cos_d = (din("cosP", [128, SP]), din("cosS", [128, SA]))
    sin_d = (din("sinP", [128, SP]), din("sinS", [128, SA]))
    y_d = (nc.dram_tensor("yp", [SP, D], F32, kind="ExternalOutput").ap(),
           nc.dram_tensor("ys", [SS, D], F32, kind="ExternalOutput").ap())

    W1s = dscr("W1s", [26, 128, 2048])
    WOs = dscr("WOs", [16, 128, 2048])
    WF1s = dscr("WF1s", [64, 128, 2048])
    WF2s = dscr("WF2s", [16, 128, 8192])
    cn_s = [dscr(f"cn{g}", [NKs[g] // TT, 128, 4 * TT]) for g in range(2)]
    kr_s = [dscr(f"kr{g}", [128, NKs[g]]) for g in range(2)]
    qa_s = [dscr(f"qa{g}", [8, 128, NQs[g]]) for g in range(2)]
    qn_s = [dscr(f"qn{g}", [8, 128, NQs[g]]) for g in range(2)]
    qr_s = [dscr(f"qr{g}", [4, 128, NQs[g]]) for g in range(2)]
    ka_s = [dscr(f"ka{g}", [2, 128, NBW[g] * 128]) for g in range(2)]
    va_s = [dscr(f"va{g}", [NBW[g], 128, 256]) for g in range(2)]
    at_s = [dscr(f"at{g}", [16, 128, NQs[g]]) for g in range(2)]

    S = Sched(nc)
    SQ = "act"
    A = lambda name, shape, dt: nc.alloc_sbuf_tensor("sb_" + name, shape, dt)
    identf = A("identf", [128, 128], F32)
    identb = A("identb", [128, 128], BF16)
    onesb = A("onesb", [128, 128], BF16)
    epsb = A("epsb", [128, 1], F32)
    zerob = A("zerob", [128, 1], F32)
    modT = A("modT", [128, 192], F32)
    A1 = A("A1", [128, 32], F32)
    A2 = A("A2", [128, 32], F32)
    gmix2 = A("gmix2", [128, 32], F32)
    gmlp2 = A("gmlp2", [128, 32], F32)
    gfin = A("gfin", [128, 16], F32)
    gkv = A("gkv", [128, 4], F32)
    hm = A("hm", [128, 2], F32)
    Ttab = A("Ttab", [128, 3, 8, 128], F32)
    ES = A("ES", [128, 8, 128], F32)
    esk = A("esk", [128, 8], F32)
    relb = A("relb", [128, 264], F32)
    onesf = A("onesf", [128, 128], F32)
    B1 = modT[:, 0:32]
    G1 = modT[:, 64:96]
    B2 = modT[:, 96:128]
    G2 = modT[:, 160:192]

    pp = [nc.alloc_psum_tensor(f"pp{i}", [128, 1024], F32) for i in range(4)]
    ps = [pp[i // 2][:, (i % 2) * 512:(i % 2 + 1) * 512] for i in range(8)]
    psring = Ring(ps, "ps")
    pring6 = Ring(ps[0:6], "ps")

    class _TR:
        i = 0

        def next(self):
            k = 6 + self.i % 2
            self.i += 1
            return ps[k], f"ps{k}"
    tring = _TR()
    cur_ring = [pring6]

    def psb(i):
        return ps[i], f"ps{i}"

    arena = Arena(nc, "arena", 184)
    CASTW = 0

    S.dma("sp", identf[:, :], ident_d[:, :], writes=["identf"])
    S.add("dve", lambda e: e.tensor_copy(identb[:, :], identf[:, :]), reads=["identf"], writes=["identb"])
    S.add("dve", lambda e: e.memset(onesb[:, :], 1.0), writes=["onesb"])
    S.add("dve", lambda e: e.memset(onesf[:, :], 1.0), writes=["onesf"])
    S.add("dve", lambda e: e.memset(epsb[:, :], EPS), writes=["epsb"])
    S.add("dve", lambda e: e.memset(zerob[:, :], 0.0), writes=["zerob"])
    for t, d_, n in ((gmix2, gmix2_d, "gmix2"), (gmlp2, gmlp2_d, "gmlp2"), (gfin, gfin_d, "gfin"),
                     (gkv, gkv_d, "gkv"), (hm, hm_d, "hm"), (esk, sinkb_d, "esk")):
        S.dma("sp", t[:, :], d_[:, :], writes=[n])
    S.dma("sp", relb[:, 0:256], relb_d[:, :], writes=["relb"])
    S.add("dve", lambda e: e.memset(relb[:, 256:264], NEGM), writes=["relb"])


    def cast_slab(src_ap, dst_sb=None, dst_dram=None, dst_res=None):
        a = src_ap.shape[1]
        if dst_sb is not None:
            S.dma("pool", dst_sb, src_ap, writes=[dst_res])
        else:
            S.dma("pool", dst_dram.rearrange("p (a b) -> p a b", a=a), src_ap, writes=[dst_res])

    w2sb = arena.take([16, 768], BF16)
    vaw = arena.take([16, 256], BF16)
    for j in range(6):
        cast_slab(w2_d[:, j * 128:(j + 1) * 128].rearrange("(k p) n -> p k n", p=128),
                  dst_sb=w2sb[:, :, j * 128:(j + 1) * 128], dst_res="w2sb")
    for j in range(26):
        cast_slab(w1_d[:, j * 128:(j + 1) * 128].rearrange("(k p) n -> p k n", p=128),
                  dst_dram=W1s[j], dst_res=f"W1s{j}")
    for j in range(2):
        cast_slab(w1_d[:, 26 * 128 + j * 128:26 * 128 + (j + 1) * 128].rearrange("(k p) n -> p k n", p=128),
                  dst_sb=vaw[:, :, j * 128:(j + 1) * 128], dst_res="vaw")
    for j in range(16):
        cast_slab(wo_d[:, j * 128:(j + 1) * 128].rearrange("(k p) n -> p k n", p=128),
                  dst_dram=WOs[j], dst_res=f"WOs{j}")
    for j in range(64):
        cast_slab(wf1_d[:, j * 128:(j + 1) * 128].rearrange("(k p) n -> p k n", p=128),
                  dst_dram=WF1s[j], dst_res=f"WF1s{j}")
    for j in range(16):
        for q in range(4):
            cast_slab(wf2_d[q * 2048:(q + 1) * 2048, j * 128:(j + 1) * 128].rearrange("(k p) n -> p k n", p=128),
                      dst_dram=WF2s[j][:, q * 2048:(q + 1) * 2048], dst_res=f"WF2s{j}")

    P1_BASE = arena.off

    cTt = arena.take([32], F32)
    sT = arena.take([32], F32)
    bT2 = arena.take([192], F32)
    waring = Ring([arena.take([16, 128], F32) for _ in range(3)], "wa")
    ohst = arena.take([33, 128], F32)
    S.dma("sp", cTt, cT_d[:, :], writes=["cTt"])
    S.dma("sp", bT2, bT2_d[:, :], writes=["bT2"])
    S.add("act", lambda e: e.activation(sT, cTt, AF.Silu), reads=["cTt"], writes=["sT"])
    S.add("act", lambda e: e.activation(esk[:, :], esk[:, :], AF.Exp), reads=["esk"], writes=["esk"])
    S.add("dve", lambda e: e.tensor_copy(ES[:, :, :], esk[:, :].unsqueeze(2).to_broadcast([128, 8, 128])),
          reads=["esk"], writes=["ES"])
    pm, pmn = psb(7)
    for ch in range(96):
        wa, wan = waring.next()
        S.dma("sp", wa, wada_d[:, ch * 128:(ch + 1) * 128].rearrange("(k p) n -> p k n", p=128), writes=[wan])
        for k in range(16):
            S.mm(pm[:, 2 * ch:2 * ch + 2], wa[:, k, :], sT[:, 2 * k:2 * k + 2], k == 0, k == 15,
                 reads=[wan, "sT"], writes=[pmn])
    S.add("dve", lambda e: e.tensor_tensor(modT[:, :], pm[:, 0:192], bT2, ALU.add), reads=[pmn, "bT2"], writes=["modT"])
    S.add("dve", lambda e: e.scalar_tensor_tensor(A1[:, :], modT[:, 32:64], 1.0, gmix2[:, :], ALU.add, ALU.mult),
          reads=["modT", "gmix2"], writes=["A1"])
    S.add("dve", lambda e: e.scalar_tensor_tensor(A2[:, :], modT[:, 128:160], 1.0, gmlp2[:, :], ALU.add, ALU.mult),
          reads=["modT", "gmlp2"], writes=["A2"])
    for o in range(3):
        S.dma("sp", ohst, oht_d[:, o * 33:(o + 1) * 33, :], writes=["ohst"])
        for h in range(8):
            S.add("dve", lambda e, o=o, h=h: e.tensor_scalar_mul(Ttab[:, o, h, :], ohst[:, 0, :], relb[:, h:h + 1]),
                  reads=["ohst", "relb"], writes=["Ttab"])
            for b in range(1, 33):
                S.add("dve", lambda e, o=o, h=h, b=b: e.scalar_tensor_tensor(
                    Ttab[:, o, h, :], ohst[:, b, :], relb[:, b * 8 + h:b * 8 + h + 1], Ttab[:, o, h, :], ALU.mult, ALU.add),
                    reads=["ohst", "relb", "Ttab"], writes=["Ttab"])

    MAIN = ("pe", "act", "dve", "sp")
    S.barrier(MAIN)
    arena.reset(P1_BASE)

    xring = Ring([arena.take([2048], F32) for _ in range(4)], "xb")
    junk = arena.take([2048], BF16)
    ssvs = [arena.take([8], F32) for _ in range(2)]
    xsbs = [arena.take([4, 2048], BF16) for _ in range(2)]
    ntile = [0]
    hTs = [arena.take([16, 512], BF16) for _ in range(2)]
    hTcur = [None, None]
    w1ring = Ring([arena.take([16, 128], BF16) for _ in range(3)], "w1r")
    cosring = Ring([arena.take([512], F32) for _ in range(2)], "cos")
    sinring = Ring([arena.take([512], F32) for _ in range(2)], "sin")
    ckvf = arena.take([4, 512], F32)
    sqb = arena.take([4, 512], BF16)
    rsb = arena.take([512], F32)
    cnb = arena.take([4, 512], BF16)
    t1 = arena.take([512], F32)
    t2 = arena.take([512], F32)
    stg = Ring([arena.take([512], BF16) for _ in range(4)], "stg")
    vstg = arena.take([4, 256], BF16)
    alt = [0]

    def evac_copy(dst, src, reads, writes):
        alt[0] += 1
        if alt[0] % 2:
            S.add("act", lambda e: e.activation(dst, src, AF.Copy), reads=reads, writes=writes)
        else:
            S.add("dve", lambda e: e.tensor_copy(dst, src), reads=reads, writes=writes)

    def norm_stats(x_ap):
        par = ntile[0] % 2
        ntile[0] += 1
        ssv, xsb = ssvs[par], xsbs[par]
        ssvn, xsbn = f"ssv{par}", f"xsb{par}"
        S.add("dve", lambda e: e.memset(ssv, 0.0), writes=[f"{ssvn}_{b}" for b in range(4)])
        for blk in range(4):
            xb, xbn = xring.next()
            S.dma("sp", xb, x_ap[blk * 128:(blk + 1) * 128, :], writes=[xbn])
            S.add("act", lambda e, xb=xb, blk=blk: e.activation(junk, xb, AF.Square, scale=float(D ** -0.5),
                                                                accum_out=ssv[:, blk:blk + 1]),
                  reads=[xbn], writes=["junk", f"{ssvn}_{blk}"])
            S.add("act", lambda e, blk=blk: e.activation(ssv[:, 4 + blk:5 + blk], ssv[:, blk:blk + 1], AF.Sqrt,
                                                         bias=epsb[:, 0:1], scale=1.0),
                  reads=[f"{ssvn}_{blk}", "epsb"], writes=[f"{ssvn}_{blk}"])
            S.add("dve", lambda e, blk=blk: e.reciprocal(ssv[:, 4 + blk:5 + blk], ssv[:, 4 + blk:5 + blk]),
                  reads=[f"{ssvn}_{blk}"], writes=[f"{ssvn}_{blk}"])
            S.add("dve", lambda e, xb=xb, blk=blk: e.tensor_scalar_mul(xsb[:, blk, :], xb, ssv[:, 4 + blk:5 + blk]),
                  reads=[xbn, f"{ssvn}_{blk}"], writes=[xsbn])
        return par

    def norm_transpose(par, g):
        xsb, xsbn = xsbs[par], f"xsb{par}"
        hT, hTn = hTs[par], f"hT{par}"
        for c2 in range(8):
            pb, pbn = tring.next()
            pv = pb.bitcast(BF16)
            for cc in range(2):
                c = c2 * 2 + cc
                for blk in range(4):
                    S.tr(pv[:, cc * 512 + blk * 128:cc * 512 + (blk + 1) * 128], xsb[:, blk, c * 128:(c + 1) * 128],
                         identb[:, :], reads=[xsbn, "identb"], writes=[pbn])
            for cc in range(2):
                c = c2 * 2 + cc
                src = pv[:, cc * 512:(cc + 1) * 512]
                if cc == 0:
                    S.add("dve", lambda e, c=c, src=src: e.tensor_scalar(
                        hT[:, c, :], src, A1[:, 2 * c + g:2 * c + g + 1], B1[:, 2 * c + g:2 * c + g + 1], ALU.mult, ALU.add),
                        reads=[pbn, "A1", "modT"], writes=[hTn])
                else:
                    S.add("act", lambda e, c=c, src=src: e.activation(
                        hT[:, c, :], src, AF.Identity, bias=B1[:, 2 * c + g:2 * c + g + 1], scale=A1[:, 2 * c + g:2 * c + g + 1]),
                        reads=[pbn, "A1", "modT"], writes=[hTn])

    def load_rope(g, t0):
        cs, csn = cosring.next()
        sn, snn = sinring.next()
        S.dma("sp", cs, cos_d[g][:, t0:t0 + TT], writes=[csn])
        S.dma("sp", sn, sin_d[g][:, t0:t0 + TT], writes=[snn])
        return cs, csn, sn, snn

    def rope_out(pa, pan, pbk, pbkn, rp, dst_dram, dres):
        cs, csn, sn, snn = rp
        S.add("dve", lambda e: e.tensor_tensor(t1, pa, cs, ALU.mult), reads=[pan, csn], writes=["t1"])
        S.add("dve", lambda e: e.tensor_tensor(t2, pbk, sn, ALU.mult), reads=[pbkn, snn], writes=["t2"])
        sg, sgn = stg.next()
        S.add("dve", lambda e: e.tensor_tensor(sg, t1, t2, ALU.add), reads=["t1", "t2"], writes=[sgn])
        S.dma(SQ, dst_dram, sg, reads=[sgn], writes=[dres])

    def proj_fm(wview, wres, nk, rhs_of, rhs_res):
        pb, pbn = cur_ring[0].next()
        for k in range(nk):
            S.mm(pb, wview(k), rhs_of(k), k == 0, k == nk - 1, reads=[wres] + rhs_res, writes=[pbn])
        return pb, pbn

    def latent_tile(g, t0, rp):
        banks = []
        for j in range(6):
            banks.append(proj_fm(lambda k, j=j: w2sb[:, k, j * 128:(j + 1) * 128], "w2sb", 16,
                                 lambda k: hTcur[0][:, k, :], [hTcur[1]]))
        for j in range(4):
            pb, pbn = banks[j]
            S.add("act", lambda e, j=j, pb=pb: e.activation(ckvf[:, j, :], pb, AF.Copy), reads=[pbn], writes=["ckvf"])
            S.add("act", lambda e, j=j, pb=pb: e.activation(sqb[:, j, :], pb, AF.Square), reads=[pbn], writes=["sqb"])
        rope_out(banks[4][0], banks[4][1], banks[5][0], banks[5][1], rp, kr_s[g][:, t0:t0 + TT], f"kr{g}")

        def fin():
            pn, pnn = cur_ring[0].next()
            for j in range(4):
                S.mm(pn, onesb[:, :], sqb[:, j, :], j == 0, j == 3, reads=["onesb", "sqb"], writes=[pnn])
            S.add("act", lambda e: e.activation(rsb, pn, AF.Sqrt, bias=epsb[:, 0:1], scale=1.0 / 512), reads=[pnn, "epsb"], writes=["rsb"])
            S.add("dve", lambda e: e.reciprocal(rsb, rsb), reads=["rsb"], writes=["rsb"])
            for j in range(4):
                S.add("dve", lambda e, j=j: e.scalar_tensor_tensor(cnb[:, j, :], ckvf[:, j, :], gkv[:, j:j + 1], rsb, ALU.mult, ALU.mult),
                      reads=["ckvf", "gkv", "rsb"], writes=["cnb"])
            S.dma(SQ, cn_s[g][t0 // TT].rearrange("p (j t) -> p j t", j=4), cnb, reads=["cnb"], writes=[f"cn{g}"])
        return fin

    def w1_chunk(j):
        wv, wn = w1ring.next()
        S.dma("sp", wv, W1s[j].rearrange("p (k n) -> p k n", k=16), reads=[f"W1s{j}"], writes=[wn])
        return proj_fm(lambda k: wv[:, k, :], wn, 16, lambda k: hTcur[0][:, k, :], [hTcur[1]])

    def store_fm(pb, pbn, dst, dres, w=TT):
        sg, sgn = stg.next()
        evac_copy(sg[:, 0:w], pb, [pbn], [sgn])
        S.dma(SQ, dst, sg[:, 0:w], reads=[sgn], writes=[dres])

    def qkv_tile(g, q0, wb0, rp, own, halo_blk=None):
        if own:
            for h in range(8):
                pb, pbn = w1_chunk(h)
                store_fm(pb, pbn, qa_s[g][h][:, q0:q0 + TT], f"qa{g}")
            for h in range(8):
                pb, pbn = w1_chunk(8 + h)
                store_fm(pb, pbn, qn_s[g][h][:, q0:q0 + TT], f"qn{g}")
            for i in range(4):
                pa, pan = w1_chunk(16 + i)
                pbk, pbkn = w1_chunk(20 + i)
                rope_out(pa, pan, pbk, pbkn, rp, qr_s[g][i][:, q0:q0 + TT], f"qr{g}")
        for gi in range(2):
            pb, pbn = w1_chunk(24 + gi)
            if own:
                store_fm(pb, pbn, ka_s[g][gi][:, wb0 * 128:wb0 * 128 + TT], f"ka{g}")
            else:
                hb, wbi = halo_blk
                store_fm(pb[:, hb * 128:(hb + 1) * 128], pbn, ka_s[g][gi][:, wbi * 128:(wbi + 1) * 128], f"ka{g}", w=128)
        for half in range(2):
            pb, pbn = cur_ring[0].next()
            for b2 in range(2):
                blk = half * 2 + b2
                for k in range(16):
                    S.mm(pb[:, b2 * 256:(b2 + 1) * 256], hTcur[0][:, k, blk * 128:(blk + 1) * 128], vaw[:, k, :], k == 0, k == 15,
                         reads=[hTcur[1], "vaw"], writes=[pbn])
            evac_copy(vstg[:, half * 2:half * 2 + 2, :], pb.rearrange("p (a b) -> p a b", a=2), [pbn], ["vstg"])
        if own:
            S.dma(SQ, va_s[g][wb0:wb0 + 4].rearrange("b p n -> p b n"), vstg, reads=["vstg"], writes=[f"va{g}"])
        else:
            hb, wbi = halo_blk
            S.dma(SQ, va_s[g][wbi], vstg[:, hb, :], reads=["vstg"], writes=[f"va{g}"])

    tiles = []
    for ti in range(NPT):
        tiles.append((0, ti))
    for ti in range(NSA):
        tiles.append((1, ti))

    def tile_body(g, ti, rp):
        t0 = ti * TT
        fin = latent_tile(g, t0, rp)
        if g == 0:
            qkv_tile(0, t0, ti * 4, rp, True)
        else:
            if ti < NSO:
                qkv_tile(1, t0, 1 + ti * 4, rp, True)
            if ti == NSO % NSA and NSA > NSO:
                qkv_tile(1, 0, 0, rp, False, halo_blk=(0, NBS - 1))
            if ti == NSA - 1 and NSA > NSO:
                qkv_tile(1, 0, 0, rp, False, halo_blk=(3, 0))
        return fin

    def xt_of(k):
        return xg[tiles[k][0]][tiles[k][1] * TT:(tiles[k][1] + 1) * TT, :]

    pend_fin = [None]
    pars = {0: norm_stats(xt_of(0))}
    norm_transpose(pars[0], tiles[0][0])
    if len(tiles) > 1:
        pars[1] = norm_stats(xt_of(1))
    for i, (g, ti) in enumerate(tiles):
        par = pars.pop(i)
        rp = load_rope(g, ti * TT)
        if i + 1 < len(tiles):
            norm_transpose(pars[i + 1], tiles[i + 1][0])
        if pend_fin[0] is not None:
            pend_fin[0]()
        if i + 2 < len(tiles):
            pars[i + 2] = norm_stats(xt_of(i + 2))
        hTcur[0], hTcur[1] = hTs[par], f"hT{par}"
        pend_fin[0] = tile_body(g, ti, rp)
    pend_fin[0]()

    cur_ring[0] = psring
    S.barrier(MAIN)
    arena.reset(CASTW)

    qat = arena.take([8, 512], BF16)
    kat = arena.take([2, 768], BF16)
    vat = arena.take([6, 256], BF16)
    wtmp = Ring([arena.take([512], F32) for _ in range(2)], "wtmp")
    wP = Ring([arena.take([512], BF16) for _ in range(4)], "wP")
    den = arena.take([512], F32)
    aoT = arena.take([8, 512], BF16)
    wscale = float(128 ** -0.5)
    for g in range(2):
        nbw = NBW[g]
        for ti in range(NQs[g] // TT):
            q0 = ti * TT
            wb0 = ti * 4 + (1 if g == 1 else 0)
            lo = max(wb0 - 1, 0)
            hi = min(wb0 + 5, nbw)
            nb_ld = hi - lo
            S.dma("sp", qat, qa_s[g][:, :, q0:q0 + TT].rearrange("h p t -> p h t"), reads=[f"qa{g}"], writes=["qat"])
            S.dma("sp", kat[:, :, 0:nb_ld * 128], ka_s[g][:, :, lo * 128:hi * 128].rearrange("c p t -> p c t"),
                  reads=[f"ka{g}"], writes=["kat"])
            S.dma("sp", vat[:, 0:nb_ld, :], va_s[g][lo:hi].rearrange("b p n -> p b n"), reads=[f"va{g}"], writes=["vat"])
            for nb in range(4):
                wn = wb0 + nb
                for gi in range(2):
                    qsl = qat[:, 4 * gi:4 * gi + 4, nb * 128:(nb + 1) * 128]
                    offs = [o for o in (-1, 0, 1) if 0 <= wn + o < nbw]
                    Ps = []
                    for o in offs:
                        kb = wn + o - lo
                        pb, pbn = psring.next()
                        S.mm(pb.rearrange("p (r q) -> p r q", r=4), kat[:, gi, kb * 128:(kb + 1) * 128], qsl, True, True,
                             reads=["kat", "qat"], writes=[pbn])
                        wt, wtn = wtmp.next()
                        S.add("dve", lambda e, wt=wt, pb=pb, o=o, gi=gi: e.scalar_tensor_tensor(
                            wt.rearrange("p (r q) -> p r q", r=4), pb.rearrange("p (r q) -> p r q", r=4), wscale,
                            Ttab[:, o + 1, 4 * gi:4 * gi + 4, :], ALU.mult, ALU.add),
                            reads=[pbn, "Ttab"], writes=[wtn])
                        if g == 1 and wn + o == 0:
                            bias = hm[:, 0:1]
                        elif g == 1 and wn + o == nbw - 1:
                            bias = hm[:, 1:2]
                        else:
                            bias = zerob[:, 0:1]
                        pt, ptn = wP.next()
                        S.add("act", lambda e, pt=pt, wt=wt, bias=bias: e.activation(pt, wt, AF.Exp, bias=bias, scale=1.0),
                              reads=[wtn, "hm", "zerob"], writes=[ptn])
                        Ps.append((pt, ptn, kb))
                    po, pon = psring.next()
                    pss, pssn = psring.next()
                    for i, (pt, ptn, kb) in enumerate(Ps):
                        S.mm(po, vat[:, kb, gi * 128:(gi + 1) * 128], pt, i == 0, i == len(Ps) - 1, reads=["vat", ptn], writes=[pon])
                    for i, (pt, ptn, kb) in enumerate(Ps):
                        S.mm(pss, onesb[:, :], pt, i == 0, i == len(Ps) - 1, reads=["onesb", ptn], writes=[pssn])
                    S.add("dve", lambda e, pss=pss, gi=gi: e.tensor_tensor(
                        den.rearrange("p (r q) -> p r q", r=4), pss.rearrange("p (r q) -> p r q", r=4),
                        ES[:, 4 * gi:4 * gi + 4, :], ALU.add), reads=[pssn, "ES"], writes=["den"])
                    S.add("dve", lambda e: e.reciprocal(den, den), reads=["den"], writes=["den"])
                    S.add("dve", lambda e, po=po, gi=gi, nb=nb: e.tensor_tensor(
                        aoT[:, 4 * gi:4 * gi + 4, nb * 128:(nb + 1) * 128], po.rearrange("p (r q) -> p r q", r=4),
                        den.rearrange("p (r q) -> p r q", r=4), ALU.mult), reads=[pon, "den"], writes=["aoT"])
            S.dma(SQ, at_s[g][0:8, :, q0:q0 + TT].rearrange("h p t -> p h t"), aoT, reads=["aoT"], writes=[f"at{g}"])

    S.barrier(MAIN)
    arena.reset(CASTW)

    NKmax = max(NKs)
    NQmax = max(NQs)
    knT = arena.take([NKmax], BF16)
    vh = arena.take([NKmax // 128, 128], BF16)
    krT = arena.take([NKmax], BF16)
    cnring = Ring([arena.take([4, 512], BF16) for _ in range(4)], "cnr")
    qnh = arena.take([NQmax], BF16)
    qrh = arena.take([NQmax], BF16)
    mP = Ring([arena.take([1024], BF16) for _ in range(4)], "mP")
    ostg = Ring([arena.take([512], BF16) for _ in range(2)], "ostg")
    rec = arena.take([512], F32)
    accs2 = [[arena.take([1024], F32) for _ in range(2)] for _ in range(2)]
    wkvb = arena.take([4, 2048], BF16)
    wkst_view = accs2[0][0].rearrange("p (a b) -> p a b", a=2)
    wkst_view2 = accs2[0][1].rearrange("p (a b) -> p a b", a=2)
    for j in range(4):
        for hf, wv in enumerate((wkst_view, wkst_view2)):
            S.dma("sp", wv, wkvb_d[hf * 256:(hf + 1) * 256, j * 512:(j + 1) * 512].rearrange("(k p) n -> p k n", p=128),
                  writes=[f"acc0_{hf}"])
            evac_copy(wkvb[:, 2 * hf:2 * hf + 2, j * 512:(j + 1) * 512], wv, [f"acc0_{hf}"], ["wkvb"])
    mscale = float(192 ** -0.5)
    for g in range(2):
        NK, NQ = NKs[g], NQs[g]
        S.dma("sp", krT[:, 0:NK], kr_s[g][:, :], reads=[f"kr{g}"], writes=["krT"])
        for h in range(8):
            for kt in range(NK // TT):
                cnv, cnn = cnring.next()
                S.dma("sp", cnv, cn_s[g][kt].rearrange("p (j t) -> p j t", j=4), reads=[f"cn{g}"], writes=[cnn])
                pb, pbn = psb(4 + kt % 2)
                for j in range(4):
                    S.mm(pb, wkvb[:, j, h * 128:(h + 1) * 128], cnv[:, j, :], j == 0, j == 3, reads=["wkvb", cnn], writes=[pbn])
                S.add("act", lambda e, pb=pb, kt=kt: e.activation(knT[:, kt * TT:(kt + 1) * TT], pb, AF.Copy), reads=[pbn], writes=["knT"])
                pv_, pvn = psb(2 * (kt % 2))
                for blk in range(4):
                    for j in range(4):
                        S.mm(pv_[:, blk * 128:(blk + 1) * 128], cnv[:, j, blk * 128:(blk + 1) * 128],
                             wkvb[:, j, 1024 + h * 128:1024 + (h + 1) * 128], j == 0, j == 3, reads=["wkvb", cnn], writes=[pvn])
                S.add("dve", lambda e, pv_=pv_, kt=kt: e.tensor_copy(vh[:, kt * 4:kt * 4 + 4, :], pv_.rearrange("p (a b) -> p a b", a=4)),
                      reads=[pvn], writes=["vh"])
            S.dma("sp", qnh[:, 0:NQ], qn_s[g][h], reads=[f"qn{g}"], writes=["qnh"])
            ro = 64 * (h % 2)
            S.add("dve", lambda e, ro=ro: e.memset(qrh[64 - ro:128 - ro, :], 0.0), writes=["qrh"])
            S.dma("sp", qrh[ro:ro + 64, 0:NQ], qr_s[g][h // 2][ro:ro + 64, :], reads=[f"qr{g}"], writes=["qrh"])
            NKB = NK // 128
            NKP = NKB // 2
            pending = [None]
            for qt in range(NQ // TT):
                po, pon = psb(6 + qt % 2)
                pss, pssn = psb(5)
                accs = accs2[qt % 2]
                an = [f"acc{qt % 2}_{a}" for a in range(2)]

                def s_pair(kp, qt=qt):
                    pbp = pp[kp % 3][:, :]
                    pbn2 = [f"ps{2 * (kp % 3)}", f"ps{2 * (kp % 3) + 1}"]
                    for hf in range(2):
                        kb = 2 * kp + hf
                        o_ = pbp[:, hf * 512:(hf + 1) * 512]
                        S.mm(o_, knT[:, kb * 128:(kb + 1) * 128], qnh[:, qt * TT:(qt + 1) * TT], True, False,
                             reads=["knT", "qnh"], writes=pbn2)
                        S.mm(o_, krT[:, kb * 128:(kb + 1) * 128], qrh[:, qt * TT:(qt + 1) * TT], False, True,
                             reads=["krT", "qrh"], writes=pbn2)
                    pt, ptn = mP.next()
                    S.add("act", lambda e, pt=pt, pbp=pbp: e.activation(pt, pbp, AF.Exp, scale=mscale), reads=pbn2, writes=[ptn])
                    return pt, ptn

                def pv_pair(kp, pt, ptn, po=po, pon=pon, accs=accs, an=an):
                    for hf in range(2):
                        kb = 2 * kp + hf
                        S.mm(po, vh[:, kb, :], pt[:, hf * 512:(hf + 1) * 512], kb == 0, kb == NKB - 1, reads=["vh", ptn], writes=[pon])
                    a = kp % 2
                    eng = "dve"
                    if kp < 2:
                        S.add(eng, lambda e, a=a: e.tensor_copy(accs[a], pt), reads=[ptn], writes=[an[a]])
                    else:
                        S.add(eng, lambda e, a=a: e.tensor_tensor(accs[a], accs[a], pt, ALU.add), reads=[ptn, an[a]], writes=[an[a]])

                def epilogue_a(accs=accs, an=an):
                    S.add("dve", lambda e: e.tensor_tensor(accs[0], accs[0], accs[1], ALU.add), reads=[an[0], an[1]], writes=[an[0]])
                    S.add("dve", lambda e: e.tensor_tensor(accs[0][:, 0:512], accs[0][:, 0:512], accs[0][:, 512:1024], ALU.add),
                          reads=[an[0]], writes=[an[0]])

                def epilogue(qt=qt, po=po, pon=pon, pss=pss, pssn=pssn, accs=accs, an=an):
                    S.mm(pss, onesf[:, :], accs[0][:, 0:512], True, True, reads=["onesf", an[0]], writes=[pssn])
                    S.add("dve", lambda e: e.tensor_copy(rec, pss), reads=[pssn], writes=["rec"])
                    S.add("dve", lambda e: e.reciprocal(rec, rec), reads=["rec"], writes=["rec"])
                    sg, sgn = ostg.next()
                    S.add("dve", lambda e: e.tensor_tensor(sg, po, rec, ALU.mult), reads=[pon, "rec"], writes=[sgn])
                    S.dma(SQ, at_s[g][8 + h][:, qt * TT:(qt + 1) * TT], sg, reads=[sgn], writes=[f"at{g}"])

                pend = {0: s_pair(0), 1: s_pair(1), 2: s_pair(2)}
                for kp in range(NKP):
                    pt, ptn = pend.pop(kp)
                    pv_pair(kp, pt, ptn)
                    if kp + 3 < NKP:
                        pend[kp + 3] = s_pair(kp + 3)
                    if kp == 1 and pending[0] is not None:
                        pending[0][0]()
                        pending[0] = (None, pending[0][1])
                    if kp == min(4, NKP - 1) and pending[0] is not None:
                        if pending[0][0] is not None:
                            pending[0][0]()
                        pending[0][1]()
                        pending[0] = None
                if pending[0] is not None:
                    if pending[0][0] is not None:
                        pending[0][0]()
                    pending[0][1]()
                pending[0] = (epilogue_a, epilogue)
            pending[0][0]()
            pending[0][1]()

    S.barrier(("pe", "act", "dve", "sp", "pool"))
    arena.reset(0)

    xt = arena.take([4, 2048], F32)
    xT = arena.take([16, 512], F32)
    atT = arena.take([16, 512], BF16)
    h2T = arena.take([16, 512], BF16)
    actT = arena.take([32, 512], BF16)
    wring = Ring([arena.take([16, 128], BF16) for _ in range(3)], "wr")
    w2ring = Ring([arena.take([32, 128], BF16) for _ in range(2)], "w2r")
    yst = Ring([arena.take([2048], F32) for _ in range(2)], "yst")
    sq3 = Ring([arena.take([512], BF16) for _ in range(4)], "sq3")
    rs3 = arena.take([512], F32)
    tmp3 = Ring([arena.take([512], F32) for _ in range(2)], "tmp3")
    p3ring = Ring(ps[0:7], "ps")
    cur_ring[0] = p3ring
    pstat, pstatn = psb(7)

    class Stats:
        def __init__(self):
            self.pend = []

        def chunk(self, c):
            sq, sqn = sq3.next()
            S.add("act", lambda e, sq=sq, c=c: e.activation(sq, xT[:, c, :], AF.Square), reads=[f"xT{c}"], writes=[sqn])
            self.pend.append((c, sq, sqn))
            if len(self.pend) > 2:
                self._mm()

        def _mm(self):
            c, sq, sqn = self.pend.pop(0)
            S.mm(pstat, onesb[:, :], sq, c == 0, c == 15, reads=["onesb", sqn], writes=[pstatn])

        def fin(self):
            while self.pend:
                self._mm()
            S.add("act", lambda e: e.activation(rs3, pstat, AF.Sqrt, bias=epsb[:, 0:1], scale=1.0 / D), reads=[pstatn, "epsb"], writes=["rs3"])
            S.add("dve", lambda e: e.reciprocal(rs3, rs3), reads=["rs3"], writes=["rs3"])

    p3tiles = [(g, ti) for g in range(2) for ti in range(NQs[g] // TT)]

    class Streamer:
        def __init__(self, views, name, seq):
            self.views, self.name, self.seq = views, name, seq
            self.depth = len(views)
            self.i = 0
            for k in range(min(self.depth, len(seq))):
                self._load(k)

        def _load(self, k):
            src, res, kk = self.seq[k]
            v = self.views[k % self.depth]
            S.dma("sp", v, src.rearrange("p (k n) -> p k n", k=kk), reads=[res], writes=[f"{self.name}{k % self.depth}"])

        def get(self):
            k = self.i
            return self.views[k % self.depth], f"{self.name}{k % self.depth}"

        def release(self):
            k = self.i
            self.i += 1
            if k + self.depth < len(self.seq):
                self._load(k + self.depth)

    seq_small, seq_big = [], []
    for _ in p3tiles:
        for f in range(16):
            seq_small.append((WOs[f], f"WOs{f}", 16))
        for half in range(2):
            for j in range(32):
                seq_small.append((WF1s[half * 32 + j], f"WF1s{half * 32 + j}", 16))
            for f in range(16):
                seq_big.append((WF2s[f][:, half * 4096:(half + 1) * 4096], f"WF2s{f}", 32))
    wst = Streamer(wring.v, "wr", seq_small)
    w2st = Streamer(w2ring.v, "w2r", seq_big)

    def load_inputs(g, ti):
        t0 = ti * TT
        for blk in range(4):
            S.dma("sp", xt[:, blk, :], xg[g][t0 + blk * 128:t0 + (blk + 1) * 128, :], writes=["xt"])
        S.dma("sp", atT, at_s[g][:, :, t0:t0 + TT].rearrange("c p t -> p c t"), reads=[f"at{g}"], writes=["atT"])

    load_inputs(*p3tiles[0])
    for it, (g, ti) in enumerate(p3tiles):
        t0 = ti * TT
        for c in range(16):
            pb, pbn = p3ring.next()
            for blk in range(4):
                S.tr(pb[:, blk * 128:(blk + 1) * 128], xt[:, blk, c * 128:(c + 1) * 128], identf[:, :],
                     reads=["xt", "identf"], writes=[pbn])
            evac_copy(xT[:, c, :], pb, [pbn], [f"xT{c}"])
        st = Stats()
        for f in range(16):
            wv, wn = wst.get()
            pb, pbn = proj_fm(lambda k: wv[:, k, :], wn, 16, lambda k: atT[:, k, :], ["atT"])
            wst.release()
            S.add("dve", lambda e, f=f, pb=pb, g=g: e.scalar_tensor_tensor(
                xT[:, f, :], pb, G1[:, 2 * f + g:2 * f + g + 1], xT[:, f, :], ALU.mult, ALU.add),
                reads=[pbn, "modT", f"xT{f}"], writes=[f"xT{f}"])
            st.chunk(f)
        st.fin()
        for c in range(16):
            tp, tpn = tmp3.next()
            S.add("dve", lambda e, c=c, tp=tp, g=g: e.scalar_tensor_tensor(
                tp, xT[:, c, :], A2[:, 2 * c + g:2 * c + g + 1], rs3, ALU.mult, ALU.mult),
                reads=[f"xT{c}", "A2", "rs3"], writes=[tpn])
            S.add("act", lambda e, c=c, tp=tp, g=g: e.activation(h2T[:, c, :], tp, AF.Identity, bias=B2[:, 2 * c + g:2 * c + g + 1], scale=1.0),
                  reads=[tpn, "modT"], writes=["h2T"])
        st = Stats()
        for half in range(2):
            for j in range(32):
                f1 = half * 32 + j
                wv, wn = wst.get()
                pb, pbn = proj_fm(lambda k: wv[:, k, :], wn, 16, lambda k: h2T[:, k, :], ["h2T"])
                wst.release()
                tp, tpn = tmp3.next()
                S.add("act", lambda e, tp=tp, pb=pb: e.activation(tp, pb, AF.Relu), reads=[pbn], writes=[tpn])
                S.add("pool", lambda e, tp=tp, j=j: e.tensor_tensor(actT[:, j, :], tp, tp, ALU.mult), reads=[tpn], writes=["actT"])
            if half == 1 and it + 1 < len(p3tiles):
                load_inputs(*p3tiles[it + 1])
            for f in range(16):
                wv, wn = w2st.get()
                pb, pbn = proj_fm(lambda k: wv[:, k, :], wn, 32, lambda k: actT[:, k, :], ["actT"])
                w2st.release()
                S.add("dve", lambda e, f=f, pb=pb, g=g: e.scalar_tensor_tensor(
                    xT[:, f, :], pb, G2[:, 2 * f + g:2 * f + g + 1], xT[:, f, :], ALU.mult, ALU.add),
                    reads=[pbn, "modT", f"xT{f}"], writes=[f"xT{f}"])
                if half == 1:
                    st.chunk(f)
        st.fin()
        for c in range(16):
            S.add("dve", lambda e, c=c: e.scalar_tensor_tensor(
                xT[:, c, :], xT[:, c, :], gfin[:, c:c + 1], rs3, ALU.mult, ALU.mult),
                reads=[f"xT{c}", "gfin", "rs3"], writes=[f"xT{c}"])
        for blk in range(4):
            ys, ysn = yst.next()
            for c4 in range(4):
                pb, pbn = p3ring.next()
                for cc in range(4):
                    c = c4 * 4 + cc
                    S.tr(pb[:, cc * 128:(cc + 1) * 128], xT[:, c, blk * 128:(blk + 1) * 128], identf[:, :],
                         reads=[f"xT{c}", "identf"], writes=[pbn])
                evac_copy(ys[:, c4 * 512:(c4 + 1) * 512], pb, [pbn], [ysn])
            S.dma(SQ, y_d[g][t0 + blk * 128:t0 + (blk + 1) * 128, :], ys, reads=[ysn], writes=["y"])

    stats_ = S.finalize_and_emit()
    return nc, stats_


def _t5_bucket(rel):
    half = 16
    max_exact = 8
    ret = np.where(rel > 0, half, 0)
    n = np.abs(rel)
    nf = np.maximum(n, 1).astype(np.float32)
    large = max_exact + (np.log(nf / max_exact) / math.log(128 / max_exact) * (half - max_exact)).astype(np.int32)
    large = np.minimum(large, half - 1)
    return ret + np.where(n < max_exact, n, large)


def _oht():
    k = np.arange(128)[:, None]
    q = np.arange(128)[None, :]
    out = np.zeros((128, 99, 128), np.float32)
    for oi, o in enumerate((-1, 0, 1)):
        rel = o * 128 + k - q
        bk = _t5_bucket(rel)
        valid = np.abs(rel) <= 128
        for b in range(32):
            out[:, oi * 33 + b, :] = ((bk == b) & valid)
        out[:, oi * 33 + 32, :] = ~valid
    return out


def _rope_tables(pos):
    half = 32
    inv = (np.float32(10000.0) ** (-np.arange(half, dtype=np.float32) / np.float32(half))).astype(np.float32)
    ang = pos.astype(np.float32)[:, None] * inv[None, :]
    cos = np.cos(ang).astype(np.float32).T
    sin = np.sin(ang).astype(np.float32).T
    cosT = np.concatenate([cos, cos, cos, cos], 0)
    sinT = np.concatenate([-sin, sin, -sin, sin], 0)
    return np.ascontiguousarray(cosT), np.ascontiguousarray(sinT)


def _fm(v, reps=1):
    a = np.asarray(v, np.float32).reshape(-1, 128).T
    return np.ascontiguousarray(np.repeat(a, reps, axis=1))


_CACHE = {}


def run(inputs, NCORES, SP, SS):
    f = lambda k: np.asarray(inputs[k], np.float32)
    SA = NCORES * SS
    w_in = f("w_in")[0]
    QA, KA, VA, QN, QR, CKV, KR = 0, 1024, 1280, 1536, 2560, 3072, 3584
    sw = np.arange(512).reshape(8, 2, 32)[:, ::-1, :].reshape(-1)
    swk = np.arange(64).reshape(2, 32)[::-1].reshape(-1)
    cols1 = np.concatenate([np.arange(QA, QA + 1024), np.arange(QN, QN + 1024), np.arange(QR, QR + 512), QR + sw,
                            np.arange(KA, KA + 256), np.arange(VA, VA + 256)])
    krc = np.arange(KR, KR + 64)
    cols2 = np.concatenate([np.arange(CKV, CKV + 512), krc, krc, KR + swk, KR + swk])
    w1 = np.ascontiguousarray(w_in[:, cols1])
    w2 = np.ascontiguousarray(w_in[:, cols2])
    wk = f("w_kv_b")[0].reshape(512, 8, 256)
    wkvb = np.ascontiguousarray(np.concatenate([wk[:, :, :128].reshape(512, 1024), wk[:, :, 128:].reshape(512, 1024)], 1))
    common = {
        "w_ada": f("w_ada")[0], "bT2": _fm(f("b_ada")[0], 2), "gmix2": _fm(f("g_mix")[0], 2), "gmlp2": _fm(f("g_mlp")[0], 2),
        "gfinT": _fm(f("g_final")), "gkvT": _fm(f("g_kv")[0]), "w1": w1, "w2": w2, "wkvb": wkvb,
        "w_o": f("w_o")[0], "w_ff1": f("w_ff1")[0], "w_ff2": f("w_ff2")[0],
        "sinkb": np.ascontiguousarray(np.broadcast_to(f("sink")[0][None, :], (128, 8))),
        "relb": np.ascontiguousarray(np.broadcast_to(f("rel_bias").reshape(1, 256), (128, 256))),
        "oht": _oht(), "ident": np.eye(128, dtype=np.float32),
    }
    cosP, sinP = _rope_tables(np.arange(SP))
    common["cosP"], common["sinP"] = cosP, sinP
    xpr, xsm = f("x_prompt"), f("x_sample")[0]
    cp, cs = f("c_prompt"), f("c_sample")[0]
    in_maps = []
    for i in range(NCORES):
        m = dict(common)
        m["xp"] = np.ascontiguousarray(xpr[i])
        m["xs"] = np.ascontiguousarray(np.roll(xsm, -i * SS, axis=0))
        pos = (np.arange(SA) + i * SS) % SA
        m["cosS"], m["sinS"] = _rope_tables(pos)
        cc = np.stack([cp[i], cs], 1)
        m["cT"] = np.ascontiguousarray(cc.reshape(16, 128, 2).transpose(1, 0, 2).reshape(128, 32))
        hmv = np.zeros((128, 2), np.float32)
        if i == 0:
            hmv[:, 0] = NEGM
        if i == NCORES - 1:
            hmv[:, 1] = NEGM
        m["hm"] = hmv
        in_maps.append(m)
    key = (NCORES, SP, SS)
    if key not in _CACHE:
        _CACHE[key] = build(NCORES, SP, SS)
    nc, st = _CACHE[key]
    res = run_bass_kernel_spmd(nc, in_maps, core_ids=list(range(NCORES)))
    yp = np.stack([np.asarray(r["yp"], np.float32) for r in res.results], 0)
    ys = np.concatenate([np.asarray(r["ys"], np.float32) for r in res.results], 0)[None]
    return yp, ys


def kernel(**inputs):
    return run(inputs, 8, 2048, 2048)
```

```python
import contextlib
import math
import numpy as np
import concourse.bass as bass
import concourse.mybir as mybir
from concourse.bass_utils import run_bass_kernel_spmd

F32 = mybir.dt.float32
BF16 = mybir.dt.bfloat16
AF = mybir.ActivationFunctionType
ALU = mybir.AluOpType

D = 2048
TT = 512
DFF = 8192
EPS = 1e-6
NEGM = -30000.0
EPOCH = 4096
DMA_R = 8


class Op:
    __slots__ = ("eng", "fn", "deps", "signal", "idx", "sig", "is_dma", "tok", "waits")

    def __init__(self, eng, fn, is_dma=False):
        self.eng = eng
        self.fn = fn
        self.deps = []
        self.signal = False
        self.idx = -1
        self.sig = -1
        self.is_dma = is_dma
        self.tok = None
        self.waits = []


class Res:
    __slots__ = ("lw", "rd")

    def __init__(self):
        self.lw = None
        self.rd = []


class Sched:
    ENGS = ("pe", "act", "dve", "pool", "sp")

    def __init__(self, nc):
        self.nc = nc
        self.ops = {e: [] for e in self.ENGS}
        self.res = {}
        self.dma_n = {e: 0 for e in self.ENGS}
        self.dma_ops = {e: [] for e in self.ENGS}
        self.bar = {e: [] for e in self.ENGS}

    def _r(self, name):
        r = self.res.get(name)
        if r is None:
            r = self.res[name] = Res()
        return r

    def barrier(self, engs):
        lst = []
        for e in engs:
            ops = self.ops[e]
            for o in reversed(ops):
                if not o.is_dma:
                    lst.append(o)
                    break
            lst.extend(self.dma_ops[e][-DMA_R:])
        for e in engs:
            self.bar[e] = list(lst)

    def add(self, eng, fn, reads=(), writes=(), is_dma=False):
        op = Op(eng, fn, is_dma)
        deps = []
        for n in reads:
            r = self._r(n)
            if r.lw is not None:
                deps.append((r.lw, "raw"))
        for n in writes:
            r = self._r(n)
            if r.lw is not None:
                deps.append((r.lw, "waw"))
            for o in r.rd:
                deps.append((o, "war"))
        for d, kind in deps:
            if d is op:
                continue
            if not d.is_dma and d.eng == eng and not is_dma:
                if kind != "raw" or eng == "pe":
                    continue
            op.deps.append(d)
        if self.bar[eng]:
            for d in self.bar[eng]:
                if d.is_dma or d.eng != eng:
                    op.deps.append(d)
            self.bar[eng] = []
        for n in reads:
            self._r(n).rd.append(op)
        for n in writes:
            r = self._r(n)
            r.lw = op
            r.rd = []
        op.idx = len(self.ops[eng])
        self.ops[eng].append(op)
        if is_dma:
            n = self.dma_n[eng]
            self.dma_n[eng] = n + 1
            op.tok = (eng, n % DMA_R, 16 * (n // DMA_R + 1))
            if n >= DMA_R:
                op.deps.append(self.dma_ops[eng][n - DMA_R])
            self.dma_ops[eng].append(op)
        return op

    def dma(self, q, out, in_, reads=(), writes=()):
        return self.add(q, lambda e: e.dma_start(out=out, in_=in_), reads, writes, is_dma=True)

    def mm(self, out, lhsT, rhs, start, stop, reads=(), writes=()):
        return self.add("pe", lambda e: e.matmul(out, lhsT, rhs, start=start, stop=stop), reads, writes)

    def tr(self, out, in_, ident, reads=(), writes=()):
        return self.add("pe", lambda e: e.transpose(out, in_, ident), reads, writes)

    def finalize_and_emit(self):
        nc = self.nc
        for eng in self.ENGS:
            seen = {}
            seen_dma = {}
            for op in self.ops[eng]:
                for d in op.deps:
                    if d.is_dma:
                        k = (d.tok[0], d.tok[1])
                        if seen_dma.get(k, 0) < d.tok[2]:
                            seen_dma[k] = d.tok[2]
                            op.waits.append(d)
                    else:
                        if seen.get(d.eng, -1) < d.idx:
                            seen[d.eng] = d.idx
                            d.signal = True
                            op.waits.append(d)
        final_waits = []
        for q in self.ENGS:
            final_waits.extend(self.dma_ops[q][-DMA_R:])
        nsig = {}
        for eng in self.ENGS:
            c = 0
            for op in self.ops[eng]:
                if op.signal and not op.is_dma:
                    op.sig = c
                    c += 1
            nsig[eng] = c
        with contextlib.ExitStack() as st:
            csem = {}
            for eng in self.ENGS:
                n_ep = (nsig[eng] + EPOCH - 1) // EPOCH
                csem[eng] = [st.enter_context(nc.semaphore(f"c_{eng}_{i}")) for i in range(max(n_ep, 1))]
            dsem = {}
            for q in self.ENGS:
                if self.dma_n[q]:
                    dsem[q] = [st.enter_context(nc.semaphore(f"d_{q}_{i}")) for i in range(DMA_R)]

            def emit(eng, e):
                for op in self.ops[eng]:
                    for d in op.waits:
                        if d.is_dma:
                            e.wait_ge(dsem[d.tok[0]][d.tok[1]], d.tok[2])
                        else:
                            e.wait_ge(csem[d.eng][d.sig // EPOCH], d.sig % EPOCH + 1)
                    ins = op.fn(e)
                    if op.is_dma:
                        ins.then_inc(dsem[op.tok[0]][op.tok[1]], 16)
                    elif op.signal:
                        ins.then_inc(csem[eng][op.sig // EPOCH], 1)
                if eng == "sp":
                    for d in final_waits:
                        e.wait_ge(dsem[d.tok[0]][d.tok[1]], d.tok[2])

            with nc.Block() as block:
                @block.tensor
                def _(e):
                    emit("pe", e)

                @block.scalar
                def _(e):
                    emit("act", e)

                @block.vector
                def _(e):
                    emit("dve", e)

                @block.gpsimd
                def _(e):
                    emit("pool", e)

                @block.sync
                def _(e):
                    emit("sp", e)
        return {e: len(self.ops[e]) for e in self.ENGS}, nsig


class Arena:
    def __init__(self, nc, name, kib):
        self.words = kib * 256
        self.t = nc.alloc_sbuf_tensor(name, [128, self.words], F32)
        self.off = 0
        self.phase = 0

    def reset(self, to=0):
        self.off = to
        self.phase += 1

    def take(self, shape, dtype):
        n = 1
        for s in shape:
            n *= s
        words = n if dtype == F32 else (n + 1) // 2
        words = (words + 7) // 8 * 8
        assert self.off + words <= self.words, ("arena overflow", self.off, words, self.words)
        v = self.t[:, self.off:self.off + words]
        self.off += words
        if dtype != F32:
            v = v.bitcast(dtype)
        v = v[:, 0:n]
        if len(shape) == 2:
            v = v.rearrange("p (a b) -> p a b", a=shape[0])
        elif len(shape) == 3:
            v = v.rearrange("p (a b c) -> p a b c", a=shape[0], b=shape[1])
        return v


class Ring:
    def __init__(self, views, name):
        self.v = views
        self.name = name
        self.i = 0

    def next(self):
        k = self.i % len(self.v)
        self.i += 1
        return self.v[k], f"{self.name}{k}"


def build(NCORES, SP, SS):
    SA = NCORES * SS
    NPT = SP // TT
    NSO = SS // TT
    NSA = SA // TT
    NBP = SP // 128
    NBS = SS // 128 + 2
    NKs = (SP, SA)
    NQs = (SP, SS)
    NBW = (NBP, NBS)

    nc = bass.Bass("TRN2", target_bir_lowering=False)

    def din(name, shape, dt=F32):
        return nc.dram_tensor(name, list(shape), dt, kind="ExternalInput").ap()

    def dscr(name, shape, dt=BF16):
        return nc.dram_tensor(name, list(shape), dt).ap()

    xg = (din("xp", [SP, D]), din("xs", [SA, D]))
    cT_d = din("cT", [128, 32])
    wada_d = din("w_ada", [D, 6 * D])
    bT2_d = din("bT2", [128, 192])
    gmix2_d = din("gmix2", [128, 32])
    gmlp2_d = din("gmlp2", [128, 32])
    gfin_d = din("gfinT", [128, 16])
    gkv_d = din("gkvT", [128, 4])
    w1_d = din("w1", [D, 26 * 128 + 256])
    w2_d = din("w2", [D, 768])
    wkvb_d = din("wkvb", [512, 2048])
    wo_d = din("w_o", [D, D])
    wf1_d = din("w_ff1", [D, DFF])
    wf2_d = din("w_ff2", [DFF, D])
    sinkb_d = din("sinkb", [128, 8])
    relb_d = din("relb", [128, 256])
    oht_d = din("oht", [128, 99, 128])
    hm_d = din("hm", [128, 2])
    ident_d = din("ident", [128, 128])
    cos_d = (din("cosP", [128, SP]), din("cosS", [128, SA]))
    sin_d = (din("sinP", [128, SP]), din("sinS", [128, SA]))
    y_d = (nc.dram_tensor("yp", [SP, D], F32, kind="ExternalOutput").ap(),
           nc.dram_tensor("ys", [SS, D], F32, kind="ExternalOutput").ap())

    W1s = dscr("W1s", [26, 128, 2048])
    WOs = dscr("WOs", [16, 128, 2048])
    WF1s = dscr("WF1s", [64, 128, 2048])
    WF2s = dscr("WF2s", [16, 128, 8192])
    cn_s = [dscr(f"cn{g}", [NKs[g] // TT, 128, 4 * TT]) for g in range(2)]
    kr_s = [dscr(f"kr{g}", [128, NKs[g]]) for g in range(2)]
    qa_s = [dscr(f"qa{g}", [8, 128, NQs[g]]) for g in range(2)]
    qn_s = [dscr(f"qn{g}", [8, 128, NQs[g]]) for g in range(2)]
    qr_s = [dscr(f"qr{g}", [4, 128, NQs[g]]) for g in range(2)]
    ka_s = [dscr(f"ka{g}", [2, 128, NBW[g] * 128]) for g in range(2)]
    va_s = [dscr(f"va{g}", [NBW[g], 128, 256]) for g in range(2)]
    at_s = [dscr(f"at{g}", [16, 128, NQs[g]]) for g in range(2)]

    S = Sched(nc)
    SQ = "act"
    A = lambda name, shape, dt: nc.alloc_sbuf_tensor("sb_" + name, shape, dt)
    identf = A("identf", [128, 128], F32)
    identb = A("identb", [128, 128], BF16)
    onesb = A("onesb", [128, 128], BF16)
    epsb = A("epsb", [128, 1], F32)
    zerob = A("zerob", [128, 1], F32)
    modT = A("modT", [128, 192], F32)
    A1 = A("A1", [128, 32], F32)
    A2 = A("A2", [128, 32], F32)
    gmix2 = A("gmix2", [128, 32], F32)
    gmlp2 = A("gmlp2", [128, 32], F32)
    gfin = A("gfin", [128, 16], F32)
    gkv = A("gkv", [128, 4], F32)
    hm = A("hm", [128, 2], F32)
    Ttab = A("Ttab", [128, 3, 8, 128], F32)
    ES = A("ES", [128, 8, 128], F32)
    esk = A("esk", [128, 8], F32)
    relb = A("relb", [128, 264], F32)
    onesf = A("onesf", [128, 128], F32)
    B1 = modT[:, 0:32]
    G1 = modT[:, 64:96]
    B2 = modT[:, 96:128]
    G2 = modT[:, 160:192]

    pp = [nc.alloc_psum_tensor(f"pp{i}", [128, 1024], F32) for i in range(4)]
    ps = [pp[i // 2][:, (i % 2) * 512:(i % 2 + 1) * 512] for i in range(8)]
    psring = Ring(ps, "ps")
    pring6 = Ring(ps[0:6], "ps")

    class _TR:
        i = 0

        def next(self):
            k = 6 + self.i % 2
            self.i += 1
            return ps[k], f"ps{k}"
    tring = _TR()
    cur_ring = [pring6]

    def psb(i):
        return ps[i], f"ps{i}"

    arena = Arena(nc, "arena", 184)
    CASTW = 0

    S.dma("sp", identf[:, :], ident_d[:, :], writes=["identf"])
    S.add("dve", lambda e: e.tensor_copy(identb[:, :], identf[:, :]), reads=["identf"], writes=["identb"])
    S.add("dve", lambda e: e.memset(onesb[:, :], 1.0), writes=["onesb"])
    S.add("dve", lambda e: e.memset(onesf[:, :], 1.0), writes=["onesf"])
    S.add("dve", lambda e: e.memset(epsb[:, :], EPS), writes=["epsb"])
    S.add("dve", lambda e: e.memset(zerob[:, :], 0.0), writes=["zerob"])
    for t, d_, n in ((gmix2, gmix2_d, "gmix2"), (gmlp2, gmlp2_d, "gmlp2"), (gfin, gfin_d, "gfin"),
                     (gkv, gkv_d, "gkv"), (hm, hm_d, "hm"), (esk, sinkb_d, "esk")):
        S.dma("sp", t[:, :], d_[:, :], writes=[n])
    S.dma("sp", relb[:, 0:256], relb_d[:, :], writes=["relb"])
    S.add("dve", lambda e: e.memset(relb[:, 256:264], NEGM), writes=["relb"])


    def cast_slab(src_ap, dst_sb=None, dst_dram=None, dst_res=None):
        a = src_ap.shape[1]
        if dst_sb is not None:
            S.dma("pool", dst_sb, src_ap, writes=[dst_res])
        else:
            S.dma("pool", dst_dram.rearrange("p (a b) -> p a b", a=a), src_ap, writes=[dst_res])

    w2sb = arena.take([16, 768], BF16)
    vaw = arena.take([16, 256], BF16)
    for j in range(6):
        cast_slab(w2_d[:, j * 128:(j + 1) * 128].rearrange("(k p) n -> p k n", p=128),
                  dst_sb=w2sb[:, :, j * 128:(j + 1) * 128], dst_res="w2sb")
    for j in range(26):
        cast_slab(w1_d[:, j * 128:(j + 1) * 128].rearrange("(k p) n -> p k n", p=128),
                  dst_dram=W1s[j], dst_res=f"W1s{j}")
    for j in range(2):
        cast_slab(w1_d[:, 26 * 128 + j * 128:26 * 128 + (j + 1) * 128].rearrange("(k p) n -> p k n", p=128),
                  dst_sb=vaw[:, :, j * 128:(j + 1) * 128], dst_res="vaw")
    for j in range(16):
        cast_slab(wo_d[:, j * 128:(j + 1) * 128].rearrange("(k p) n -> p k n", p=128),
                  dst_dram=WOs[j], dst_res=f"WOs{j}")
    for j in range(64):
        cast_slab(wf1_d[:, j * 128:(j + 1) * 128].rearrange("(k p) n -> p k n", p=128),
                  dst_dram=WF1s[j], dst_res=f"WF1s{j}")
    for j in range(16):
        for q in range(4):
            cast_slab(wf2_d[q * 2048:(q + 1) * 2048, j * 128:(j + 1) * 128].rearrange("(k p) n -> p k n", p=128),
                      dst_dram=WF2s[j][:, q * 2048:(q + 1) * 2048], dst_res=f"WF2s{j}")

    P1_BASE = arena.off

    cTt = arena.take([32], F32)
    sT = arena.take([32], F32)
    bT2 = arena.take([192], F32)
    waring = Ring([arena.take([16, 128], F32) for _ in range(3)], "wa")
    ohst = arena.take([33, 128], F32)
    S.dma("sp", cTt, cT_d[:, :], writes=["cTt"])
    S.dma("sp", bT2, bT2_d[:, :], writes=["bT2"])
    S.add("act", lambda e: e.activation(sT, cTt, AF.Silu), reads=["cTt"], writes=["sT"])
    S.add("act", lambda e: e.activation(esk[:, :], esk[:, :], AF.Exp), reads=["esk"], writes=["esk"])
    S.add("dve", lambda e: e.tensor_copy(ES[:, :, :], esk[:, :].unsqueeze(2).to_broadcast([128, 8, 128])),
          reads=["esk"], writes=["ES"])
    pm, pmn = psb(7)
    for ch in range(96):
        wa, wan = waring.next()
        S.dma("sp", wa, wada_d[:, ch * 128:(ch + 1) * 128].rearrange("(k p) n -> p k n", p=128), writes=[wan])
        for k in range(16):
            S.mm(pm[:, 2 * ch:2 * ch + 2], wa[:, k, :], sT[:, 2 * k:2 * k + 2], k == 0, k == 15,
                 reads=[wan, "sT"], writes=[pmn])
    S.add("dve", lambda e: e.tensor_tensor(modT[:, :], pm[:, 0:192], bT2, ALU.add), reads=[pmn, "bT2"], writes=["modT"])
    S.add("dve", lambda e: e.scalar_tensor_tensor(A1[:, :], modT[:, 32:64], 1.0, gmix2[:, :], ALU.add, ALU.mult),
          reads=["modT", "gmix2"], writes=["A1"])
    S.add("dve", lambda e: e.scalar_tensor_tensor(A2[:, :], modT[:, 128:160], 1.0, gmlp2[:, :], ALU.add, ALU.mult),
          reads=["modT", "gmlp2"], writes=["A2"])
    for o in range(3):
        S.dma("sp", ohst, oht_d[:, o * 33:(o + 1) * 33, :], writes=["ohst"])
        for h in range(8):
            S.add("dve", lambda e, o=o, h=h: e.tensor_scalar_mul(Ttab[:, o, h, :], ohst[:, 0, :], relb[:, h:h + 1]),
                  reads=["ohst", "relb"], writes=["Ttab"])
            for b in range(1, 33):
                S.add("dve", lambda e, o=o, h=h, b=b: e.scalar_tensor_tensor(
                    Ttab[:, o, h, :], ohst[:, b, :], relb[:, b * 8 + h:b * 8 + h + 1], Ttab[:, o, h, :], ALU.mult, ALU.add),
                    reads=["ohst", "relb", "Ttab"], writes=["Ttab"])

    MAIN = ("pe", "act", "dve", "sp")
    S.barrier(MAIN)
    arena.reset(P1_BASE)

    xring = Ring([arena.take([2048], F32) for _ in range(4)], "xb")
    junk = arena.take([2048], BF16)
    ssvs = [arena.take([8], F32) for _ in range(2)]
    xsbs = [arena.take([4, 2048], BF16) for _ in range(2)]
    ntile = [0]
    hTs = [arena.take([16, 512], BF16) for _ in range(2)]
    hTcur = [None, None]
    w1ring = Ring([arena.take([16, 128], BF16) for _ in range(3)], "w1r")
    cosring = Ring([arena.take([512], F32) for _ in range(2)], "cos")
    sinring = Ring([arena.take([512], F32) for _ in range(2)], "sin")
    ckvf = arena.take([4, 512], F32)
    sqb = arena.take([4, 512], BF16)
    rsb = arena.take([512], F32)
    cnb = arena.take([4, 512], BF16)
    t1 = arena.take([512], F32)
    t2 = arena.take([512], F32)
    stg = Ring([arena.take([512], BF16) for _ in range(4)], "stg")
    vstg = arena.take([4, 256], BF16)
    alt = [0]

    def evac_copy(dst, src, reads, writes):
        alt[0] += 1
        if alt[0] % 2:
            S.add("act", lambda e: e.activation(dst, src, AF.Copy), reads=reads, writes=writes)
        else:
            S.add("dve", lambda e: e.tensor_copy(dst, src), reads=reads, writes=writes)

    def norm_stats(x_ap):
        par = ntile[0] % 2
        ntile[0] += 1
        ssv, xsb = ssvs[par], xsbs[par]
        ssvn, xsbn = f"ssv{par}", f"xsb{par}"
        S.add("dve", lambda e: e.memset(ssv, 0.0), writes=[f"{ssvn}_{b}" for b in range(4)])
        for blk in range(4):
            xb, xbn = xring.next()
            S.dma("sp", xb, x_ap[blk * 128:(blk + 1) * 128, :], writes=[xbn])
            S.add("act", lambda e, xb=xb, blk=blk: e.activation(junk, xb, AF.Square, scale=float(D ** -0.5),
                                                                accum_out=ssv[:, blk:blk + 1]),
                  reads=[xbn], writes=["junk", f"{ssvn}_{blk}"])
            S.add("act", lambda e, blk=blk: e.activation(ssv[:, 4 + blk:5 + blk], ssv[:, blk:blk + 1], AF.Sqrt,
                                                         bias=epsb[:, 0:1], scale=1.0),
                  reads=[f"{ssvn}_{blk}", "epsb"], writes=[f"{ssvn}_{blk}"])
            S.add("dve", lambda e, blk=blk: e.reciprocal(ssv[:, 4 + blk:5 + blk], ssv[:, 4 + blk:5 + blk]),
                  reads=[f"{ssvn}_{blk}"], writes=[f"{ssvn}_{blk}"])
            S.add("dve", lambda e, xb=xb, blk=blk: e.tensor_scalar_mul(xsb[:, blk, :], xb, ssv[:, 4 + blk:5 + blk]),
                  reads=[xbn, f"{ssvn}_{blk}"], writes=[xsbn])
        return par

    def norm_transpose(par, g):
        xsb, xsbn = xsbs[par], f"xsb{par}"
        hT, hTn = hTs[par], f"hT{par}"
        for c2 in range(8):
            pb, pbn = tring.next()
            pv = pb.bitcast(BF16)
            for cc in range(2):
                c = c2 * 2 + cc
                for blk in range(4):
                    S.tr(pv[:, cc * 512 + blk * 128:cc * 512 + (blk + 1) * 128], xsb[:, blk, c * 128:(c + 1) * 128],
                         identb[:, :], reads=[xsbn, "identb"], writes=[pbn])
            for cc in range(2):
                c = c2 * 2 + cc
                src = pv[:, cc * 512:(cc + 1) * 512]
                if cc == 0:
                    S.add("dve", lambda e, c=c, src=src: e.tensor_scalar(
                        hT[:, c, :], src, A1[:, 2 * c + g:2 * c + g + 1], B1[:, 2 * c + g:2 * c + g + 1], ALU.mult, ALU.add),
                        reads=[pbn, "A1", "modT"], writes=[hTn])
                else:
                    S.add("act", lambda e, c=c, src=src: e.activation(
                        hT[:, c, :], src, AF.Identity, bias=B1[:, 2 * c + g:2 * c + g + 1], scale=A1[:, 2 * c + g:2 * c + g + 1]),
                        reads=[pbn, "A1", "modT"], writes=[hTn])

    def load_rope(g, t0):
        cs, csn = cosring.next()
        sn, snn = sinring.next()
        S.dma("sp", cs, cos_d[g][:, t0:t0 + TT], writes=[csn])
        S.dma("sp", sn, sin_d[g][:, t0:t0 + TT], writes=[snn])
        return cs, csn, sn, snn

    def rope_out(pa, pan, pbk, pbkn, rp, dst_dram, dres):
        cs, csn, sn, snn = rp
        S.add("dve", lambda e: e.tensor_tensor(t1, pa, cs, ALU.mult), reads=[pan, csn], writes=["t1"])
        S.add("dve", lambda e: e.tensor_tensor(t2, pbk, sn, ALU.mult), reads=[pbkn, snn], writes=["t2"])
        sg, sgn = stg.next()
        S.add("dve", lambda e: e.tensor_tensor(sg, t1, t2, ALU.add), reads=["t1", "t2"], writes=[sgn])
        S.dma(SQ, dst_dram, sg, reads=[sgn], writes=[dres])

    def proj_fm(wview, wres, nk, rhs_of, rhs_res):
        pb, pbn = cur_ring[0].next()
        for k in range(nk):
            S.mm(pb, wview(k), rhs_of(k), k == 0, k == nk - 1, reads=[wres] + rhs_res, writes=[pbn])
        return pb, pbn

    def latent_tile(g, t0, rp):
        banks = []
        for j in range(6):
            banks.append(proj_fm(lambda k, j=j: w2sb[:, k, j * 128:(j + 1) * 128], "w2sb", 16,
                                 lambda k: hTcur[0][:, k, :], [hTcur[1]]))
        for j in range(4):
            pb, pbn = banks[j]
            S.add("act", lambda e, j=j, pb=pb: e.activation(ckvf[:, j, :], pb, AF.Copy), reads=[pbn], writes=["ckvf"])
            S.add("act", lambda e, j=j, pb=pb: e.activation(sqb[:, j, :], pb, AF.Square), reads=[pbn], writes=["sqb"])
        rope_out(banks[4][0], banks[4][1], banks[5][0], banks[5][1], rp, kr_s[g][:, t0:t0 + TT], f"kr{g}")

        def fin():
            pn, pnn = cur_ring[0].next()
            for j in range(4):
                S.mm(pn, onesb[:, :], sqb[:, j, :], j == 0, j == 3, reads=["onesb", "sqb"], writes=[pnn])
            S.add("act", lambda e: e.activation(rsb, pn, AF.Sqrt, bias=epsb[:, 0:1], scale=1.0 / 512), reads=[pnn, "epsb"], writes=["rsb"])
            S.add("dve", lambda e: e.reciprocal(rsb, rsb), reads=["rsb"], writes=["rsb"])
            for j in range(4):
                S.add("dve", lambda e, j=j: e.scalar_tensor_tensor(cnb[:, j, :], ckvf[:, j, :], gkv[:, j:j + 1], rsb, ALU.mult, ALU.mult),
                      reads=["ckvf", "gkv", "rsb"], writes=["cnb"])
            S.dma(SQ, cn_s[g][t0 // TT].rearrange("p (j t) -> p j t", j=4), cnb, reads=["cnb"], writes=[f"cn{g}"])
        return fin

    def w1_chunk(j):
        wv, wn = w1ring.next()
        S.dma("sp", wv, W1s[j].rearrange("p (k n) -> p k n", k=16), reads=[f"W1s{j}"], writes=[wn])
        return proj_fm(lambda k: wv[:, k, :], wn, 16, lambda k: hTcur[0][:, k, :], [hTcur[1]])

    def store_fm(pb, pbn, dst, dres, w=TT):
        sg, sgn = stg.next()
        evac_copy(sg[:, 0:w], pb, [pbn], [sgn])
        S.dma(SQ, dst, sg[:, 0:w], reads=[sgn], writes=[dres])

    def qkv_tile(g, q0, wb0, rp, own, halo_blk=None):
        if own:
            for h in range(8):
                pb, pbn = w1_chunk(h)
                store_fm(pb, pbn, qa_s[g][h][:, q0:q0 + TT], f"qa{g}")
            for h in range(8):
                pb, pbn = w1_chunk(8 + h)
                store_fm(pb, pbn, qn_s[g][h][:, q0:q0 + TT], f"qn{g}")
            for i in range(4):
                pa, pan = w1_chunk(16 + i)
                pbk, pbkn = w1_chunk(20 + i)
                rope_out(pa, pan, pbk, pbkn, rp, qr_s[g][i][:, q0:q0 + TT], f"qr{g}")
        for gi in range(2):
            pb, pbn = w1_chunk(24 + gi)
            if own:
                store_fm(pb, pbn, ka_s[g][gi][:, wb0 * 128:wb0 * 128 + TT], f"ka{g}")
            else:
                hb, wbi = halo_blk
                store_fm(pb[:, hb * 128:(hb + 1) * 128], pbn, ka_s[g][gi][:, wbi * 128:(wbi + 1) * 128], f"ka{g}", w=128)
        for half in range(2):
            pb, pbn = cur_ring[0].next()
            for b2 in range(2):
                blk = half * 2 + b2
                for k in range(16):
                    S.mm(pb[:, b2 * 256:(b2 + 1) * 256], hTcur[0][:, k, blk * 128:(blk + 1) * 128], vaw[:, k, :], k == 0, k == 15,
                         reads=[hTcur[1], "vaw"], writes=[pbn])
            evac_copy(vstg[:, half * 2:half * 2 + 2, :], pb.rearrange("p (a b) -> p a b", a=2), [pbn], ["vstg"])
        if own:
            S.dma(SQ, va_s[g][wb0:wb0 + 4].rearrange("b p n -> p b n"), vstg, reads=["vstg"], writes=[f"va{g}"])
        else:
            hb, wbi = halo_blk
            S.dma(SQ, va_s[g][wbi], vstg[:, hb, :], reads=["vstg"], writes=[f"va{g}"])

    tiles = []
    for ti in range(NPT):
        tiles.append((0, ti))
    for ti in range(NSA):
        tiles.append((1, ti))

    def tile_body(g, ti, rp):
        t0 = ti * TT
        fin = latent_tile(g, t0, rp)
        if g == 0:
            qkv_tile(0, t0, ti * 4, rp, True)
        else:
            if ti < NSO:
                qkv_tile(1, t0, 1 + ti * 4, rp, True)
            if ti == NSO % NSA and NSA > NSO:
                qkv_tile(1, 0, 0, rp, False, halo_blk=(0, NBS - 1))
            if ti == NSA - 1 and NSA > NSO:
                qkv_tile(1, 0, 0, rp, False, halo_blk=(3, 0))
        return fin

    def xt_of(k):
        return xg[tiles[k][0]][tiles[k][1] * TT:(tiles[k][1] + 1) * TT, :]

    pend_fin = [None]
    pars = {0: norm_stats(xt_of(0))}
    norm_transpose(pars[0], tiles[0][0])
    if len(tiles) > 1:
        pars[1] = norm_stats(xt_of(1))
    for i, (g, ti) in enumerate(tiles):
        par = pars.pop(i)
        rp = load_rope(g, ti * TT)
        if i + 1 < len(tiles):
            norm_transpose(pars[i + 1], tiles[i + 1][0])
        if pend_fin[0] is not None:
            pend_fin[0]()
        if i + 2 < len(tiles):
            pars[i + 2] = norm_stats(xt_of(i + 2))
        hTcur[0], hTcur[1] = hTs[par], f"hT{par}"
        pend_fin[0] = tile_body(g, ti, rp)
    pend_fin[0]()

    cur_ring[0] = psring
    S.barrier(MAIN)
    arena.reset(CASTW)

    qat = arena.take([8, 512], BF16)
    kat = arena.take([2, 768], BF16)
    vat = arena.take([6, 256], BF16)
    wtmp = Ring([arena.take([512], F32) for _ in range(2)], "wtmp")
    wP = Ring([arena.take([512], BF16) for _ in range(4)], "wP")
    den = arena.take([512], F32)
    aoT = arena.take([8, 512], BF16)
    wscale = float(128 ** -0.5)
    for g in range(2):
        nbw = NBW[g]
        for ti in range(NQs[g] // TT):
            q0 = ti * TT
            wb0 = ti * 4 + (1 if g == 1 else 0)
            lo = max(wb0 - 1, 0)
            hi = min(wb0 + 5, nbw)
            nb_ld = hi - lo
            S.dma("sp", qat, qa_s[g][:, :, q0:q0 + TT].rearrange("h p t -> p h t"), reads=[f"qa{g}"], writes=["qat"])
            S.dma("sp", kat[:, :, 0:nb_ld * 128], ka_s[g][:, :, lo * 128:hi * 128].rearrange("c p t -> p c t"),
                  reads=[f"ka{g}"], writes=["kat"])
            S.dma("sp", vat[:, 0:nb_ld, :], va_s[g][lo:hi].rearrange("b p n -> p b n"), reads=[f"va{g}"], writes=["vat"])
            for nb in range(4):
                wn = wb0 + nb
                for gi in range(2):
                    qsl = qat[:, 4 * gi:4 * gi + 4, nb * 128:(nb + 1) * 128]
                    offs = [o for o in (-1, 0, 1) if 0 <= wn + o < nbw]
                    Ps = []
                    for o in offs:
                        kb = wn + o - lo
                        pb, pbn = psring.next()
                        S.mm(pb.rearrange("p (r q) -> p r q", r=4), kat[:, gi, kb * 128:(kb + 1) * 128], qsl, True, True,
                             reads=["kat", "qat"], writes=[pbn])
                        wt, wtn = wtmp.next()
                        S.add("dve", lambda e, wt=wt, pb=pb, o=o, gi=gi: e.scalar_tensor_tensor(
                            wt.rearrange("p (r q) -> p r q", r=4), pb.rearrange("p (r q) -> p r q", r=4), wscale,
                            Ttab[:, o + 1, 4 * gi:4 * gi + 4, :], ALU.mult, ALU.add),
                            reads=[pbn, "Ttab"], writes=[wtn])
                        if g == 1 and wn + o == 0:
                            bias = hm[:, 0:1]
                        elif g == 1 and wn + o == nbw - 1:
                            bias = hm[:, 1:2]
                        else:
                            bias = zerob[:, 0:1]
                        pt, ptn = wP.next()
                        S.add("act", lambda e, pt=pt, wt=wt, bias=bias: e.activation(pt, wt, AF.Exp, bias=bias, scale=1.0),
                              reads=[wtn, "hm", "zerob"], writes=[ptn])
                        Ps.append((pt, ptn, kb))
                    po, pon = psring.next()
                    pss, pssn = psring.next()
                    for i, (pt, ptn, kb) in enumerate(Ps):
                        S.mm(po, vat[:, kb, gi * 128:(gi + 1) * 128], pt, i == 0, i == len(Ps) - 1, reads=["vat", ptn], writes=[pon])
                    for i, (pt, ptn, kb) in enumerate(Ps):
                        S.mm(pss, onesb[:, :], pt, i == 0, i == len(Ps) - 1, reads=["onesb", ptn], writes=[pssn])
                    S.add("dve", lambda e, pss=pss, gi=gi: e.tensor_tensor(
                        den.rearrange("p (r q) -> p r q", r=4), pss.rearrange("p (r q) -> p r q", r=4),
                        ES[:, 4 * gi:4 * gi + 4, :], ALU.add), reads=[pssn, "ES"], writes=["den"])
                    S.add("dve", lambda e: e.reciprocal(den, den), reads=["den"], writes=["den"])
                    S.add("dve", lambda e, po=po, gi=gi, nb=nb: e.tensor_tensor(
                        aoT[:, 4 * gi:4 * gi + 4, nb * 128:(nb + 1) * 128], po.rearrange("p (r q) -> p r q", r=4),
                        den.rearrange("p (r q) -> p r q", r=4), ALU.mult), reads=[pon, "den"], writes=["aoT"])
            S.dma(SQ, at_s[g][0:8, :, q0:q0 + TT].rearrange("h p t -> p h t"), aoT, reads=["aoT"], writes=[f"at{g}"])

    S.barrier(MAIN)
    arena.reset(CASTW)

    NKmax = max(NKs)
    NQmax = max(NQs)
    knT = arena.take([NKmax], BF16)
    vh = arena.take([NKmax // 128, 128], BF16)
    krT = arena.take([NKmax], BF16)
    cnring = Ring([arena.take([4, 512], BF16) for _ in range(4)], "cnr")
    qnh = arena.take([NQmax], BF16)
    qrh = arena.take([NQmax], BF16)
    mP = Ring([arena.take([1024], BF16) for _ in range(4)], "mP")
    ostg = Ring([arena.take([512], BF16) for _ in range(2)], "ostg")
    rec = arena.take([512], F32)
    accs2 = [[arena.take([1024], F32) for _ in range(2)] for _ in range(2)]
    wkvb = arena.take([4, 2048], BF16)
    wkst_view = accs2[0][0].rearrange("p (a b) -> p a b", a=2)
    wkst_view2 = accs2[0][1].rearrange("p (a b) -> p a b", a=2)
    for j in range(4):
        for hf, wv in enumerate((wkst_view, wkst_view2)):
            S.dma("sp", wv, wkvb_d[hf * 256:(hf + 1) * 256, j * 512:(j + 1) * 512].rearrange("(k p) n -> p k n", p=128),
                  writes=[f"acc0_{hf}"])
            evac_copy(wkvb[:, 2 * hf:2 * hf + 2, j * 512:(j + 1) * 512], wv, [f"acc0_{hf}"], ["wkvb"])
    mscale = float(192 ** -0.5)
    for g in range(2):
        NK, NQ = NKs[g], NQs[g]
        S.dma("sp", krT[:, 0:NK], kr_s[g][:, :], reads=[f"kr{g}"], writes=["krT"])
        for h in range(8):
            for kt in range(NK // TT):
                cnv, cnn = cnring.next()
                S.dma("sp", cnv, cn_s[g][kt].rearrange("p (j t) -> p j t", j=4), reads=[f"cn{g}"], writes=[cnn])
                pb, pbn = psb(4 + kt % 2)
                for j in range(4):
                    S.mm(pb, wkvb[:, j, h * 128:(h + 1) * 128], cnv[:, j, :], j == 0, j == 3, reads=["wkvb", cnn], writes=[pbn])
                S.add("act", lambda e, pb=pb, kt=kt: e.activation(knT[:, kt * TT:(kt + 1) * TT], pb, AF.Copy), reads=[pbn], writes=["knT"])
                pv_, pvn = psb(2 * (kt % 2))
                for blk in range(4):
                    for j in range(4):
                        S.mm(pv_[:, blk * 128:(blk + 1) * 128], cnv[:, j, blk * 128:(blk + 1) * 128],
                             wkvb[:, j, 1024 + h * 128:1024 + (h + 1) * 128], j == 0, j == 3, reads=["wkvb", cnn], writes=[pvn])
                S.add("dve", lambda e, pv_=pv_, kt=kt: e.tensor_copy(vh[:, kt * 4:kt * 4 + 4, :], pv_.rearrange("p (a b) -> p a b", a=4)),
                      reads=[pvn], writes=["vh"])
            S.dma("sp", qnh[:, 0:NQ], qn_s[g][h], reads=[f"qn{g}"], writes=["qnh"])
            ro = 64 * (h % 2)
            S.add("dve", lambda e, ro=ro: e.memset(qrh[64 - ro:128 - ro, :], 0.0), writes=["qrh"])
            S.dma("sp", qrh[ro:ro + 64, 0:NQ], qr_s[g][h // 2][ro:ro + 64, :], reads=[f"qr{g}"], writes=["qrh"])
            NKB = NK // 128
            NKP = NKB // 2
            pending = [None]
            NQT = NQ // TT
            items = [(qt, kp) for qt in range(NQT) for kp in range(NKP)]

            def s_pair(gi):
                qt, kp = items[gi]
                pbp = pp[gi % 3][:, :]
                pbn2 = [f"ps{2 * (gi % 3)}", f"ps{2 * (gi % 3) + 1}"]
                for hf in range(2):
                    kb = 2 * kp + hf
                    o_ = pbp[:, hf * 512:(hf + 1) * 512]
                    S.mm(o_, knT[:, kb * 128:(kb + 1) * 128], qnh[:, qt * TT:(qt + 1) * TT], True, False,
                         reads=["knT", "qnh"], writes=pbn2)
                    S.mm(o_, krT[:, kb * 128:(kb + 1) * 128], qrh[:, qt * TT:(qt + 1) * TT], False, True,
                         reads=["krT", "qrh"], writes=pbn2)
                pt, ptn = mP.next()
                S.add("act", lambda e, pt=pt, pbp=pbp: e.activation(pt, pbp, AF.Exp, scale=mscale), reads=pbn2, writes=[ptn])
                return pt, ptn

            def pv_pair(gi, pt, ptn):
                qt, kp = items[gi]
                po, pon = psb(6 + qt % 2)
                accs = accs2[qt % 2]
                an = [f"acc{qt % 2}_{a}" for a in range(2)]
                for hf in range(2):
                    kb = 2 * kp + hf
                    S.mm(po, vh[:, kb, :], pt[:, hf * 512:(hf + 1) * 512], kb == 0, kb == NKB - 1, reads=["vh", ptn], writes=[pon])
                a = kp % 2
                if kp < 2:
                    S.add("dve", lambda e, a=a: e.tensor_copy(accs[a], pt), reads=[ptn], writes=[an[a]])
                else:
                    S.add("dve", lambda e, a=a: e.tensor_tensor(accs[a], accs[a], pt, ALU.add), reads=[ptn, an[a]], writes=[an[a]])

            def make_epilogue(qt):
                po, pon = psb(6 + qt % 2)
                pss, pssn = psb(5)
                accs = accs2[qt % 2]
                an = [f"acc{qt % 2}_{a}" for a in range(2)]

                def epilogue_a():
                    S.add("dve", lambda e: e.tensor_tensor(accs[0], accs[0], accs[1], ALU.add), reads=[an[0], an[1]], writes=[an[0]])
                    S.add("dve", lambda e: e.tensor_tensor(accs[0][:, 0:512], accs[0][:, 0:512], accs[0][:, 512:1024], ALU.add),
                          reads=[an[0]], writes=[an[0]])

                def epilogue():
                    S.mm(pss, onesf[:, :], accs[0][:, 0:512], True, True, reads=["onesf", an[0]], writes=[pssn])
                    S.add("dve", lambda e: e.tensor_copy(rec, pss), reads=[pssn], writes=["rec"])
                    S.add("dve", lambda e: e.reciprocal(rec, rec), reads=["rec"], writes=["rec"])
                    sg, sgn = ostg.next()
                    S.add("dve", lambda e: e.tensor_tensor(sg, po, rec, ALU.mult), reads=[pon, "rec"], writes=[sgn])
                    S.dma(SQ, at_s[g][8 + h][:, qt * TT:(qt + 1) * TT], sg, reads=[sgn], writes=[f"at{g}"])
                return epilogue_a, epilogue

            pend = {}
            for gi in range(min(3, len(items))):
                pend[gi] = s_pair(gi)
            for gi, (qt, kp) in enumerate(items):
                pt, ptn = pend.pop(gi)
                pv_pair(gi, pt, ptn)
                if gi + 3 < len(items):
                    pend[gi + 3] = s_pair(gi + 3)
                if kp == 1 and pending[0] is not None and pending[0][0] is not None:
                    pending[0][0]()
                    pending[0] = (None, pending[0][1])
                if kp == min(4, NKP - 1) and pending[0] is not None:
                    if pending[0][0] is not None:
                        pending[0][0]()
                    pending[0][1]()
                    pending[0] = None
                if kp == NKP - 1:
                    if pending[0] is not None:
                        if pending[0][0] is not None:
                            pending[0][0]()
                        pending[0][1]()
                    pending[0] = make_epilogue(qt)
            pending[0][0]()
            pending[0][1]()

    S.barrier(("pe", "act", "dve", "sp", "pool"))
    arena.reset(0)

    xt = arena.take([4, 2048], F32)
    xT = arena.take([16, 512], F32)
    atT = arena.take([16, 512], BF16)
    h2T = arena.take([16, 512], BF16)
    actT = arena.take([32, 512], BF16)
    wring = Ring([arena.take([16, 128], BF16) for _ in range(3)], "wr")
    w2ring = Ring([arena.take([32, 128], BF16) for _ in range(2)], "w2r")
    yst = Ring([arena.take([2048], F32) for _ in range(2)], "yst")
    sq3 = Ring([arena.take([512], BF16) for _ in range(4)], "sq3")
    rs3 = arena.take([512], F32)
    tmp3 = Ring([arena.take([512], F32) for _ in range(2)], "tmp3")
    p3ring = Ring(ps[0:7], "ps")
    cur_ring[0] = p3ring
    pstat, pstatn = psb(7)

    class Stats:
        def __init__(self):
            self.pend = []

        def chunk(self, c):
            sq, sqn = sq3.next()
            S.add("act", lambda e, sq=sq, c=c: e.activation(sq, xT[:, c, :], AF.Square), reads=[f"xT{c}"], writes=[sqn])
            self.pend.append((c, sq, sqn))
            if len(self.pend) > 2:
                self._mm()

        def _mm(self):
            c, sq, sqn = self.pend.pop(0)
            S.mm(pstat, onesb[:, :], sq, c == 0, c == 15, reads=["onesb", sqn], writes=[pstatn])

        def fin(self):
            while self.pend:
                self._mm()
            S.add("act", lambda e: e.activation(rs3, pstat, AF.Sqrt, bias=epsb[:, 0:1], scale=1.0 / D), reads=[pstatn, "epsb"], writes=["rs3"])
            S.add("dve", lambda e: e.reciprocal(rs3, rs3), reads=["rs3"], writes=["rs3"])

    p3tiles = [(g, ti) for g in range(2) for ti in range(NQs[g] // TT)]

    class Streamer:
        def __init__(self, views, name, seq):
            self.views, self.name, self.seq = views, name, seq
            self.depth = len(views)
            self.i = 0
            for k in range(min(self.depth, len(seq))):
                self._load(k)

        def _load(self, k):
            src, res, kk = self.seq[k]
            v = self.views[k % self.depth]
            S.dma("sp", v, src.rearrange("p (k n) -> p k n", k=kk), reads=[res], writes=[f"{self.name}{k % self.depth}"])

        def get(self):
            k = self.i
            return self.views[k % self.depth], f"{self.name}{k % self.depth}"

        def release(self):
            k = self.i
            self.i += 1
            if k + self.depth < len(self.seq):
                self._load(k + self.depth)

    seq_small, seq_big = [], []
    for _ in p3tiles:
        for f in range(16):
            seq_small.append((WOs[f], f"WOs{f}", 16))
        for half in range(2):
            for j in range(32):
                seq_small.append((WF1s[half * 32 + j], f"WF1s{half * 32 + j}", 16))
            for f in range(16):
                seq_big.append((WF2s[f][:, half * 4096:(half + 1) * 4096], f"WF2s{f}", 32))
    wst = Streamer(wring.v, "wr", seq_small)
    w2st = Streamer(w2ring.v, "w2r", seq_big)

    def load_inputs(g, ti):
        t0 = ti * TT
        for blk in range(4):
            S.dma("sp", xt[:, blk, :], xg[g][t0 + blk * 128:t0 + (blk + 1) * 128, :], writes=["xt"])
        S.dma("sp", atT, at_s[g][:, :, t0:t0 + TT].rearrange("c p t -> p c t"), reads=[f"at{g}"], writes=["atT"])

    load_inputs(*p3tiles[0])
    for it, (g, ti) in enumerate(p3tiles):
        t0 = ti * TT
        for c in range(16):
            pb, pbn = p3ring.next()
            for blk in range(4):
                S.tr(pb[:, blk * 128:(blk + 1) * 128], xt[:, blk, c * 128:(c + 1) * 128], identf[:, :],
                     reads=["xt", "identf"], writes=[pbn])
            evac_copy(xT[:, c, :], pb, [pbn], [f"xT{c}"])
        st = Stats()
        for f in range(16):
            wv, wn = wst.get()
            pb, pbn = proj_fm(lambda k: wv[:, k, :], wn, 16, lambda k: atT[:, k, :], ["atT"])
            wst.release()
            S.add("dve", lambda e, f=f, pb=pb, g=g: e.scalar_tensor_tensor(
                xT[:, f, :], pb, G1[:, 2 * f + g:2 * f + g + 1], xT[:, f, :], ALU.mult, ALU.add),
                reads=[pbn, "modT", f"xT{f}"], writes=[f"xT{f}"])
            st.chunk(f)
        st.fin()
        for c in range(16):
            tp, tpn = tmp3.next()
            S.add("dve", lambda e, c=c, tp=tp, g=g: e.scalar_tensor_tensor(
                tp, xT[:, c, :], A2[:, 2 * c + g:2 * c + g + 1], rs3, ALU.mult, ALU.mult),
                reads=[f"xT{c}", "A2", "rs3"], writes=[tpn])
            S.add("act", lambda e, c=c, tp=tp, g=g: e.activation(h2T[:, c, :], tp, AF.Identity, bias=B2[:, 2 * c + g:2 * c + g + 1], scale=1.0),
                  reads=[tpn, "modT"], writes=["h2T"])
        st = Stats()
        for half in range(2):
            for j in range(32):
                f1 = half * 32 + j
                wv, wn = wst.get()
                pb, pbn = proj_fm(lambda k: wv[:, k, :], wn, 16, lambda k: h2T[:, k, :], ["h2T"])
                wst.release()
                tp, tpn = tmp3.next()
                S.add("act", lambda e, tp=tp, pb=pb: e.activation(tp, pb, AF.Relu), reads=[pbn], writes=[tpn])
                S.add("pool", lambda e, tp=tp, j=j: e.tensor_tensor(actT[:, j, :], tp, tp, ALU.mult), reads=[tpn], writes=["actT"])
            if half == 1 and it + 1 < len(p3tiles):
                load_inputs(*p3tiles[it + 1])
            for f in range(16):
                wv, wn = w2st.get()
                pb, pbn = proj_fm(lambda k: wv[:, k, :], wn, 32, lambda k: actT[:, k, :], ["actT"])
                w2st.release()
                S.add("dve", lambda e, f=f, pb=pb, g=g: e.scalar_tensor_tensor(
                    xT[:, f, :], pb, G2[:, 2 * f + g:2 * f + g + 1], xT[:, f, :], ALU.mult, ALU.add),
                    reads=[pbn, "modT", f"xT{f}"], writes=[f"xT{f}"])
                if half == 1:
                    st.chunk(f)
        st.fin()
        for c in range(16):
            S.add("dve", lambda e, c=c: e.scalar_tensor_tensor(
                xT[:, c, :], xT[:, c, :], gfin[:, c:c + 1], rs3, ALU.mult, ALU.mult),
                reads=[f"xT{c}", "gfin", "rs3"], writes=[f"xT{c}"])
        for blk in range(4):
            ys, ysn = yst.next()
            for c4 in range(4):
                pb, pbn = p3ring.next()
                for cc in range(4):
                    c = c4 * 4 + cc
                    S.tr(pb[:, cc * 128:(cc + 1) * 128], xT[:, c, blk * 128:(blk + 1) * 128], identf[:, :],
                         reads=[f"xT{c}", "identf"], writes=[pbn])
                evac_copy(ys[:, c4 * 512:(c4 + 1) * 512], pb, [pbn], [ysn])
            S.dma(SQ, y_d[g][t0 + blk * 128:t0 + (blk + 1) * 128, :], ys, reads=[ysn], writes=["y"])

    stats_ = S.finalize_and_emit()
    return nc, stats_


def _t5_bucket(rel):
    half = 16
    max_exact = 8
    ret = np.where(rel > 0, half, 0)
    n = np.abs(rel)
    nf = np.maximum(n, 1).astype(np.float32)
    large = max_exact + (np.log(nf / max_exact) / math.log(128 / max_exact) * (half - max_exact)).astype(np.int32)
    large = np.minimum(large, half - 1)
    return ret + np.where(n < max_exact, n, large)


def _oht():
    k = np.arange(128)[:, None]
    q = np.arange(128)[None, :]
    out = np.zeros((128, 99, 128), np.float32)
    for oi, o in enumerate((-1, 0, 1)):
        rel = o * 128 + k - q
        bk = _t5_bucket(rel)
        valid = np.abs(rel) <= 128
        for b in range(32):
            out[:, oi * 33 + b, :] = ((bk == b) & valid)
        out[:, oi * 33 + 32, :] = ~valid
    return out


def _rope_tables(pos):
    half = 32
    inv = (np.float32(10000.0) ** (-np.arange(half, dtype=np.float32) / np.float32(half))).astype(np.float32)
    ang = pos.astype(np.float32)[:, None] * inv[None, :]
    cos = np.cos(ang).astype(np.float32).T
    sin = np.sin(ang).astype(np.float32).T
    cosT = np.concatenate([cos, cos, cos, cos], 0)
    sinT = np.concatenate([-sin, sin, -sin, sin], 0)
    return np.ascontiguousarray(cosT), np.ascontiguousarray(sinT)


def _fm(v, reps=1):
    a = np.asarray(v, np.float32).reshape(-1, 128).T
    return np.ascontiguousarray(np.repeat(a, reps, axis=1))


_CACHE = {}


def run(inputs, NCORES, SP, SS):
    f = lambda k: np.asarray(inputs[k], np.float32)
    SA = NCORES * SS
    w_in = f("w_in")[0]
    QA, KA, VA, QN, QR, CKV, KR = 0, 1024, 1280, 1536, 2560, 3072, 3584
    sw = np.arange(512).reshape(8, 2, 32)[:, ::-1, :].reshape(-1)
    swk = np.arange(64).reshape(2, 32)[::-1].reshape(-1)
    cols1 = np.concatenate([np.arange(QA, QA + 1024), np.arange(QN, QN + 1024), np.arange(QR, QR + 512), QR + sw,
                            np.arange(KA, KA + 256), np.arange(VA, VA + 256)])
    krc = np.arange(KR, KR + 64)
    cols2 = np.concatenate([np.arange(CKV, CKV + 512), krc, krc, KR + swk, KR + swk])
    w1 = np.ascontiguousarray(w_in[:, cols1])
    w2 = np.ascontiguousarray(w_in[:, cols2])
    wk = f("w_kv_b")[0].reshape(512, 8, 256)
    wkvb = np.ascontiguousarray(np.concatenate([wk[:, :, :128].reshape(512, 1024), wk[:, :, 128:].reshape(512, 1024)], 1))
    common = {
        "w_ada": f("w_ada")[0], "bT2": _fm(f("b_ada")[0], 2), "gmix2": _fm(f("g_mix")[0], 2), "gmlp2": _fm(f("g_mlp")[0], 2),
        "gfinT": _fm(f("g_final")), "gkvT": _fm(f("g_kv")[0]), "w1": w1, "w2": w2, "wkvb": wkvb,
        "w_o": f("w_o")[0], "w_ff1": f("w_ff1")[0], "w_ff2": f("w_ff2")[0],
        "sinkb": np.ascontiguousarray(np.broadcast_to(f("sink")[0][None, :], (128, 8))),
        "relb": np.ascontiguousarray(np.broadcast_to(f("rel_bias").reshape(1, 256), (128, 256))),
        "oht": _oht(), "ident": np.eye(128, dtype=np.float32),
    }
    cosP, sinP = _rope_tables(np.arange(SP))
    common["cosP"], common["sinP"] = cosP, sinP
    xpr, xsm = f("x_prompt"), f("x_sample")[0]
    cp, cs = f("c_prompt"), f("c_sample")[0]
    in_maps = []
    for i in range(NCORES):
        m = dict(common)
        m["xp"] = np.ascontiguousarray(xpr[i])
        m["xs"] = np.ascontiguousarray(np.roll(xsm, -i * SS, axis=0))
        pos = (np.arange(SA) + i * SS) % SA
        m["cosS"], m["sinS"] = _rope_tables(pos)
        cc = np.stack([cp[i], cs], 1)
        m["cT"] = np.ascontiguousarray(cc.reshape(16, 128, 2).transpose(1, 0, 2).reshape(128, 32))
        hmv = np.zeros((128, 2), np.float32)
        if i == 0:
            hmv[:, 0] = NEGM
        if i == NCORES - 1:
            hmv[:, 1] = NEGM
        m["hm"] = hmv
        in_maps.append(m)
    key = (NCORES, SP, SS)
    if key not in _CACHE:
        _CACHE[key] = build(NCORES, SP, SS)
    nc, st = _CACHE[key]
    res = run_bass_kernel_spmd(nc, in_maps, core_ids=list(range(NCORES)))
    yp = np.stack([np.asarray(r["yp"], np.float32) for r in res.results], 0)
    ys = np.concatenate([np.asarray(r["ys"], np.float32) for r in res.results], 0)[None]
    return yp, ys


def kernel(**inputs):
    return run(inputs, 8, 2048, 2048)
```

```python
import contextlib
import math
import numpy as np
import concourse.bass as bass
import concourse.mybir as mybir
from concourse.bass_utils import run_bass_kernel_spmd

F32 = mybir.dt.float32
BF16 = mybir.dt.bfloat16
AF = mybir.ActivationFunctionType
ALU = mybir.AluOpType

D = 2048
TT = 512
DFF = 8192
EPS = 1e-6
NEGM = -30000.0
EPOCH = 4096
DMA_R = 8


class Op:
    __slots__ = ("eng", "fn", "deps", "signal", "idx", "sig", "is_dma", "tok", "waits")

    def __init__(self, eng, fn, is_dma=False):
        self.eng = eng
        self.fn = fn
        self.deps = []
        self.signal = False
        self.idx = -1
        self.sig = -1
        self.is_dma = is_dma
        self.tok = None
        self.waits = []


class Res:
    __slots__ = ("lw", "rd")

    def __init__(self):
        self.lw = None
        self.rd = []


class Sched:
    ENGS = ("pe", "act", "dve", "pool", "sp")

    def __init__(self, nc):
        self.nc = nc
        self.ops = {e: [] for e in self.ENGS}
        self.res = {}
        self.dma_n = {e: 0 for e in self.ENGS}
        self.dma_ops = {e: [] for e in self.ENGS}
        self.bar = {e: [] for e in self.ENGS}

    def _r(self, name):
        r = self.res.get(name)
        if r is None:
            r = self.res[name] = Res()
        return r

    def barrier(self, engs):
        lst = []
        for e in engs:
            ops = self.ops[e]
            for o in reversed(ops):
                if not o.is_dma:
                    lst.append(o)
                    break
            lst.extend(self.dma_ops[e][-DMA_R:])
        for e in engs:
            self.bar[e] = list(lst)

    def add(self, eng, fn, reads=(), writes=(), is_dma=False):
        op = Op(eng, fn, is_dma)
        deps = []
        for n in reads:
            r = self._r(n)
            if r.lw is not None:
                deps.append((r.lw, "raw"))
        for n in writes:
            r = self._r(n)
            if r.lw is not None:
                deps.append((r.lw, "waw"))
            for o in r.rd:
                deps.append((o, "war"))
        for d, kind in deps:
            if d is op:
                continue
            if not d.is_dma and d.eng == eng and not is_dma:
                if kind != "raw" or eng == "pe":
                    continue
            op.deps.append(d)
        if self.bar[eng]:
            for d in self.bar[eng]:
                if d.is_dma or d.eng != eng:
                    op.deps.append(d)
            self.bar[eng] = []
        for n in reads:
            self._r(n).rd.append(op)
        for n in writes:
            r = self._r(n)
            r.lw = op
            r.rd = []
        op.idx = len(self.ops[eng])
        self.ops[eng].append(op)
        if is_dma:
            n = self.dma_n[eng]
            self.dma_n[eng] = n + 1
            op.tok = (eng, n % DMA_R, 16 * (n // DMA_R + 1))
            if n >= DMA_R:
                op.deps.append(self.dma_ops[eng][n - DMA_R])
            self.dma_ops[eng].append(op)
        return op

    def dma(self, q, out, in_, reads=(), writes=()):
        return self.add(q, lambda e: e.dma_start(out=out, in_=in_), reads, writes, is_dma=True)

    def mm(self, out, lhsT, rhs, start, stop, reads=(), writes=()):
        return self.add("pe", lambda e: e.matmul(out, lhsT, rhs, start=start, stop=stop), reads, writes)

    def tr(self, out, in_, ident, reads=(), writes=()):
        return self.add("pe", lambda e: e.transpose(out, in_, ident), reads, writes)

    def finalize_and_emit(self):
        nc = self.nc
        for eng in self.ENGS:
            seen = {}
            seen_dma = {}
            for op in self.ops[eng]:
                for d in op.deps:
                    if d.is_dma:
                        k = (d.tok[0], d.tok[1])
                        if seen_dma.get(k, 0) < d.tok[2]:
                            seen_dma[k] = d.tok[2]
                            op.waits.append(d)
                    else:
                        if seen.get(d.eng, -1) < d.idx:
                            seen[d.eng] = d.idx
                            d.signal = True
                            op.waits.append(d)
        final_waits = []
        for q in self.ENGS:
            final_waits.extend(self.dma_ops[q][-DMA_R:])
        nsig = {}
        for eng in self.ENGS:
            c = 0
            for op in self.ops[eng]:
                if op.signal and not op.is_dma:
                    op.sig = c
                    c += 1
            nsig[eng] = c
        with contextlib.ExitStack() as st:
            csem = {}
            for eng in self.ENGS:
                n_ep = (nsig[eng] + EPOCH - 1) // EPOCH
                csem[eng] = [st.enter_context(nc.semaphore(f"c_{eng}_{i}")) for i in range(max(n_ep, 1))]
            dsem = {}
            for q in self.ENGS:
                if self.dma_n[q]:
                    dsem[q] = [st.enter_context(nc.semaphore(f"d_{q}_{i}")) for i in range(DMA_R)]

            def emit(eng, e):
                for op in self.ops[eng]:
                    for d in op.waits:
                        if d.is_dma:
                            e.wait_ge(dsem[d.tok[0]][d.tok[1]], d.tok[2])
                        else:
                            e.wait_ge(csem[d.eng][d.sig // EPOCH], d.sig % EPOCH + 1)
                    ins = op.fn(e)
                    if op.is_dma:
                        ins.then_inc(dsem[op.tok[0]][op.tok[1]], 16)
                    elif op.signal:
                        ins.then_inc(csem[eng][op.sig // EPOCH], 1)
                if eng == "sp":
                    for d in final_waits:
                        e.wait_ge(dsem[d.tok[0]][d.tok[1]], d.tok[2])

            with nc.Block() as block:
                @block.tensor
                def _(e):
                    emit("pe", e)

                @block.scalar
                def _(e):
                    emit("act", e)

                @block.vector
                def _(e):
                    emit("dve", e)

                @block.gpsimd
                def _(e):
                    emit("pool", e)

                @block.sync
                def _(e):
                    emit("sp", e)
        return {e: len(self.ops[e]) for e in self.ENGS}, nsig


class Arena:
    def __init__(self, nc, name, kib):
        self.words = kib * 256
        self.t = nc.alloc_sbuf_tensor(name, [128, self.words], F32)
        self.off = 0
        self.phase = 0

    def reset(self, to=0):
        self.off = to
        self.phase += 1

    def take(self, shape, dtype):
        n = 1
        for s in shape:
            n *= s
        words = n if dtype == F32 else (n + 1) // 2
        words = (words + 7) // 8 * 8
        assert self.off + words <= self.words, ("arena overflow", self.off, words, self.words)
        v = self.t[:, self.off:self.off + words]
        self.off += words
        if dtype != F32:
            v = v.bitcast(dtype)
        v = v[:, 0:n]
        if len(shape) == 2:
            v = v.rearrange("p (a b) -> p a b", a=shape[0])
        elif len(shape) == 3:
            v = v.rearrange("p (a b c) -> p a b c", a=shape[0], b=shape[1])
        return v


class Ring:
    def __init__(self, views, name):
        self.v = views
        self.name = name
        self.i = 0

    def next(self):
        k = self.i % len(self.v)
        self.i += 1
        return self.v[k], f"{self.name}{k}"


def build(NCORES, SP, SS):
    SA = NCORES * SS
    NPT = SP // TT
    NSO = SS // TT
    NSA = SA // TT
    NBP = SP // 128
    NBS = SS // 128 + 2
    NKs = (SP, SA)
    NQs = (SP, SS)
    NBW = (NBP, NBS)

    nc = bass.Bass("TRN2", target_bir_lowering=False)

    def din(name, shape, dt=F32):
        return nc.dram_tensor(name, list(shape), dt, kind="ExternalInput").ap()

    def dscr(name, shape, dt=BF16):
        return nc.dram_tensor(name, list(shape), dt).ap()

    xg = (din("xp", [SP, D]), din("xs", [SA, D]))
    cT_d = din("cT", [128, 32])
    wada_d = din("w_ada", [D, 6 * D])
    bT2_d = din("bT2", [128, 192])
    gmix2_d = din("gmix2", [128, 32])
    gmlp2_d = din("gmlp2", [128, 32])
    gfin_d = din("gfinT", [128, 16])
    gkv_d = din("gkvT", [128, 4])
    w1_d = din("w1", [D, 26 * 128 + 256])
    w2_d = din("w2", [D, 768])
    wkvb_d = din("wkvb", [512, 2048])
    wo_d = din("w_o", [D, D])
    wf1_d = din("w_ff1", [D, DFF])
    wf2_d = din("w_ff2", [DFF, D])
    sinkb_d = din("sinkb", [128, 8])
    relb_d = din("relb", [128, 256])
    oht_d = din("oht", [128, 99, 128])
    hm_d = din("hm", [128, 2])
    ident_d = din("ident", [128, 128])
    cos_d = (din("cosP", [128, SP]), din("cosS", [128, SA]))
    sin_d = (din("sinP", [128, SP]), din("sinS", [128, SA]))
    y_d = (nc.dram_tensor("yp", [SP, D], F32, kind="ExternalOutput").ap(),
           nc.dram_tensor("ys", [SS, D], F32, kind="ExternalOutput").ap())

    W1s = dscr("W1s", [26, 128, 2048])
    WOs = dscr("WOs", [16, 128, 2048])
    WF1s = dscr("WF1s", [64, 128, 2048])
    WF2s = dscr("WF2s", [16, 128, 8192])
    cn_s = [dscr(f"cn{g}", [NKs[g] // TT, 128, 4 * TT]) for g in range(2)]
    kr_s = [dscr(f"kr{g}", [128, NKs[g]]) for g in range(2)]
    qa_s = [dscr(f"qa{g}", [8, 128, NQs[g]]) for g in range(2)]
    qn_s = [dscr(f"qn{g}", [8, 128, NQs[g]]) for g in range(2)]
    qr_s = [dscr(f"qr{g}", [4, 128, NQs[g]]) for g in range(2)]
    ka_s = [dscr(f"ka{g}", [2, 128, NBW[g] * 128]) for g in range(2)]
    va_s = [dscr(f"va{g}", [NBW[g], 128, 256]) for g in range(2)]
    at_s = [dscr(f"at{g}", [16, 128, NQs[g]]) for g in range(2)]

    S = Sched(nc)
    SQ = "act"
    A = lambda name, shape, dt: nc.alloc_sbuf_tensor("sb_" + name, shape, dt)
    identf = A("identf", [128, 128], F32)
    identb = A("identb", [128, 128], BF16)
    onesb = A("onesb", [128, 128], BF16)
    epsb = A("epsb", [128, 1], F32)
    zerob = A("zerob", [128, 1], F32)
    modT = A("modT", [128, 192], F32)
    A1 = A("A1", [128, 32], F32)
    A2 = A("A2", [128, 32], F32)
    gmix2 = A("gmix2", [128, 32], F32)
    gmlp2 = A("gmlp2", [128, 32], F32)
    gfin = A("gfin", [128, 16], F32)
    gkv = A("gkv", [128, 4], F32)
    hm = A("hm", [128, 2], F32)
    Ttab = A("Ttab", [128, 3, 8, 128], F32)
    ES = A("ES", [128, 8, 128], F32)
    esk = A("esk", [128, 8], F32)
    relb = A("relb", [128, 264], F32)
    onesf = A("onesf", [128, 128], F32)
    B1 = modT[:, 0:32]
    G1 = modT[:, 64:96]
    B2 = modT[:, 96:128]
    G2 = modT[:, 160:192]

    pp = [nc.alloc_psum_tensor(f"pp{i}", [128, 1024], F32) for i in range(4)]
    ps = [pp[i // 2][:, (i % 2) * 512:(i % 2 + 1) * 512] for i in range(8)]
    psring = Ring(ps, "ps")
    pring6 = Ring(ps[0:6], "ps")

    class _TR:
        i = 0

        def next(self):
            k = 6 + self.i % 2
            self.i += 1
            return ps[k], f"ps{k}"
    tring = _TR()
    cur_ring = [pring6]

    def psb(i):
        return ps[i], f"ps{i}"

    arena = Arena(nc, "arena", 184)
    CASTW = 0

    S.dma("sp", identf[:, :], ident_d[:, :], writes=["identf"])
    S.add("dve", lambda e: e.tensor_copy(identb[:, :], identf[:, :]), reads=["identf"], writes=["identb"])
    S.add("dve", lambda e: e.memset(onesb[:, :], 1.0), writes=["onesb"])
    S.add("dve", lambda e: e.memset(onesf[:, :], 1.0), writes=["onesf"])
    S.add("dve", lambda e: e.memset(epsb[:, :], EPS), writes=["epsb"])
    S.add("dve", lambda e: e.memset(zerob[:, :], 0.0), writes=["zerob"])
    for t, d_, n in ((gmix2, gmix2_d, "gmix2"), (gmlp2, gmlp2_d, "gmlp2"), (gfin, gfin_d, "gfin"),
                     (gkv, gkv_d, "gkv"), (hm, hm_d, "hm"), (esk, sinkb_d, "esk")):
        S.dma("sp", t[:, :], d_[:, :], writes=[n])
    S.dma("sp", relb[:, 0:256], relb_d[:, :], writes=["relb"])
    S.add("dve", lambda e: e.memset(relb[:, 256:264], NEGM), writes=["relb"])


    def cast_slab(src_ap, dst_sb=None, dst_dram=None, dst_res=None):
        a = src_ap.shape[1]
        if dst_sb is not None:
            S.dma("pool", dst_sb, src_ap, writes=[dst_res])
        else:
            S.dma("pool", dst_dram.rearrange("p (a b) -> p a b", a=a), src_ap, writes=[dst_res])

    w2sb = arena.take([16, 768], BF16)
    vaw = arena.take([16, 256], BF16)
    for j in range(6):
        cast_slab(w2_d[:, j * 128:(j + 1) * 128].rearrange("(k p) n -> p k n", p=128),
                  dst_sb=w2sb[:, :, j * 128:(j + 1) * 128], dst_res="w2sb")
    for j in range(26):
        cast_slab(w1_d[:, j * 128:(j + 1) * 128].rearrange("(k p) n -> p k n", p=128),
                  dst_dram=W1s[j], dst_res=f"W1s{j}")
    for j in range(2):
        cast_slab(w1_d[:, 26 * 128 + j * 128:26 * 128 + (j + 1) * 128].rearrange("(k p) n -> p k n", p=128),
                  dst_sb=vaw[:, :, j * 128:(j + 1) * 128], dst_res="vaw")
    for j in range(16):
        cast_slab(wo_d[:, j * 128:(j + 1) * 128].rearrange("(k p) n -> p k n", p=128),
                  dst_dram=WOs[j], dst_res=f"WOs{j}")
    for j in range(64):
        cast_slab(wf1_d[:, j * 128:(j + 1) * 128].rearrange("(k p) n -> p k n", p=128),
                  dst_dram=WF1s[j], dst_res=f"WF1s{j}")
    for j in range(16):
        for q in range(4):
            cast_slab(wf2_d[q * 2048:(q + 1) * 2048, j * 128:(j + 1) * 128].rearrange("(k p) n -> p k n", p=128),
                      dst_dram=WF2s[j][:, q * 2048:(q + 1) * 2048], dst_res=f"WF2s{j}")

    P1_BASE = arena.off

    cTt = arena.take([32], F32)
    sT = arena.take([32], F32)
    bT2 = arena.take([192], F32)
    waring = Ring([arena.take([16, 128], F32) for _ in range(3)], "wa")
    ohst = arena.take([33, 128], F32)
    S.dma("sp", cTt, cT_d[:, :], writes=["cTt"])
    S.dma("sp", bT2, bT2_d[:, :], writes=["bT2"])
    S.add("act", lambda e: e.activation(sT, cTt, AF.Silu), reads=["cTt"], writes=["sT"])
    S.add("act", lambda e: e.activation(esk[:, :], esk[:, :], AF.Exp), reads=["esk"], writes=["esk"])
    S.add("dve", lambda e: e.tensor_copy(ES[:, :, :], esk[:, :].unsqueeze(2).to_broadcast([128, 8, 128])),
          reads=["esk"], writes=["ES"])
    pm, pmn = psb(7)
    for ch in range(96):
        wa, wan = waring.next()
        S.dma("sp", wa, wada_d[:, ch * 128:(ch + 1) * 128].rearrange("(k p) n -> p k n", p=128), writes=[wan])
        for k in range(16):
            S.mm(pm[:, 2 * ch:2 * ch + 2], wa[:, k, :], sT[:, 2 * k:2 * k + 2], k == 0, k == 15,
                 reads=[wan, "sT"], writes=[pmn])
    S.add("dve", lambda e: e.tensor_tensor(modT[:, :], pm[:, 0:192], bT2, ALU.add), reads=[pmn, "bT2"], writes=["modT"])
    S.add("dve", lambda e: e.scalar_tensor_tensor(A1[:, :], modT[:, 32:64], 1.0, gmix2[:, :], ALU.add, ALU.mult),
          reads=["modT", "gmix2"], writes=["A1"])
    S.add("dve", lambda e: e.scalar_tensor_tensor(A2[:, :], modT[:, 128:160], 1.0, gmlp2[:, :], ALU.add, ALU.mult),
          reads=["modT", "gmlp2"], writes=["A2"])
    for o in range(3):
        S.dma("sp", ohst, oht_d[:, o * 33:(o + 1) * 33, :], writes=["ohst"])
        for h in range(8):
            S.add("dve", lambda e, o=o, h=h: e.tensor_scalar_mul(Ttab[:, o, h, :], ohst[:, 0, :], relb[:, h:h + 1]),
                  reads=["ohst", "relb"], writes=["Ttab"])
            for b in range(1, 33):
                S.add("dve", lambda e, o=o, h=h, b=b: e.scalar_tensor_tensor(
                    Ttab[:, o, h, :], ohst[:, b, :], relb[:, b * 8 + h:b * 8 + h + 1], Ttab[:, o, h, :], ALU.mult, ALU.add),
                    reads=["ohst", "relb", "Ttab"], writes=["Ttab"])

    MAIN = ("pe", "act", "dve", "sp")
    S.barrier(MAIN)
    arena.reset(P1_BASE)

    xring = Ring([arena.take([2048], F32) for _ in range(4)], "xb")
    junk = arena.take([2048], BF16)
    ssvs = [arena.take([8], F32) for _ in range(2)]
    xsbs = [arena.take([4, 2048], BF16) for _ in range(2)]
    ntile = [0]
    hTs = [arena.take([16, 512], BF16) for _ in range(2)]
    hTcur = [None, None]
    w1ring = Ring([arena.take([16, 128], BF16) for _ in range(3)], "w1r")
    cosring = Ring([arena.take([512], F32) for _ in range(2)], "cos")
    sinring = Ring([arena.take([512], F32) for _ in range(2)], "sin")
    ckvf = arena.take([4, 512], F32)
    sqb = arena.take([4, 512], BF16)
    rsb = arena.take([512], F32)
    cnb = arena.take([4, 512], BF16)
    t1 = arena.take([512], F32)
    t2 = arena.take([512], F32)
    stg = Ring([arena.take([512], BF16) for _ in range(4)], "stg")
    vstg = arena.take([4, 256], BF16)
    alt = [0]

    def evac_copy(dst, src, reads, writes):
        alt[0] += 1
        if alt[0] % 2:
            S.add("act", lambda e: e.activation(dst, src, AF.Copy), reads=reads, writes=writes)
        else:
            S.add("dve", lambda e: e.tensor_copy(dst, src), reads=reads, writes=writes)

    def norm_stats(x_ap):
        par = ntile[0] % 2
        ntile[0] += 1
        ssv, xsb = ssvs[par], xsbs[par]
        ssvn, xsbn = f"ssv{par}", f"xsb{par}"
        S.add("dve", lambda e: e.memset(ssv, 0.0), writes=[f"{ssvn}_{b}" for b in range(4)])
        for blk in range(4):
            xb, xbn = xring.next()
            S.dma("sp", xb, x_ap[blk * 128:(blk + 1) * 128, :], writes=[xbn])
            S.add("act", lambda e, xb=xb, blk=blk: e.activation(junk, xb, AF.Square, scale=float(D ** -0.5),
                                                                accum_out=ssv[:, blk:blk + 1]),
                  reads=[xbn], writes=["junk", f"{ssvn}_{blk}"])
            S.add("act", lambda e, blk=blk: e.activation(ssv[:, 4 + blk:5 + blk], ssv[:, blk:blk + 1], AF.Sqrt,
                                                         bias=epsb[:, 0:1], scale=1.0),
                  reads=[f"{ssvn}_{blk}", "epsb"], writes=[f"{ssvn}_{blk}"])
            S.add("dve", lambda e, blk=blk: e.reciprocal(ssv[:, 4 + blk:5 + blk], ssv[:, 4 + blk:5 + blk]),
                  reads=[f"{ssvn}_{blk}"], writes=[f"{ssvn}_{blk}"])
            S.add("dve", lambda e, xb=xb, blk=blk: e.tensor_scalar_mul(xsb[:, blk, :], xb, ssv[:, 4 + blk:5 + blk]),
                  reads=[xbn, f"{ssvn}_{blk}"], writes=[xsbn])
        return par

    def norm_transpose(par, g):
        xsb, xsbn = xsbs[par], f"xsb{par}"
        hT, hTn = hTs[par], f"hT{par}"
        for c2 in range(8):
            pb, pbn = tring.next()
            pv = pb.bitcast(BF16)
            for cc in range(2):
                c = c2 * 2 + cc
                for blk in range(4):
                    S.tr(pv[:, cc * 512 + blk * 128:cc * 512 + (blk + 1) * 128], xsb[:, blk, c * 128:(c + 1) * 128],
                         identb[:, :], reads=[xsbn, "identb"], writes=[pbn])
            for cc in range(2):
                c = c2 * 2 + cc
                src = pv[:, cc * 512:(cc + 1) * 512]
                if cc == 0:
                    S.add("dve", lambda e, c=c, src=src: e.tensor_scalar(
                        hT[:, c, :], src, A1[:, 2 * c + g:2 * c + g + 1], B1[:, 2 * c + g:2 * c + g + 1], ALU.mult, ALU.add),
                        reads=[pbn, "A1", "modT"], writes=[hTn])
                else:
                    S.add("act", lambda e, c=c, src=src: e.activation(
                        hT[:, c, :], src, AF.Identity, bias=B1[:, 2 * c + g:2 * c + g + 1], scale=A1[:, 2 * c + g:2 * c + g + 1]),
                        reads=[pbn, "A1", "modT"], writes=[hTn])

    def load_rope(g, t0):
        cs, csn = cosring.next()
        sn, snn = sinring.next()
        S.dma("sp", cs, cos_d[g][:, t0:t0 + TT], writes=[csn])
        S.dma("sp", sn, sin_d[g][:, t0:t0 + TT], writes=[snn])
        return cs, csn, sn, snn

    def rope_out(pa, pan, pbk, pbkn, rp, dst_dram, dres):
        cs, csn, sn, snn = rp
        S.add("dve", lambda e: e.tensor_tensor(t1, pa, cs, ALU.mult), reads=[pan, csn], writes=["t1"])
        S.add("dve", lambda e: e.tensor_tensor(t2, pbk, sn, ALU.mult), reads=[pbkn, snn], writes=["t2"])
        sg, sgn = stg.next()
        S.add("dve", lambda e: e.tensor_tensor(sg, t1, t2, ALU.add), reads=["t1", "t2"], writes=[sgn])
        S.dma(SQ, dst_dram, sg, reads=[sgn], writes=[dres])

    def proj_fm(wview, wres, nk, rhs_of, rhs_res):
        pb, pbn = cur_ring[0].next()
        for k in range(nk):
            S.mm(pb, wview(k), rhs_of(k), k == 0, k == nk - 1, reads=[wres] + rhs_res, writes=[pbn])
        return pb, pbn

    def latent_tile(g, t0, rp):
        banks = []
        for j in range(6):
            banks.append(proj_fm(lambda k, j=j: w2sb[:, k, j * 128:(j + 1) * 128], "w2sb", 16,
                                 lambda k: hTcur[0][:, k, :], [hTcur[1]]))
        for j in range(4):
            pb, pbn = banks[j]
            S.add("act", lambda e, j=j, pb=pb: e.activation(ckvf[:, j, :], pb, AF.Copy), reads=[pbn], writes=["ckvf"])
            S.add("act", lambda e, j=j, pb=pb: e.activation(sqb[:, j, :], pb, AF.Square), reads=[pbn], writes=["sqb"])
        rope_out(banks[4][0], banks[4][1], banks[5][0], banks[5][1], rp, kr_s[g][:, t0:t0 + TT], f"kr{g}")

        def fin():
            pn, pnn = cur_ring[0].next()
            for j in range(4):
                S.mm(pn, onesb[:, :], sqb[:, j, :], j == 0, j == 3, reads=["onesb", "sqb"], writes=[pnn])
            S.add("act", lambda e: e.activation(rsb, pn, AF.Sqrt, bias=epsb[:, 0:1], scale=1.0 / 512), reads=[pnn, "epsb"], writes=["rsb"])
            S.add("dve", lambda e: e.reciprocal(rsb, rsb), reads=["rsb"], writes=["rsb"])
            for j in range(4):
                S.add("dve", lambda e, j=j: e.scalar_tensor_tensor(cnb[:, j, :], ckvf[:, j, :], gkv[:, j:j + 1], rsb, ALU.mult, ALU.mult),
                      reads=["ckvf", "gkv", "rsb"], writes=["cnb"])
            S.dma(SQ, cn_s[g][t0 // TT].rearrange("p (j t) -> p j t", j=4), cnb, reads=["cnb"], writes=[f"cn{g}"])
        return fin

    def w1_chunk(j):
        wv, wn = w1ring.next()
        S.dma("sp", wv, W1s[j].rearrange("p (k n) -> p k n", k=16), reads=[f"W1s{j}"], writes=[wn])
        return proj_fm(lambda k: wv[:, k, :], wn, 16, lambda k: hTcur[0][:, k, :], [hTcur[1]])

    def store_fm(pb, pbn, dst, dres, w=TT):
        sg, sgn = stg.next()
        evac_copy(sg[:, 0:w], pb, [pbn], [sgn])
        S.dma(SQ, dst, sg[:, 0:w], reads=[sgn], writes=[dres])

    def qkv_tile(g, q0, wb0, rp, own, halo_blk=None):
        if own:
            for h in range(8):
                pb, pbn = w1_chunk(h)
                store_fm(pb, pbn, qa_s[g][h][:, q0:q0 + TT], f"qa{g}")
            for h in range(8):
                pb, pbn = w1_chunk(8 + h)
                store_fm(pb, pbn, qn_s[g][h][:, q0:q0 + TT], f"qn{g}")
            for i in range(4):
                pa, pan = w1_chunk(16 + i)
                pbk, pbkn = w1_chunk(20 + i)
                rope_out(pa, pan, pbk, pbkn, rp, qr_s[g][i][:, q0:q0 + TT], f"qr{g}")
        for gi in range(2):
            pb, pbn = w1_chunk(24 + gi)
            if own:
                store_fm(pb, pbn, ka_s[g][gi][:, wb0 * 128:wb0 * 128 + TT], f"ka{g}")
            else:
                hb, wbi = halo_blk
                store_fm(pb[:, hb * 128:(hb + 1) * 128], pbn, ka_s[g][gi][:, wbi * 128:(wbi + 1) * 128], f"ka{g}", w=128)
        for half in range(2):
            pb, pbn = cur_ring[0].next()
            for b2 in range(2):
                blk = half * 2 + b2
                for k in range(16):
                    S.mm(pb[:, b2 * 256:(b2 + 1) * 256], hTcur[0][:, k, blk * 128:(blk + 1) * 128], vaw[:, k, :], k == 0, k == 15,
                         reads=[hTcur[1], "vaw"], writes=[pbn])
            evac_copy(vstg[:, half * 2:half * 2 + 2, :], pb.rearrange("p (a b) -> p a b", a=2), [pbn], ["vstg"])
        if own:
            S.dma(SQ, va_s[g][wb0:wb0 + 4].rearrange("b p n -> p b n"), vstg, reads=["vstg"], writes=[f"va{g}"])
        else:
            hb, wbi = halo_blk
            S.dma(SQ, va_s[g][wbi], vstg[:, hb, :], reads=["vstg"], writes=[f"va{g}"])

    tiles = []
    for ti in range(NPT):
        tiles.append((0, ti))
    for ti in range(NSA):
        tiles.append((1, ti))

    def tile_body(g, ti, rp):
        t0 = ti * TT
        fin = latent_tile(g, t0, rp)
        if g == 0:
            qkv_tile(0, t0, ti * 4, rp, True)
        else:
            if ti < NSO:
                qkv_tile(1, t0, 1 + ti * 4, rp, True)
            if ti == NSO % NSA and NSA > NSO:
                qkv_tile(1, 0, 0, rp, False, halo_blk=(0, NBS - 1))
            if ti == NSA - 1 and NSA > NSO:
                qkv_tile(1, 0, 0, rp, False, halo_blk=(3, 0))
        return fin

    def xt_of(k):
        return xg[tiles[k][0]][tiles[k][1] * TT:(tiles[k][1] + 1) * TT, :]

    pend_fin = [None]
    pars = {0: norm_stats(xt_of(0))}
    norm_transpose(pars[0], tiles[0][0])
    if len(tiles) > 1:
        pars[1] = norm_stats(xt_of(1))
    for i, (g, ti) in enumerate(tiles):
        par = pars.pop(i)
        rp = load_rope(g, ti * TT)
        if i + 1 < len(tiles):
            norm_transpose(pars[i + 1], tiles[i + 1][0])
        if pend_fin[0] is not None:
            pend_fin[0]()
        if i + 2 < len(tiles):
            pars[i + 2] = norm_stats(xt_of(i + 2))
        hTcur[0], hTcur[1] = hTs[par], f"hT{par}"
        pend_fin[0] = tile_body(g, ti, rp)
    pend_fin[0]()

    cur_ring[0] = psring
    S.barrier(MAIN)
    arena.reset(CASTW)

    qat = arena.take([8, 512], BF16)
    kat = arena.take([2, 768], BF16)
    vat = arena.take([6, 256], BF16)
    wtmp = Ring([arena.take([512], F32) for _ in range(2)], "wtmp")
    wP = Ring([arena.take([512], BF16) for _ in range(4)], "wP")
    den = arena.take([512], F32)
    aoT = arena.take([8, 512], BF16)
    wscale = float(128 ** -0.5)
    for g in range(2):
        nbw = NBW[g]
        for ti in range(NQs[g] // TT):
            q0 = ti * TT
            wb0 = ti * 4 + (1 if g == 1 else 0)
            lo = max(wb0 - 1, 0)
            hi = min(wb0 + 5, nbw)
            nb_ld = hi - lo
            S.dma("sp", qat, qa_s[g][:, :, q0:q0 + TT].rearrange("h p t -> p h t"), reads=[f"qa{g}"], writes=["qat"])
            S.dma("sp", kat[:, :, 0:nb_ld * 128], ka_s[g][:, :, lo * 128:hi * 128].rearrange("c p t -> p c t"),
                  reads=[f"ka{g}"], writes=["kat"])
            S.dma("sp", vat[:, 0:nb_ld, :], va_s[g][lo:hi].rearrange("b p n -> p b n"), reads=[f"va{g}"], writes=["vat"])
            for nb in range(4):
                wn = wb0 + nb
                for gi in range(2):
                    qsl = qat[:, 4 * gi:4 * gi + 4, nb * 128:(nb + 1) * 128]
                    offs = [o for o in (-1, 0, 1) if 0 <= wn + o < nbw]
                    Ps = []
                    for o in offs:
                        kb = wn + o - lo
                        pb, pbn = psring.next()
                        S.mm(pb.rearrange("p (r q) -> p r q", r=4), kat[:, gi, kb * 128:(kb + 1) * 128], qsl, True, True,
                             reads=["kat", "qat"], writes=[pbn])
                        wt, wtn = wtmp.next()
                        S.add("dve", lambda e, wt=wt, pb=pb, o=o, gi=gi: e.scalar_tensor_tensor(
                            wt.rearrange("p (r q) -> p r q", r=4), pb.rearrange("p (r q) -> p r q", r=4), wscale,
                            Ttab[:, o + 1, 4 * gi:4 * gi + 4, :], ALU.mult, ALU.add),
                            reads=[pbn, "Ttab"], writes=[wtn])
                        if g == 1 and wn + o == 0:
                            bias = hm[:, 0:1]
                        elif g == 1 and wn + o == nbw - 1:
                            bias = hm[:, 1:2]
                        else:
                            bias = zerob[:, 0:1]
                        pt, ptn = wP.next()
                        S.add("act", lambda e, pt=pt, wt=wt, bias=bias: e.activation(pt, wt, AF.Exp, bias=bias, scale=1.0),
                              reads=[wtn, "hm", "zerob"], writes=[ptn])
                        Ps.append((pt, ptn, kb))
                    po, pon = psring.next()
                    pss, pssn = psring.next()
                    for i, (pt, ptn, kb) in enumerate(Ps):
                        S.mm(po, vat[:, kb, gi * 128:(gi + 1) * 128], pt, i == 0, i == len(Ps) - 1, reads=["vat", ptn], writes=[pon])
                    for i, (pt, ptn, kb) in enumerate(Ps):
                        S.mm(pss, onesb[:, :], pt, i == 0, i == len(Ps) - 1, reads=["onesb", ptn], writes=[pssn])
                    S.add("dve", lambda e, pss=pss, gi=gi: e.tensor_tensor(
                        den.rearrange("p (r q) -> p r q", r=4), pss.rearrange("p (r q) -> p r q", r=4),
                        ES[:, 4 * gi:4 * gi + 4, :], ALU.add), reads=[pssn, "ES"], writes=["den"])
                    S.add("dve", lambda e: e.reciprocal(den, den), reads=["den"], writes=["den"])
                    S.add("dve", lambda e, po=po, gi=gi, nb=nb: e.tensor_tensor(
                        aoT[:, 4 * gi:4 * gi + 4, nb * 128:(nb + 1) * 128], po.rearrange("p (r q) -> p r q", r=4),
                        den.rearrange("p (r q) -> p r q", r=4), ALU.mult), reads=[pon, "den"], writes=["aoT"])
            S.dma(SQ, at_s[g][0:8, :, q0:q0 + TT].rearrange("h p t -> p h t"), aoT, reads=["aoT"], writes=[f"at{g}"])

    S.barrier(MAIN)
    arena.reset(CASTW)

    NKmax = max(NKs)
    NQmax = max(NQs)
    knT = arena.take([NKmax], BF16)
    vh = arena.take([NKmax // 128, 128], BF16)
    krT = arena.take([NKmax], BF16)
    cnring = Ring([arena.take([4, 512], BF16) for _ in range(4)], "cnr")
    qnh = arena.take([NQmax], BF16)
    qrh = arena.take([NQmax], BF16)
    mP = Ring([arena.take([1024], BF16) for _ in range(10)], "mP")
    ostg = Ring([arena.take([512], BF16) for _ in range(2)], "ostg")
    rec = arena.take([512], F32)
    accs2 = [[arena.take([1024], F32) for _ in range(2)] for _ in range(2)]
    wkvb = arena.take([4, 2048], BF16)
    wkst_view = accs2[0][0].rearrange("p (a b) -> p a b", a=2)
    wkst_view2 = accs2[0][1].rearrange("p (a b) -> p a b", a=2)
    for j in range(4):
        for hf, wv in enumerate((wkst_view, wkst_view2)):
            S.dma("sp", wv, wkvb_d[hf * 256:(hf + 1) * 256, j * 512:(j + 1) * 512].rearrange("(k p) n -> p k n", p=128),
                  writes=[f"acc0_{hf}"])
            evac_copy(wkvb[:, 2 * hf:2 * hf + 2, j * 512:(j + 1) * 512], wv, [f"acc0_{hf}"], ["wkvb"])
    mscale = float(192 ** -0.5)
    for g in range(2):
        NK, NQ = NKs[g], NQs[g]
        S.dma("sp", krT[:, 0:NK], kr_s[g][:, :], reads=[f"kr{g}"], writes=["krT"])
        for h in range(8):
            for kt in range(NK // TT):
                cnv, cnn = cnring.next()
                S.dma("sp", cnv, cn_s[g][kt].rearrange("p (j t) -> p j t", j=4), reads=[f"cn{g}"], writes=[cnn])
                pb, pbn = psb(4 + kt % 2)
                for j in range(4):
                    S.mm(pb, wkvb[:, j, h * 128:(h + 1) * 128], cnv[:, j, :], j == 0, j == 3, reads=["wkvb", cnn], writes=[pbn])
                S.add("act", lambda e, pb=pb, kt=kt: e.activation(knT[:, kt * TT:(kt + 1) * TT], pb, AF.Copy), reads=[pbn], writes=["knT"])
                pv_, pvn = psb(2 * (kt % 2))
                for blk in range(4):
                    for j in range(4):
                        S.mm(pv_[:, blk * 128:(blk + 1) * 128], cnv[:, j, blk * 128:(blk + 1) * 128],
                             wkvb[:, j, 1024 + h * 128:1024 + (h + 1) * 128], j == 0, j == 3, reads=["wkvb", cnn], writes=[pvn])
                S.add("dve", lambda e, pv_=pv_, kt=kt: e.tensor_copy(vh[:, kt * 4:kt * 4 + 4, :], pv_.rearrange("p (a b) -> p a b", a=4)),
                      reads=[pvn], writes=["vh"])
            S.dma("sp", qnh[:, 0:NQ], qn_s[g][h], reads=[f"qn{g}"], writes=["qnh"])
            ro = 64 * (h % 2)
            S.add("dve", lambda e, ro=ro: e.memset(qrh[64 - ro:128 - ro, :], 0.0), writes=["qrh"])
            S.dma("sp", qrh[ro:ro + 64, 0:NQ], qr_s[g][h // 2][ro:ro + 64, :], reads=[f"qr{g}"], writes=["qrh"])
            NKB = NK // 128
            NKP = NKB // 2
            pending = [None]
            NQT = NQ // TT
            items = [(qt, kp) for qt in range(NQT) for kp in range(NKP)]

            def s_pair(gi):
                qt, kp = items[gi]
                pbp = pp[gi % 3][:, :]
                pbn2 = [f"ps{2 * (gi % 3)}", f"ps{2 * (gi % 3) + 1}"]
                for hf in range(2):
                    kb = 2 * kp + hf
                    o_ = pbp[:, hf * 512:(hf + 1) * 512]
                    S.mm(o_, knT[:, kb * 128:(kb + 1) * 128], qnh[:, qt * TT:(qt + 1) * TT], True, False,
                         reads=["knT", "qnh"], writes=pbn2)
                    S.mm(o_, krT[:, kb * 128:(kb + 1) * 128], qrh[:, qt * TT:(qt + 1) * TT], False, True,
                         reads=["krT", "qrh"], writes=pbn2)
                pt, ptn = mP.next()
                S.add("act", lambda e, pt=pt, pbp=pbp: e.activation(pt, pbp, AF.Exp, scale=mscale), reads=pbn2, writes=[ptn])
                return pt, ptn

            def pv_pair(gi, pt, ptn):
                qt, kp = items[gi]
                po, pon = psb(6 + qt % 2)
                accs = accs2[qt % 2]
                an = [f"acc{qt % 2}_{a}" for a in range(2)]
                for hf in range(2):
                    kb = 2 * kp + hf
                    S.mm(po, vh[:, kb, :], pt[:, hf * 512:(hf + 1) * 512], kb == 0, kb == NKB - 1, reads=["vh", ptn], writes=[pon])
                a = kp % 2
                if kp < 2:
                    S.add("dve", lambda e, a=a: e.tensor_copy(accs[a], pt), reads=[ptn], writes=[an[a]])
                else:
                    S.add("dve", lambda e, a=a: e.tensor_tensor(accs[a], accs[a], pt, ALU.add), reads=[ptn, an[a]], writes=[an[a]])

            def make_epilogue(qt):
                po, pon = psb(6 + qt % 2)
                pss, pssn = psb(5)
                accs = accs2[qt % 2]
                an = [f"acc{qt % 2}_{a}" for a in range(2)]

                def epilogue_a():
                    S.add("dve", lambda e: e.tensor_tensor(accs[0], accs[0], accs[1], ALU.add), reads=[an[0], an[1]], writes=[an[0]])
                    S.add("dve", lambda e: e.tensor_tensor(accs[0][:, 0:512], accs[0][:, 0:512], accs[0][:, 512:1024], ALU.add),
                          reads=[an[0]], writes=[an[0]])

                def epilogue():
                    S.mm(pss, onesf[:, :], accs[0][:, 0:512], True, True, reads=["onesf", an[0]], writes=[pssn])
                    S.add("dve", lambda e: e.tensor_copy(rec, pss), reads=[pssn], writes=["rec"])
                    S.add("dve", lambda e: e.reciprocal(rec, rec), reads=["rec"], writes=["rec"])
                    sg, sgn = ostg.next()
                    S.add("dve", lambda e: e.tensor_tensor(sg, po, rec, ALU.mult), reads=[pon, "rec"], writes=[sgn])
                    S.dma(SQ, at_s[g][8 + h][:, qt * TT:(qt + 1) * TT], sg, reads=[sgn], writes=[f"at{g}"])
                return epilogue_a, epilogue

            pend = {}
            for gi in range(min(3, len(items))):
                pend[gi] = s_pair(gi)
            for gi, (qt, kp) in enumerate(items):
                pt, ptn = pend.pop(gi)
                pv_pair(gi, pt, ptn)
                if gi + 3 < len(items):
                    pend[gi + 3] = s_pair(gi + 3)
                if kp == 1 and pending[0] is not None and pending[0][0] is not None:
                    pending[0][0]()
                    pending[0] = (None, pending[0][1])
                if kp == min(4, NKP - 1) and pending[0] is not None:
                    if pending[0][0] is not None:
                        pending[0][0]()
                    pending[0][1]()
                    pending[0] = None
                if kp == NKP - 1:
                    if pending[0] is not None:
                        if pending[0][0] is not None:
                            pending[0][0]()
                        pending[0][1]()
                    pending[0] = make_epilogue(qt)
            pending[0][0]()
            pending[0][1]()

    S.barrier(("pe", "act", "dve", "sp", "pool"))
    arena.reset(0)

    xt = arena.take([4, 2048], F32)
    xT = arena.take([16, 512], F32)
    atT = arena.take([16, 512], BF16)
    h2T = arena.take([16, 512], BF16)
    actT = arena.take([32, 512], BF16)
    wring = Ring([arena.take([16, 128], BF16) for _ in range(3)], "wr")
    w2ring = Ring([arena.take([32, 128], BF16) for _ in range(2)], "w2r")
    yst = Ring([arena.take([2048], F32) for _ in range(2)], "yst")
    sq3 = Ring([arena.take([512], BF16) for _ in range(4)], "sq3")
    rs3 = arena.take([512], F32)
    tmp3 = Ring([arena.take([512], F32) for _ in range(2)], "tmp3")
    p3ring = Ring(ps[0:7], "ps")
    cur_ring[0] = p3ring
    pstat, pstatn = psb(7)

    class Stats:
        def __init__(self):
            self.pend = []

        def chunk(self, c):
            sq, sqn = sq3.next()
            S.add("act", lambda e, sq=sq, c=c: e.activation(sq, xT[:, c, :], AF.Square), reads=[f"xT{c}"], writes=[sqn])
            self.pend.append((c, sq, sqn))
            if len(self.pend) > 2:
                self._mm()

        def _mm(self):
            c, sq, sqn = self.pend.pop(0)
            S.mm(pstat, onesb[:, :], sq, c == 0, c == 15, reads=["onesb", sqn], writes=[pstatn])

        def fin(self):
            while self.pend:
                self._mm()
            S.add("act", lambda e: e.activation(rs3, pstat, AF.Sqrt, bias=epsb[:, 0:1], scale=1.0 / D), reads=[pstatn, "epsb"], writes=["rs3"])
            S.add("dve", lambda e: e.reciprocal(rs3, rs3), reads=["rs3"], writes=["rs3"])

    p3tiles = [(g, ti) for g in range(2) for ti in range(NQs[g] // TT)]

    class Streamer:
        def __init__(self, views, name, seq):
            self.views, self.name, self.seq = views, name, seq
            self.depth = len(views)
            self.i = 0
            for k in range(min(self.depth, len(seq))):
                self._load(k)

        def _load(self, k):
            src, res, kk = self.seq[k]
            v = self.views[k % self.depth]
            S.dma("sp", v, src.rearrange("p (k n) -> p k n", k=kk), reads=[res], writes=[f"{self.name}{k % self.depth}"])

        def get(self):
            k = self.i
            return self.views[k % self.depth], f"{self.name}{k % self.depth}"

        def release(self):
            k = self.i
            self.i += 1
            if k + self.depth < len(self.seq):
                self._load(k + self.depth)

    seq_small, seq_big = [], []
    for _ in p3tiles:
        for f in range(16):
            seq_small.append((WOs[f], f"WOs{f}", 16))
        for half in range(2):
            for j in range(32):
                seq_small.append((WF1s[half * 32 + j], f"WF1s{half * 32 + j}", 16))
            for f in range(16):
                seq_big.append((WF2s[f][:, half * 4096:(half + 1) * 4096], f"WF2s{f}", 32))
    wst = Streamer(wring.v, "wr", seq_small)
    w2st = Streamer(w2ring.v, "w2r", seq_big)

    def load_inputs(g, ti):
        t0 = ti * TT
        for blk in range(4):
            S.dma("sp", xt[:, blk, :], xg[g][t0 + blk * 128:t0 + (blk + 1) * 128, :], writes=["xt"])
        S.dma("sp", atT, at_s[g][:, :, t0:t0 + TT].rearrange("c p t -> p c t"), reads=[f"at{g}"], writes=["atT"])

    load_inputs(*p3tiles[0])
    for it, (g, ti) in enumerate(p3tiles):
        t0 = ti * TT
        for c in range(16):
            pb, pbn = p3ring.next()
            for blk in range(4):
                S.tr(pb[:, blk * 128:(blk + 1) * 128], xt[:, blk, c * 128:(c + 1) * 128], identf[:, :],
                     reads=["xt", "identf"], writes=[pbn])
            evac_copy(xT[:, c, :], pb, [pbn], [f"xT{c}"])
        st = Stats()
        for f in range(16):
            wv, wn = wst.get()
            pb, pbn = proj_fm(lambda k: wv[:, k, :], wn, 16, lambda k: atT[:, k, :], ["atT"])
            wst.release()
            S.add("dve", lambda e, f=f, pb=pb, g=g: e.scalar_tensor_tensor(
                xT[:, f, :], pb, G1[:, 2 * f + g:2 * f + g + 1], xT[:, f, :], ALU.mult, ALU.add),
                reads=[pbn, "modT", f"xT{f}"], writes=[f"xT{f}"])
            st.chunk(f)
        st.fin()
        for c in range(16):
            tp, tpn = tmp3.next()
            S.add("dve", lambda e, c=c, tp=tp, g=g: e.scalar_tensor_tensor(
                tp, xT[:, c, :], A2[:, 2 * c + g:2 * c + g + 1], rs3, ALU.mult, ALU.mult),
                reads=[f"xT{c}", "A2", "rs3"], writes=[tpn])
            S.add("act", lambda e, c=c, tp=tp, g=g: e.activation(h2T[:, c, :], tp, AF.Identity, bias=B2[:, 2 * c + g:2 * c + g + 1], scale=1.0),
                  reads=[tpn, "modT"], writes=["h2T"])
        st = Stats()
        for half in range(2):
            for j in range(32):
                f1 = half * 32 + j
                wv, wn = wst.get()
                pb, pbn = proj_fm(lambda k: wv[:, k, :], wn, 16, lambda k: h2T[:, k, :], ["h2T"])
                wst.release()
                tp, tpn = tmp3.next()
                S.add("act", lambda e, tp=tp, pb=pb: e.activation(tp, pb, AF.Relu), reads=[pbn], writes=[tpn])
                S.add("pool", lambda e, tp=tp, j=j: e.tensor_tensor(actT[:, j, :], tp, tp, ALU.mult), reads=[tpn], writes=["actT"])
            if half == 1 and it + 1 < len(p3tiles):
                load_inputs(*p3tiles[it + 1])
            for f in range(16):
                wv, wn = w2st.get()
                pb, pbn = proj_fm(lambda k: wv[:, k, :], wn, 32, lambda k: actT[:, k, :], ["actT"])
                w2st.release()
                S.add("dve", lambda e, f=f, pb=pb, g=g: e.scalar_tensor_tensor(
                    xT[:, f, :], pb, G2[:, 2 * f + g:2 * f + g + 1], xT[:, f, :], ALU.mult, ALU.add),
                    reads=[pbn, "modT", f"xT{f}"], writes=[f"xT{f}"])
                if half == 1:
                    st.chunk(f)
        st.fin()
        for c in range(16):
            S.add("dve", lambda e, c=c: e.scalar_tensor_tensor(
                xT[:, c, :], xT[:, c, :], gfin[:, c:c + 1], rs3, ALU.mult, ALU.mult),
                reads=[f"xT{c}", "gfin", "rs3"], writes=[f"xT{c}"])
        for blk in range(4):
            ys, ysn = yst.next()
            for c4 in range(4):
                pb, pbn = p3ring.next()
                for cc in range(4):
                    c = c4 * 4 + cc
                    S.tr(pb[:, cc * 128:(cc + 1) * 128], xT[:, c, blk * 128:(blk + 1) * 128], identf[:, :],
                         reads=[f"xT{c}", "identf"], writes=[pbn])
                evac_copy(ys[:, c4 * 512:(c4 + 1) * 512], pb, [pbn], [ysn])
            S.dma(SQ, y_d[g][t0 + blk * 128:t0 + (blk + 1) * 128, :], ys, reads=[ysn], writes=["y"])

    stats_ = S.finalize_and_emit()
    return nc, stats_


def _t5_bucket(rel):
    half = 16
    max_exact = 8
    ret = np.where(rel > 0, half, 0)
    n = np.abs(rel)
    nf = np.maximum(n, 1).astype(np.float32)
    large = max_exact + (np.log(nf / max_exact) / math.log(128 / max_exact) * (half - max_exact)).astype(np.int32)
    large = np.minimum(large, half - 1)
    return ret + np.where(n < max_exact, n, large)


def _oht():
    k = np.arange(128)[:, None]
    q = np.arange(128)[None, :]
    out = np.zeros((128, 99, 128), np.float32)
    for oi, o in enumerate((-1, 0, 1)):
        rel = o * 128 + k - q
        bk = _t5_bucket(rel)
        valid = np.abs(rel) <= 128
        for b in range(32):
            out[:, oi * 33 + b, :] = ((bk == b) & valid)
        out[:, oi * 33 + 32, :] = ~valid
    return out


def _rope_tables(pos):
    half = 32
    inv = (np.float32(10000.0) ** (-np.arange(half, dtype=np.float32) / np.float32(half))).astype(np.float32)
    ang = pos.astype(np.float32)[:, None] * inv[None, :]
    cos = np.cos(ang).astype(np.float32).T
    sin = np.sin(ang).astype(np.float32).T
    cosT = np.concatenate([cos, cos, cos, cos], 0)
    sinT = np.concatenate([-sin, sin, -sin, sin], 0)
    return np.ascontiguousarray(cosT), np.ascontiguousarray(sinT)


def _fm(v, reps=1):
    a = np.asarray(v, np.float32).reshape(-1, 128).T
    return np.ascontiguousarray(np.repeat(a, reps, axis=1))


_CACHE = {}


def run(inputs, NCORES, SP, SS):
    f = lambda k: np.asarray(inputs[k], np.float32)
    SA = NCORES * SS
    w_in = f("w_in")[0]
    QA, KA, VA, QN, QR, CKV, KR = 0, 1024, 1280, 1536, 2560, 3072, 3584
    sw = np.arange(512).reshape(8, 2, 32)[:, ::-1, :].reshape(-1)
    swk = np.arange(64).reshape(2, 32)[::-1].reshape(-1)
    cols1 = np.concatenate([np.arange(QA, QA + 1024), np.arange(QN, QN + 1024), np.arange(QR, QR + 512), QR + sw,
                            np.arange(KA, KA + 256), np.arange(VA, VA + 256)])
    krc = np.arange(KR, KR + 64)
    cols2 = np.concatenate([np.arange(CKV, CKV + 512), krc, krc, KR + swk, KR + swk])
    w1 = np.ascontiguousarray(w_in[:, cols1])
    w2 = np.ascontiguousarray(w_in[:, cols2])
    wk = f("w_kv_b")[0].reshape(512, 8, 256)
    wkvb = np.ascontiguousarray(np.concatenate([wk[:, :, :128].reshape(512, 1024), wk[:, :, 128:].reshape(512, 1024)], 1))
    common = {
        "w_ada": f("w_ada")[0], "bT2": _fm(f("b_ada")[0], 2), "gmix2": _fm(f("g_mix")[0], 2), "gmlp2": _fm(f("g_mlp")[0], 2),
        "gfinT": _fm(f("g_final")), "gkvT": _fm(f("g_kv")[0]), "w1": w1, "w2": w2, "wkvb": wkvb,
        "w_o": f("w_o")[0], "w_ff1": f("w_ff1")[0], "w_ff2": f("w_ff2")[0],
        "sinkb": np.ascontiguousarray(np.broadcast_to(f("sink")[0][None, :], (128, 8))),
        "relb": np.ascontiguousarray(np.broadcast_to(f("rel_bias").reshape(1, 256), (128, 256))),
        "oht": _oht(), "ident": np.eye(128, dtype=np.float32),
    }
    cosP, sinP = _rope_tables(np.arange(SP))
    common["cosP"], common["sinP"] = cosP, sinP
    xpr, xsm = f("x_prompt"), f("x_sample")[0]
    cp, cs = f("c_prompt"), f("c_sample")[0]
    in_maps = []
    for i in range(NCORES):
        m = dict(common)
        m["xp"] = np.ascontiguousarray(xpr[i])
        m["xs"] = np.ascontiguousarray(np.roll(xsm, -i * SS, axis=0))
        pos = (np.arange(SA) + i * SS) % SA
        m["cosS"], m["sinS"] = _rope_tables(pos)
        cc = np.stack([cp[i], cs], 1)
        m["cT"] = np.ascontiguousarray(cc.reshape(16, 128, 2).transpose(1, 0, 2).reshape(128, 32))
        hmv = np.zeros((128, 2), np.float32)
        if i == 0:
            hmv[:, 0] = NEGM
        if i == NCORES - 1:
            hmv[:, 1] = NEGM
        m["hm"] = hmv
        in_maps.append(m)
    key = (NCORES, SP, SS)
    if key not in _CACHE:
        _CACHE[key] = build(NCORES, SP, SS)
    nc, st = _CACHE[key]
    res = run_bass_kernel_spmd(nc, in_maps, core_ids=list(range(NCORES)))
    yp = np.stack([np.asarray(r["yp"], np.float32) for r in res.results], 0)
    ys = np.concatenate([np.asarray(r["ys"], np.float32) for r in res.results], 0)[None]
    return yp, ys


def kernel(**inputs):
    return run(inputs, 8, 2048, 2048)
```
